# Optimizing a Trainium2 kernel written in Bass

```python
import math
import jax, jax.numpy as jnp
from jax import lax
import numpy as np

D_MODEL = 1024
BATCH = 8
SEQ = 8192
DEPTH = 1

N_META = 16
BLOCK = 128
PAD_FRONT = BLOCK - N_META
WINDOW = 128
A_HEADS = 4
A_QK_DIM = 64
A_V_DIM = 2 * A_QK_DIM
A_WIDTH = A_HEADS * A_V_DIM
B_HEADS = 8
B_KV_HEADS = 2
B_GROUP = B_HEADS // B_KV_HEADS
B_HEAD_DIM = 64
B_WIDTH = B_HEADS * B_HEAD_DIM
B_KV_WIDTH = B_KV_HEADS * B_HEAD_DIM
N_BUCKETS = 32
MAX_DISTANCE = 128
N_BIAS_HEADS = A_HEADS + B_HEADS
EPS = 1e-6
NEG = -1e30
COLS = [A_HEADS * 2 * A_QK_DIM, A_HEADS * 2 * A_QK_DIM, A_WIDTH, A_WIDTH,
        B_WIDTH, B_KV_WIDTH, B_KV_WIDTH, B_WIDTH, D_MODEL, D_MODEL]
SPLITS = [int(v) for v in np.cumsum(COLS)[:-1]]
D_IN = int(sum(COLS))

kernel_name = "hybrid_diffattn_swa_gated_encoder"


def rmsnorm(x, g):
    xf = x.astype(jnp.float32)
    y = xf * lax.rsqrt(jnp.mean(xf * xf, axis=-1, keepdims=True) + EPS)
    return (y * g.astype(jnp.float32)).astype(x.dtype)


def t5_bucket(rel):
    half = N_BUCKETS // 2
    max_exact = half // 2
    ret = jnp.where(rel > 0, half, 0)
    n = jnp.abs(rel)
    nf = jnp.maximum(n, 1).astype(jnp.float32)
    large = max_exact + (jnp.log(nf / max_exact) / math.log(MAX_DISTANCE / max_exact)
                         * (half - max_exact)).astype(jnp.int32)
    large = jnp.minimum(large, half - 1)
    return ret + jnp.where(n < max_exact, n, large)


def diff_attention(q, k, v, pos, valid, bias_tab, lam, lam_init, subln_g):
    B, P, H, _, dk = q.shape
    dv = v.shape[-1]
    nb = P // BLOCK
    qb = q.reshape(B, nb, BLOCK, H, 2, dk).transpose(1, 0, 2, 3, 4, 5)
    posb = pos.reshape(nb, BLOCK)
    scale = dk ** -0.5

    def block(args):
        q_blk, qpos = args
        s = jnp.einsum('bqhmd,bkhmd->bhmqk', q_blk, k).astype(jnp.float32) * scale
        bias = bias_tab[t5_bucket(pos[None, :] - qpos[:, None])]
        s = s + bias.transpose(2, 0, 1)[None, :, None].astype(jnp.float32)
        s = jnp.where(valid[None, None, None, None, :], s, NEG)
        p = jax.nn.softmax(s, axis=-1)
        a = p[:, :, 0] - lam * p[:, :, 1]
        return jnp.einsum('bhqk,bkhe->bqhe', a.astype(v.dtype), v)

    o = lax.map(block, (qb, posb))
    o = o.transpose(1, 0, 2, 3, 4).reshape(B, P, H, dv)
    o = rmsnorm(o, subln_g) * (1.0 - lam_init)
    return o.reshape(B, P, H * dv)


def windowed_gqa(q, k, v, valid, bias_tab, sink):
    B, P, Hkv, G, d = q.shape
    nb = P // BLOCK
    qb = q.reshape(B, nb, BLOCK, Hkv, G, d).transpose(1, 0, 2, 3, 4, 5)

    def band(t):
        t = jnp.pad(t, ((0, 0), (BLOCK, BLOCK), (0, 0), (0, 0)))
        t = t.reshape(B, nb + 2, BLOCK, Hkv, d)
        t = jnp.concatenate([t[:, :-2], t[:, 1:-1], t[:, 2:]], axis=2)
        return t.transpose(1, 0, 2, 3, 4)

    kb, vb = band(k), band(v)
    vext = jnp.pad(valid, (BLOCK, BLOCK)).reshape(nb + 2, BLOCK)
    kvalid = jnp.concatenate([vext[:-2], vext[1:-1], vext[2:]], axis=1)
    rel = jnp.arange(3 * BLOCK)[None, :] - BLOCK - jnp.arange(BLOCK)[:, None]
    inwin = jnp.abs(rel) <= WINDOW
    bias = bias_tab[t5_bucket(rel)].transpose(2, 0, 1).reshape(Hkv, G, BLOCK, 3 * BLOCK).astype(jnp.float32)
    sink_col = sink.reshape(Hkv, G, 1, 1).astype(jnp.float32)
    scale = d ** -0.5

    def block(args):
        q_blk, k_blk, v_blk, kv_ok = args
        s = jnp.einsum('bqgrd,bkgd->bgrqk', q_blk, k_blk).astype(jnp.float32) * scale + bias
        s = jnp.where(inwin & kv_ok[None, :], s, NEG)
        sk = jnp.broadcast_to(sink_col, (B, Hkv, G, BLOCK, 1))
        p = jax.nn.softmax(jnp.concatenate([s, sk], axis=-1), axis=-1)[..., :-1]
        return jnp.einsum('bgrqk,bkgd->bqgrd', p.astype(v_blk.dtype), v_blk)

    o = lax.map(block, (qb, kb, vb, kvalid))
    return o.transpose(1, 0, 2, 3, 4, 5).reshape(B, P, Hkv * G * d)


def setup_inputs(seed: int = 0) -> dict:
    key = jax.random.key(seed)
    ks = jax.random.split(key, 20)
    f32 = jnp.float32
    nrm = lambda k, shape, s: jax.random.normal(k, shape, f32) * s
    return {
        "x": nrm(ks[0], (BATCH, SEQ, D_MODEL), 1.0),
        "meta_tokens": nrm(ks[1], (N_META, D_MODEL), 1.0),
        "rel_bias": nrm(ks[2], (N_BUCKETS, N_BIAS_HEADS), 0.5),
        "pre_norm_g": 1.0 + nrm(ks[3], (DEPTH, D_MODEL), 0.01),
        "w_in": nrm(ks[4], (DEPTH, D_MODEL, D_IN), D_MODEL ** -0.5),
        "b_in": nrm(ks[5], (DEPTH, D_IN), 0.01),
        "lambda_q1": nrm(ks[6], (DEPTH, A_QK_DIM), 0.1),
        "lambda_k1": nrm(ks[7], (DEPTH, A_QK_DIM), 0.1),
        "lambda_q2": nrm(ks[8], (DEPTH, A_QK_DIM), 0.1),
        "lambda_k2": nrm(ks[9], (DEPTH, A_QK_DIM), 0.1),
        "subln_g": 1.0 + nrm(ks[10], (DEPTH, A_V_DIM), 0.01),
        "sink": nrm(ks[11], (DEPTH, B_HEADS), 0.5),
        "w_out_a": nrm(ks[12], (DEPTH, A_WIDTH, D_MODEL), A_WIDTH ** -0.5),
        "w_out_b": nrm(ks[13], (DEPTH, B_WIDTH, D_MODEL), B_WIDTH ** -0.5),
        "w_out": nrm(ks[14], (DEPTH, D_MODEL, D_MODEL), D_MODEL ** -0.5),
        "post_norm_g": 1.0 + nrm(ks[15], (DEPTH, D_MODEL), 0.01),
    }


def reference(x, meta_tokens, rel_bias, pre_norm_g, w_in, b_in, lambda_q1, lambda_k1,
              lambda_q2, lambda_k2, subln_g, sink, w_out_a, w_out_b, w_out, post_norm_g):
    B = x.shape[0]
    meta = jnp.broadcast_to(meta_tokens[None].astype(x.dtype), (B, N_META, D_MODEL))
    h = jnp.concatenate([meta, x], axis=1)
    L = h.shape[1]
    P = L + PAD_FRONT
    pos = jnp.arange(P, dtype=jnp.int32) - PAD_FRONT
    valid = pos >= 0
    bias_a = rel_bias[:, :A_HEADS]
    bias_b = rel_bias[:, A_HEADS:]

    def padseq(t):
        return jnp.pad(t, ((0, 0), (PAD_FRONT, 0), (0, 0)))

    for layer in range(DEPTH):
        hn = rmsnorm(h, pre_norm_g[layer])
        proj = hn @ w_in[layer] + b_in[layer]
        qa, ka, va, za, qb, kb, vb, zb, ga, gb = jnp.split(proj, SPLITS, axis=-1)

        lam_init = 0.8 - 0.6 * math.exp(-0.3 * layer)
        lam = (jnp.exp(jnp.sum(lambda_q1[layer].astype(jnp.float32) * lambda_k1[layer].astype(jnp.float32)))
               - jnp.exp(jnp.sum(lambda_q2[layer].astype(jnp.float32) * lambda_k2[layer].astype(jnp.float32)))
               + lam_init)
        oa = diff_attention(
            padseq(qa).reshape(B, P, A_HEADS, 2, A_QK_DIM),
            padseq(ka).reshape(B, P, A_HEADS, 2, A_QK_DIM),
            padseq(va).reshape(B, P, A_HEADS, A_V_DIM),
            pos, valid, bias_a, lam, lam_init, subln_g[layer])[:, PAD_FRONT:]
        ob = windowed_gqa(
            padseq(qb).reshape(B, P, B_KV_HEADS, B_GROUP, B_HEAD_DIM),
            padseq(kb).reshape(B, P, B_KV_HEADS, B_HEAD_DIM),
            padseq(vb).reshape(B, P, B_KV_HEADS, B_HEAD_DIM),
            valid, bias_b, sink[layer])[:, PAD_FRONT:]

        ya = (oa * jax.nn.silu(za)) @ w_out_a[layer]
        yb = (ob * jax.nn.silu(zb)) @ w_out_b[layer]
        mixed = jax.nn.sigmoid(ga) * ya + jax.nn.sigmoid(gb) * yb
        h = h + rmsnorm(mixed @ w_out[layer], post_norm_g[layer])
    return h[:, N_META:]
```

```python
import math
import numpy as np
import concourse.bass as bass
import concourse.mybir as mybir
from concourse.bass_utils import run_bass_kernel_spmd

F32 = mybir.dt.float32
BF16 = mybir.dt.bfloat16
AF = mybir.ActivationFunctionType
ALU = mybir.AluOpType
AX = mybir.AxisListType

D = 1024
DIN = 5376
EPS = 1e-6
NMETA = 16
ENGS = ("sp", "act", "dve", "pool", "pe")
C_QA, C_KA, C_VA, C_ZA, C_QB, C_KB, C_VB, C_ZB, C_GA, C_GB = 0, 512, 1024, 1536, 2048, 2560, 2688, 2816, 3328, 4352
LAM_INIT = 0.8 - 0.6 * math.exp(-0.3 * 0)


class Sem:
    def __init__(self, nc, name, group=False):
        self.h = nc.alloc_semaphore(name)
        self.name = name
        self.v = 0
        self.group = group


class Op:
    __slots__ = ("eng", "fn", "deps", "dsem", "sem", "val", "need", "idx")

    def __init__(self, eng, fn, dsem):
        self.eng, self.fn, self.dsem = eng, fn, dsem
        self.deps = []
        self.sem = None
        self.val = 0
        self.need = False


def I(method, *args, **kw):
    return (method, args, kw)


class Buf:
    __slots__ = ("w", "r", "name")

    def __init__(self, name=""):
        self.name = name
        self.w = []
        self.r = {}


class Prog:
    def __init__(self, nc):
        self.nc = nc
        self.ops = {e: [] for e in ENGS}
        self.esem = {e: Sem(nc, "es_" + e) for e in ENGS}
        self.bufs = []
        self.dma_since_barrier = []

    def buf(self, name=""):
        b = Buf(name)
        self.bufs.append(b)
        return b

    def bufs_n(self, n, name=""):
        return [self.buf(name + str(i)) for i in range(n)]

    def op(self, eng, fn, reads=(), writes=(), dsem=None, after=()):
        o = Op(eng, fn, dsem)
        deps = {}
        for b in reads:
            for d in b.w:
                deps[id(d)] = d
        for b in writes:
            for d in b.w:
                deps[id(d)] = d
            for d in b.r.values():
                deps[id(d)] = d
        for d in after:
            deps[id(d)] = d
        for d in deps.values():
            if d.eng == "pe" and eng == "pe" and d.dsem is None:
                continue
            if d is o:
                continue
            o.deps.append(d)
            d.need = True
        for b in writes:
            b.w = [o]
            b.r = {}
        for b in reads:
            if b in writes:
                continue
            key = eng if dsem is None else ("dma", id(dsem))
            b.r[key] = o
        self.ops[eng].append(o)
        if dsem is not None:
            self.dma_since_barrier.append(o)
        return o

    def barrier(self):
        lasts = []
        for e in ENGS:
            for o in reversed(self.ops[e]):
                if o.fn is not None and o.dsem is None:
                    lasts.append(o)
                    break
        deps = lasts + self.dma_since_barrier
        self.dma_since_barrier = []
        for e in ENGS:
            self.op(e, None, after=deps)
        for b in self.bufs:
            b.w = []
            b.r = {}

    def flush(self, block):
        for e in ENGS:
            for o in self.ops[e]:
                if o.fn is None:
                    continue
                if o.dsem is not None:
                    o.dsem.v += 16
                    o.sem, o.val = o.dsem, o.dsem.v
                elif o.need:
                    s = self.esem[e]
                    s.v += 1
                    o.sem, o.val = s, s.v
        engmap = {"sp": block.sync, "act": block.scalar, "dve": block.vector, "pool": block.gpsimd, "pe": block.tensor}
        for e in ENGS:
            ops = self.ops[e]

            def body(eng, ops=ops):
                waited = {}
                for o in ops:
                    for d in o.deps:
                        v = d.sem.v if d.sem.group else d.val
                        if waited.get(d.sem.name, 0) >= v:
                            continue
                        waited[d.sem.name] = v
                        eng.wait_ge(d.sem.h, v)
                    if o.fn is None:
                        continue
                    m, a, kw = o.fn
                    ins = getattr(eng, m)(*a, **kw)
                    if o.dsem is not None:
                        ins.then_inc(o.sem.h, 16)
                    elif o.need:
                        ins.then_inc(o.sem.h, 1)

            engmap[e](body)


def t5_bucket_np(rel):
    half = 16
    max_exact = 8
    ret = np.where(rel > 0, half, 0)
    n = np.abs(rel)
    nf = np.maximum(n, 1).astype(np.float32)
    large = max_exact + (np.log(nf / max_exact) / math.log(128 / max_exact) * (half - max_exact)).astype(np.int32)
    large = np.minimum(large, half - 1)
    return ret + np.where(n < max_exact, n, large)


def build(NT):
    G = NT // 4
    NB = NT + 1
    nc = bass.Bass("TRN2", target_bir_lowering=False)
    P = Prog(nc)

    def din(name, shape, dt=F32):
        return nc.dram_tensor(name, shape, dt, kind="ExternalInput").ap()

    x_d = din("x", [NT * 128, D])
    meta_d = din("meta_pad", [128, D])
    win_d = din("w_in", [D, DIN])
    woa_d = din("w_out_a", [512, D])
    wob_d = din("w_out_b", [512, D])
    wo_d = din("w_out", [D, D])
    gcol_d = din("gcol", [128, 8])
    bcol_d = din("bcol", [128, 42])
    bqb_d = din("bqb", [128, 4])
    brow_d = din("brow", [1, 640])
    strips_d = din("strips", [12, 128, 1152])
    cfar_d = din("cfar", [128, 24])
    maskb_d = din("maskb", [128, 384])
    lamv_d = din("lamv", [128, 256])
    gsub_d = din("gsub", [128, 1])
    sink_d = din("sinkr", [128, 8])
    postg_d = din("postg", [128, D])
    ident_d = din("ident", [128, 128])
    y_d = nc.dram_tensor("y", [NT * 128, D], F32, kind="ExternalOutput").ap()
    wscr = nc.dram_tensor("wscr", [8, 128, DIN], BF16, kind="Internal").ap()
    woascr = nc.dram_tensor("woascr", [4, 128, D], BF16, kind="Internal").ap()
    wobscr = nc.dram_tensor("wobscr", [4, 128, D], BF16, kind="Internal").ap()
    woscr = nc.dram_tensor("woscr", [8, 128, D], BF16, kind="Internal").ap()

    KVW = 2 * NB * 128 + NB * 258 + 64
    KVW += KVW % 2
    KVW = max(KVW, NB * 128 + NB * 130 + 2 + 2 * 8200)
    kv_t = nc.alloc_sbuf_tensor("kvarea", [128, KVW], BF16)
    oat_t = nc.alloc_sbuf_tensor("oat", [128, 4, NT * 128], BF16)
    xb_t = nc.alloc_sbuf_tensor("xb", [128, 2, D], F32)
    xs_t = nc.alloc_sbuf_tensor("xs", [128, 2, D], BF16)
    hn_t = nc.alloc_sbuf_tensor("hnT", [128, 8, 512], BF16)
    cst_t = nc.alloc_sbuf_tensor("cst", [128, 512], F32)
    cbf_t = nc.alloc_sbuf_tensor("cbf", [128, 1024], BF16)
    ARENA_F32 = 12 * 1024 + 512
    ar_t = nc.alloc_sbuf_tensor("arena", [128, ARENA_F32], F32)
    ps_t = nc.alloc_psum_tensor("psf", [128, 7 * 512], F32)
    pst_t = nc.alloc_psum_tensor("pst", [128, 1024], BF16)

    class Arena:
        def __init__(self, ap_f32_2d, nwords):
            self.t = ap_f32_2d
            self.n = nwords
            self.pos = 0

        def reset(self):
            self.pos = 0

        def f32(self, n):
            a = self.t[:, self.pos:self.pos + n]
            self.pos += n
            assert self.pos <= self.n, ("arena overflow", self.pos, self.n)
            return a

        def bf16(self, n):
            n2 = (n + 1) // 2
            return self.f32(n2).bitcast(BF16)[:, 0:n]

    arena = Arena(ar_t[:, :], ARENA_F32)

    o_ = [0]

    def cslot(n):
        a = cst_t[:, o_[0]:o_[0] + n]
        o_[0] += n
        return a

    gcol = cslot(8)
    bcol = cslot(42)
    bhalf = cslot(42)
    bqb = cslot(4)
    cfar = cslot(24)
    zcol = cslot(1)
    gsub8 = cslot(1)
    esink = cslot(8)
    lamt = cslot(8)
    neglam = cslot(1)
    ssq = cslot(2)
    rstd = cslot(2)
    nhalf = cslot(4)
    assert o_[0] <= 512
    ident = cbf_t[:, 0:128]
    ones_row = cbf_t[0:1, 128:256]
    brow = cbf_t[0:1, 256:896]

    B_cst = P.buf("cst")
    xb = P.bufs_n(2, "xb")
    xsb = P.bufs_n(2, "xs")
    xb_sem = [Sem(nc, "xbs%d" % i) for i in range(2)]
    xs_sem = [Sem(nc, "xss%d" % i) for i in range(2)]
    B_hn = P.bufs_n(4, "hn")
    B_ssq = P.bufs_n(2, "ssq")
    B_rstd = P.bufs_n(2, "rstd")
    psb = P.bufs_n(7, "ps")
    B_pst = P.buf("pst")
    B_kt = P.buf("kt")
    B_va = P.buf("va")
    B_oat = P.buf("oat")
    setup_sem = Sem(nc, "setup", group=True)
    ld_sem = Sem(nc, "ld")
    ld_sem.group = False

    def psbank(b, n=512):
        return ps_t[:, b * 512:b * 512 + n]

    arena.reset()
    stage_small = arena.f32(1152)
    B_stage_small = P.buf("stage_small")
    lamv = arena.f32(256)
    sinkr = arena.f32(8)
    gsubr = arena.f32(1)
    identf = arena.f32(128)
    browf = arena.f32(640)

    setup_loads = []
    for (o, i) in [(gcol, gcol_d[:, :]), (bcol, bcol_d[:, :]), (bqb, bqb_d[:, :]), (cfar, cfar_d[:, :]),
                   (lamv, lamv_d[:, :]), (sinkr, sink_d[:, :]), (gsubr, gsub_d[:, :]), (identf, ident_d[:, :]),
                   (browf[0:1, :], brow_d[:, :])]:
        op = Op("sp", (I("dma_start", out=o, in_=i)), setup_sem)
        P.ops["sp"].append(op)
        P.dma_since_barrier.append(op)
        setup_loads.append(op)

    def cop(eng, fn):
        return P.op(eng, fn, reads=[], writes=[B_cst], after=setup_loads)

    cop("dve", I("memset", zcol, 0.0))
    cop("dve", I("memset", nhalf, -0.5))
    cop("dve", I("tensor_scalar", out=bhalf, in0=bcol, scalar1=0.5, scalar2=None, op0=ALU.mult))
    cop("dve", I("tensor_scalar", out=gsub8, in0=gsubr, scalar1=(1.0 - LAM_INIT), scalar2=None, op0=ALU.mult))
    cop("dve", I("tensor_copy", out=ident, in_=identf))
    cop("dve", I("memset", ones_row, 1.0))
    cop("dve", I("tensor_copy", out=brow, in_=browf[0:1, :]))
    prod = arena.f32(128)
    cop("dve", I("tensor_tensor", out=prod[:, 0:64], in0=lamv[:, 0:64], in1=lamv[:, 64:128], op=ALU.mult))
    cop("dve", I("tensor_tensor", out=prod[:, 64:128], in0=lamv[:, 128:192], in1=lamv[:, 192:256], op=ALU.mult))
    cop("dve", I("reduce_sum", out=lamt[:, 0:1], in_=prod[:, 0:64], axis=AX.X))
    cop("dve", I("reduce_sum", out=lamt[:, 1:2], in_=prod[:, 64:128], axis=AX.X))
    cop("act", I("activation", out=lamt[:, 2:4], in_=lamt[:, 0:2], func=AF.Exp))
    cop("act", I("activation", out=esink, in_=sinkr, func=AF.Exp))
    cop("dve", I("tensor_tensor", out=lamt[:, 4:5], in0=lamt[:, 3:4], in1=lamt[:, 2:3], op=ALU.subtract))
    cop("dve", I("tensor_scalar", out=neglam, in0=lamt[:, 4:5], scalar1=-LAM_INIT, scalar2=None, op0=ALU.add))

    cvt_i = [0]

    def convert(src_ap, dst_ap, n, scale_ap=None, scale_f=None, perm_qb=False):
        s = cvt_i[0] % 2
        cvt_i[0] += 1
        st = xb_t[:, s, 0:n]
        ot = xs_t[:, s, 0:n]
        P.op("sp", I("dma_start", out=st, in_=src_ap), writes=[xb[s]], dsem=xb_sem[s])
        if perm_qb:
            src_v = xb_t[:, s, 0:512].rearrange("p (s j d) -> p s j d", s=2, j=4, d=64)
            dst_v = xs_t[:, s, 0:512].rearrange("p (j s d) -> p s j d", s=2, j=4, d=64)
            P.op("dve", I("tensor_scalar", out=dst_v, in0=src_v, scalar1=scale_ap, scalar2=None, op0=ALU.mult),
                 reads=[xb[s], B_cst], writes=[xsb[s]])
            st2 = xb_t[:, s, 512:n]
            ot2 = xs_t[:, s, 512:n]
            o2 = P.op("dve", I("tensor_scalar", out=ot2, in0=st2, scalar1=scale_ap, scalar2=None, op0=ALU.mult),
                      reads=[xb[s], B_cst], writes=[])
            P.op("pool", I("dma_start", out=dst_ap, in_=ot), reads=[xsb[s]], dsem=xs_sem[s], after=[o2])
            xsb[s].r[("dve2")] = o2
        else:
            sc = scale_ap if scale_ap is not None else scale_f
            P.op("dve", I("tensor_scalar", out=ot, in0=st, scalar1=sc, scalar2=None, op0=ALU.mult),
                 reads=[xb[s], B_cst], writes=[xsb[s]])
            P.op("pool", I("dma_start", out=dst_ap, in_=ot), reads=[xsb[s]], dsem=xs_sem[s])

    for k in range(8):
        for cb in range(6):
            c0 = cb * 1024
            n = min(1024, DIN - c0)
            convert(win_d[k * 128:(k + 1) * 128, c0:c0 + n], wscr[k, :, c0:c0 + n], n,
                    scale_ap=gcol[:, k:k + 1], perm_qb=(cb == 2))
    for k in range(4):
        convert(woa_d[k * 128:(k + 1) * 128, :], woascr[k, :, :], 1024, scale_f=0.5)
    for k in range(4):
        convert(wob_d[k * 128:(k + 1) * 128, :], wobscr[k, :, :], 1024, scale_f=0.5)
    for k in range(8):
        convert(wo_d[k * 128:(k + 1) * 128, :], woscr[k, :, :], 1024, scale_f=0.5)
    P.barrier()

    xcnt = [0]

    def prep_tile(src_ap, col):
        s = xcnt[0] % 2
        xcnt[0] += 1
        xt = xb_t[:, s, :]
        xo = xs_t[:, s, :]
        P.op("sp", I("dma_start", out=xt, in_=src_ap), writes=[xb[s]], dsem=xb_sem[s])
        P.op("act", I("activation", out=xo, in_=xt, func=AF.Square, accum_out=ssq[:, s:s + 1]),
             reads=[xb[s]], writes=[xsb[s], B_ssq[s]])
        P.op("dve", I("tensor_scalar", out=rstd[:, s:s + 1], in0=ssq[:, s:s + 1], scalar1=1.0 / D, scalar2=EPS,
                                              op0=ALU.mult, op1=ALU.add), reads=[B_ssq[s]], writes=[B_rstd[s]])
        P.op("pool", I("tensor_tensor", out=rstd[:, s:s + 1], in0=rstd[:, s:s + 1], in1=nhalf[:, 0:1], op=ALU.pow),
             reads=[B_rstd[s], B_cst], writes=[B_rstd[s]])
        P.op("dve", I("tensor_scalar", out=xo, in0=xt, scalar1=rstd[:, s:s + 1], scalar2=None, op0=ALU.mult),
             reads=[xb[s], B_rstd[s]], writes=[xsb[s]])
        for k in range(8):
            P.op("pe", I("transpose", out=pst_t[:, k * 128:(k + 1) * 128], in_=xs_t[:, s, k * 128:(k + 1) * 128],
                                                  identity=ident),
                 reads=[xsb[s]] if k else [xsb[s], B_cst], writes=[B_pst] if k == 0 else [])
        lastT = P.ops["pe"][-1]
        B_pst.w = [lastT]
        xsb[s].r["pe"] = lastT
        P.op("dve", I("tensor_copy", out=hn_t[:, :, col * 128:(col + 1) * 128],
                                            in_=pst_t[:, :].rearrange("p (k t) -> p k t", k=8)),
             reads=[B_pst], writes=[B_hn[col]])

    def tile_src(c):
        return meta_d[:, :] if c == 0 else x_d[(c - 1) * 128:c * 128, :]

    def groups_kv():
        yield [0]
        for g in range(G):
            yield [1 + 4 * g + j for j in range(4)]

    def load_w(dst_ap, src_ap, buf, sem):
        return P.op("sp", I("dma_start", out=dst_ap, in_=src_ap), writes=[buf], dsem=sem)

    wsem = [Sem(nc, "wsem%d" % i) for i in range(8)]

    for p in range(2):
        arena.reset()
        KT = kv_t[:, 0:2 * NB * 128].rearrange("p (h n) -> p h n", h=2)
        VA = kv_t[:, 2 * NB * 128:2 * NB * 128 + NB * 258].rearrange("p (c n) -> p c n", n=258)
        wk = arena.bf16(8 * 256).rearrange("p (k n) -> p k n", k=8)
        wv = arena.bf16(8 * 256).rearrange("p (k n) -> p k n", k=8)
        B_wk, B_wv = P.buf("wk"), P.buf("wv")
        load_w(wk, wscr[:, :, C_KA + p * 256:C_KA + (p + 1) * 256].rearrange("k p n -> p k n"), B_wk, wsem[0])
        load_w(wv, wscr[:, :, C_VA + p * 256:C_VA + (p + 1) * 256].rearrange("k p n -> p k n"), B_wv, wsem[1])
        B_vcol = P.buf("vcol")
        P.op("pool", I("memset", VA[:, :, 128:129], 1.0), writes=[B_vcol])
        P.op("pool", I("memset", VA[:, :, 257:258], 1.0), writes=[B_vcol])
        B_ktc = {}
        B_vac = {}
        for blocks in groups_kv():
            nt = len(blocks)
            c0 = blocks[0]
            for j, c in enumerate(blocks):
                prep_tile(tile_src(c), j)
            for hh in range(2):
                bank = hh
                for k in range(8):
                    P.op("pe", I("matmul",
                        psbank(bank, nt * 128), lhsT=wk[:, k, hh * 128:(hh + 1) * 128], rhs=hn_t[:, k, 0:nt * 128],
                        start=(k == 0), stop=(k == 7)),
                        reads=([B_wk] + B_hn[0:nt]) if k == 0 else [], writes=[psb[bank]] if k == 0 else [])
                last = P.ops["pe"][-1]
                psb[bank].w = [last]
                for b in B_hn[0:nt]:
                    b.r["pe"] = last
                kb = P.buf("ktc")
                B_ktc[(blocks[0], hh)] = kb
                ch = (C_KA + p * 256) // 128 + hh
                P.op("dve", I("tensor_scalar",
                    out=KT[:, hh, c0 * 128:(c0 + nt) * 128], in0=psbank(bank, nt * 128), scalar1=bcol[:, ch:ch + 1],
                    scalar2=None, op0=ALU.add), reads=[psb[bank], B_cst], writes=[kb])
            for j, c in enumerate(blocks):
                bank = 2 + (j % 4)
                for k in range(8):
                    P.op("pe", I("matmul",
                        psbank(bank, 256), lhsT=hn_t[:, k, j * 128:(j + 1) * 128], rhs=wv[:, k, :],
                        start=(k == 0), stop=False),
                        reads=([B_wv, B_hn[j]]) if k == 0 else [], writes=[psb[bank]] if k == 0 else [])
                P.op("pe", I("matmul",
                    psbank(bank, 256), lhsT=ones_row, rhs=brow[0:1, p * 256:(p + 1) * 256], start=False, stop=True),
                    reads=[B_cst])
                last = P.ops["pe"][-1]
                psb[bank].w = [last]
                B_hn[j].r["pe"] = last
                vb = P.buf("vac")
                B_vac[c] = vb
                P.op("act", I("activation",
                    out=VA[:, c, :].rearrange("p (h n) -> p h n", h=2)[:, :, 0:128],
                    in_=psbank(bank, 256).rearrange("p (h n) -> p h n", h=2), func=AF.Copy),
                    reads=[psb[bank]], writes=[vb])
                if c == 0:
                    P.op("pool", I("memset", VA[0:128 - NMETA, 0, :], 0.0), reads=[], writes=[vb], after=list(B_vcol.w))
        P.barrier()

        arena.reset()
        wq = arena.bf16(8 * 256).rearrange("p (k n) -> p k n", k=8)
        strips = arena.bf16(2 * 1152).rearrange("p (h n) -> p h n", h=2)
        QT = arena.bf16(2 * 2 * 512).rearrange("p (s h n) -> p s h n", s=2, h=2)
        ET = arena.bf16(3 * 1024).rearrange("p (s n) -> p s n", s=3)
        accs = arena.f32(8 * 129).rearrange("p (a n) -> p a n", a=8)
        ofin = arena.f32(4 * 128).rearrange("p (j n) -> p j n", j=4)
        t1 = arena.f32(128)
        sq = arena.f32(128)
        onb = arena.bf16(4 * 128).rearrange("p (j n) -> p j n", j=4)
        sm = arena.f32(32)
        sstage = arena.f32(1152)
        B_wq, B_strips = P.buf("wq"), P.buf("strips")
        B_QT = P.bufs_n(2, "QT")
        B_ET = P.bufs_n(3, "ET")
        B_accs, B_fin, B_on, B_sstage = P.buf("accs"), P.buf("fin"), P.buf("on"), P.buf("sstage")
        load_w(wq, wscr[:, :, C_QA + p * 256:C_QA + (p + 1) * 256].rearrange("k p n -> p k n"), B_wq, wsem[0])
        for hh in range(2):
            P.op("sp", I("dma_start", out=sstage, in_=strips_d[2 * p + hh, :, :]), writes=[B_sstage], dsem=wsem[2])
            P.op("dve", I("tensor_copy", out=strips[:, hh, :], in_=sstage), reads=[B_sstage], writes=[B_strips])
        ucnt = [0]
        ecnt = [0]
        for g in range(G):
            qs = g % 2
            for j in range(4):
                prep_tile(tile_src(1 + 4 * g + j), j)
            for hh in range(2):
                bank = 2 * hh
                for k in range(8):
                    P.op("pe", I("matmul",
                        psbank(bank), lhsT=wq[:, k, hh * 128:(hh + 1) * 128], rhs=hn_t[:, k, :],
                        start=(k == 0), stop=(k == 7)),
                        reads=([B_wq] + B_hn) if k == 0 else [], writes=[psb[bank]] if k == 0 else [])
                last = P.ops["pe"][-1]
                psb[bank].w = [last]
                for b in B_hn:
                    b.r["pe"] = last
                ch = (C_QA + p * 256) // 128 + hh
                P.op("dve", I("tensor_scalar",
                    out=QT[:, qs, hh, :], in0=psbank(bank), scalar1=bcol[:, ch:ch + 1], scalar2=0.125,
                    op0=ALU.add, op1=ALU.mult), reads=[psb[bank], B_cst], writes=[B_QT[qs]] if hh == 0 else [])
                if hh == 1:
                    B_QT[qs].w.append(P.ops["dve"][-1])
                    psb[bank].r["dve"] = P.ops["dve"][-1]
            b0 = 1 + 4 * g
            for hh in range(2):
                h = 2 * p + hh
                pending = None

                def emit_scores(c):
                    slot = ucnt[0] % 2
                    ucnt[0] += 1
                    bk = 2 * slot
                    Dd = c - b0
                    mixed = (-1 <= Dd <= 4)
                    P.op("pe", I("matmul", psbank(bk), lhsT=KT[0:64, hh, c * 128:(c + 1) * 128],
                                                  rhs=QT[0:64, qs, hh, :], start=True, stop=not mixed),
                         reads=[B_ktc_get(c, hh), B_QT[qs]], writes=[psb[bk], psb[bk + 1]])
                    P.op("pe", I("matmul", psbank(bk + 1), lhsT=KT[64:128, hh, c * 128:(c + 1) * 128],
                                                  rhs=QT[64:128, qs, hh, :], start=True, stop=not mixed))
                    if mixed:
                        o0 = (4 - Dd) * 128
                        P.op("pe", I("matmul", psbank(bk), lhsT=ident, rhs=strips[:, hh, o0:o0 + 512],
                                                      start=False, stop=True), reads=[B_strips, B_cst])
                        P.op("pe", I("matmul", psbank(bk + 1), lhsT=ident, rhs=strips[:, hh, o0:o0 + 512],
                                                      start=False, stop=True))
                    last = P.ops["pe"][-1]
                    psb[bk].w = [last]
                    psb[bk + 1].w = [last]
                    B_QT[qs].r["pe"] = last
                    es = ecnt[0] % 3
                    ecnt[0] += 1
                    if mixed:
                        bias_ap = zcol
                    elif Dd <= -2:
                        bias_ap = cfar[:, h:h + 1]
                    else:
                        bias_ap = cfar[:, 12 + h:13 + h]
                    P.op("act", I("activation", out=ET[:, es, :], in_=ps_t[:, bk * 512:bk * 512 + 1024], func=AF.Exp,
                                                       bias=bias_ap, scale=1.0),
                         reads=[psb[bk], psb[bk + 1], B_cst], writes=[B_ET[es]])
                    return (c, es)

                def B_ktc_get(c, hh):
                    for (c0, h2), b in B_ktc.items():
                        if h2 == hh and (c0 == c or (c0 >= 1 and c0 <= c < c0 + 4 and c >= 1)):
                            return b
                    raise KeyError((c, hh))

                def emit_av(c, es):
                    first = (c == 0)
                    lastc = (c == NB - 1)
                    for j in range(4):
                        for m in range(2):
                            a = 2 * j + m
                            bank = 4 + a // 3
                            off = (a % 3) * 129
                            P.op("pe", I("matmul",
                                ps_t[:, bank * 512 + off:bank * 512 + off + 129],
                                lhsT=ET[:, es, m * 512 + j * 128:m * 512 + (j + 1) * 128],
                                rhs=VA[:, c, hh * 129:hh * 129 + 129], start=(first and a % 3 == 0), stop=lastc,
                                skip_group_check=True),
                                reads=([B_ET[es], B_vac[c]]) if a == 0 else [],
                                writes=[psb[4], psb[5], psb[6]] if (first and a == 0) else [])
                    last = P.ops["pe"][-1]
                    B_ET[es].r["pe"] = last
                    if lastc:
                        for b in (psb[4], psb[5], psb[6]):
                            b.w = [last]

                prev = emit_scores(0)
                for c in range(1, NB):
                    cur = emit_scores(c)
                    emit_av(*prev)
                    prev = cur
                emit_av(*prev)
                for bi, na in ((4, 3), (5, 3), (6, 2)):
                    a0 = (bi - 4) * 3
                    P.op("dve", I("tensor_copy",
                        out=accs[:, a0:a0 + na, :], in_=psbank(bi, na * 129).rearrange("p (a n) -> p a n", a=na)),
                        reads=[psb[bi]], writes=[B_accs] if bi == 4 else [])
                    if bi != 4:
                        B_accs.w.append(P.ops["dve"][-1])
                        psb[bi].r["dve"] = P.ops["dve"][-1]

                def fin(fn, first=False):
                    return P.op("dve", fn, reads=[B_accs, B_fin, B_cst], writes=[B_fin])

                fin(I("reciprocal", out=sm[:, 0:8], in_=accs[:, :, 128:129].rearrange("p a o -> p (a o)")))
                fin(I("tensor_scalar", out=sm[:, 8:12], in0=sm[:, 0:8].rearrange("p (j m) -> p j m", m=2)[:, :, 1],
                                              scalar1=neglam, scalar2=None, op0=ALU.mult))
                for j in range(4):
                    fin(I("tensor_scalar", out=t1, in0=accs[:, 2 * j + 1, 0:128], scalar1=sm[:, 8 + j:9 + j],
                                                       scalar2=None, op0=ALU.mult))
                    fin(I("scalar_tensor_tensor", out=ofin[:, j, :], in0=accs[:, 2 * j, 0:128],
                                                              scalar=sm[:, 2 * j:2 * j + 1], in1=t1, op0=ALU.mult, op1=ALU.add))
                    fin(I("tensor_tensor", out=sq, in0=ofin[:, j, :], in1=ofin[:, j, :], op=ALU.mult))
                    fin(I("reduce_sum", out=sm[:, 12 + j:13 + j], in_=sq, axis=AX.X))
                fin(I("tensor_scalar", out=sm[:, 16:20], in0=sm[:, 12:16], scalar1=1.0 / 128, scalar2=EPS,
                                              op0=ALU.mult, op1=ALU.add))
                P.op("pool", I("tensor_tensor", out=sm[:, 16:20], in0=sm[:, 16:20], in1=nhalf[:, 0:4], op=ALU.pow),
                     reads=[B_fin, B_cst], writes=[B_fin])
                for j in range(4):
                    P.op("dve", I("tensor_scalar", out=onb[:, j, :], in0=ofin[:, j, :], scalar1=sm[:, 16 + j:17 + j],
                                                               scalar2=None, op0=ALU.mult),
                         reads=[B_fin], writes=[B_on] if j == 0 else [])
                    if j:
                        B_on.w.append(P.ops["dve"][-1])
                for j in range(4):
                    P.op("pe", I("transpose", out=pst_t[:, j * 128:(j + 1) * 128], in_=onb[:, j, :], identity=ident),
                         reads=[B_on, B_cst] if j == 0 else [], writes=[B_pst] if j == 0 else [])
                lastT = P.ops["pe"][-1]
                B_pst.w = [lastT]
                B_on.r["pe"] = lastT
                P.op("dve", I("tensor_scalar", out=oat_t[:, h, g * 512:(g + 1) * 512], in0=pst_t[:, 0:512],
                                                               scalar1=gsub8, scalar2=None, op0=ALU.mult),
                     reads=[B_pst, B_cst], writes=[])
                B_pst.r["dve"] = P.ops["dve"][-1]
        P.barrier()

    arena.reset()
    KTB = kv_t[:, 0:NB * 128]
    VB = kv_t[:, NB * 128:NB * 128 + NB * 130].rearrange("p (c n) -> p c n", n=130)
    kvfree0 = NB * 128 + NB * 130
    kvfree0 += kvfree0 % 2
    wkb = arena.bf16(8 * 128).rearrange("p (k n) -> p k n", k=8)
    wvb = arena.bf16(8 * 128).rearrange("p (k n) -> p k n", k=8)
    B_wk, B_wv = P.buf("wkb"), P.buf("wvb")
    load_w(wkb, wscr[:, :, C_KB:C_KB + 128].rearrange("k p n -> p k n"), B_wk, wsem[0])
    load_w(wvb, wscr[:, :, C_VB:C_VB + 128].rearrange("k p n -> p k n"), B_wv, wsem[1])
    B_vcol = P.buf("vcolb")
    P.op("pool", I("memset", VB[:, :, 64:65], 1.0), writes=[B_vcol])
    P.op("pool", I("memset", VB[:, :, 129:130], 1.0), writes=[B_vcol])
    for blocks in groups_kv():
        nt = len(blocks)
        c0 = blocks[0]
        for j, c in enumerate(blocks):
            prep_tile(tile_src(c), j)
        for k in range(8):
            P.op("pe", I("matmul", psbank(0, nt * 128), lhsT=wkb[:, k, :], rhs=hn_t[:, k, 0:nt * 128],
                                               start=(k == 0), stop=(k == 7)),
                 reads=([B_wk] + B_hn[0:nt]) if k == 0 else [], writes=[psb[0]] if k == 0 else [])
        last = P.ops["pe"][-1]
        psb[0].w = [last]
        for b in B_hn[0:nt]:
            b.r["pe"] = last
        P.op("dve", I("tensor_scalar",
            out=KTB[:, c0 * 128:(c0 + nt) * 128], in0=psbank(0, nt * 128), scalar1=bcol[:, C_KB // 128:C_KB // 128 + 1],
            scalar2=None, op0=ALU.add), reads=[psb[0], B_cst], writes=[])
        psb[0].r["dve"] = P.ops["dve"][-1]
        for j, c in enumerate(blocks):
            bank = 2 + (j % 4)
            for k in range(8):
                P.op("pe", I("matmul",
                    psbank(bank, 128), lhsT=hn_t[:, k, j * 128:(j + 1) * 128], rhs=wvb[:, k, :], start=(k == 0), stop=False),
                    reads=([B_wv, B_hn[j]]) if k == 0 else [], writes=[psb[bank]] if k == 0 else [])
            P.op("pe", I("matmul", psbank(bank, 128), lhsT=ones_row, rhs=brow[0:1, 512:640],
                                                     start=False, stop=True), reads=[B_cst])
            last = P.ops["pe"][-1]
            psb[bank].w = [last]
            B_hn[j].r["pe"] = last
            P.op("act", I("activation",
                out=VB[:, c, :].rearrange("p (h n) -> p h n", h=2)[:, :, 0:64],
                in_=psbank(bank, 128).rearrange("p (h n) -> p h n", h=2), func=AF.Copy),
                reads=[psb[bank]], writes=[])
            psb[bank].r["act"] = P.ops["act"][-1]
            if c == 0:
                P.op("pool", I("memset", VB[0:128 - NMETA, 0, :], 0.0), after=list(B_vcol.w) + [P.ops["act"][-1]])
    P.barrier()

    arena.reset()
    ar2 = Arena(kv_t[:, kvfree0:KVW].bitcast(F32), (KVW - kvfree0) // 2)
    biasB = arena.bf16(8 * 384).rearrange("p (h n) -> p h n", h=8)
    Woa = arena.bf16(4 * 1024).rearrange("p (k n) -> p k n", k=4)
    Wob = arena.bf16(4 * 1024).rearrange("p (k n) -> p k n", k=4)
    Wring = [arena.bf16(4096), arena.bf16(4096)]
    postg = arena.f32(1024)
    ttile = arena.f32(512)
    zf = arena.f32(512)
    sz = arena.f32(512)
    mixedT = ar2.bf16(8 * 512).rearrange("p (k n) -> p k n", k=8)
    goaT = ar2.bf16(4 * 512).rearrange("p (k n) -> p k n", k=4)
    gobT = ar2.bf16(4 * 512).rearrange("p (k n) -> p k n", k=4)
    QTb = ar2.bf16(4 * 512).rearrange("p (k n) -> p k n", k=4)
    obt = ar2.bf16(4 * 512).rearrange("p (t n) -> p t n", t=4)
    EB = [ar2.bf16(384), ar2.bf16(384)]
    outt = ar2.f32(1024)
    ta = arena.f32(512)
    tb = ar2.f32(512)
    smf = ar2.f32(16)
    bstage = zf[:, 0:384]
    maskb = sz[:, 0:384]
    B_biasB, B_Woa, B_Wob, B_postg = P.buf("biasB"), P.buf("Woa"), P.buf("Wob"), P.buf("postg")
    B_ring = P.bufs_n(2, "ring")
    ring_sem = [wsem[3], wsem[4]]
    B_t, B_zf, B_sz, B_mixed, B_goa, B_gob, B_QTb = (P.buf("t"), P.buf("zf"), P.buf("sz"), P.buf("mixed"), P.buf("goa"),
                                                     P.buf("gob"), P.buf("QTb"))
    B_ob = P.bufs_n(4, "ob")
    B_EB = P.bufs_n(2, "EB")
    B_out, B_ta, B_tb, B_smf, B_bstage, B_maskb = P.buf("out"), P.buf("ta"), P.buf("tb"), P.buf("smf"), P.buf("bstage"), P.buf("maskb")
    out_sem = wsem[5]
    load_w(Woa, woascr.rearrange("k p n -> p k n"), B_Woa, wsem[0])
    load_w(Wob, wobscr.rearrange("k p n -> p k n"), B_Wob, wsem[1])
    load_w(postg, postg_d[:, :], B_postg, wsem[2])
    load_w(maskb, maskb_d[:, :], B_maskb, wsem[6])
    for h8 in range(8):
        P.op("sp", I("dma_start", out=bstage, in_=strips_d[4 + h8, :, 384:768]), writes=[B_bstage], dsem=wsem[7])
        P.op("dve", I("tensor_tensor", out=biasB[:, h8, :], in0=bstage, in1=maskb, op=ALU.add),
             reads=[B_bstage, B_maskb], writes=[B_biasB])
    rcnt = [0]

    def ring_load(src_ap):
        s = rcnt[0] % 2
        rcnt[0] += 1
        load_w(Wring[s].rearrange("p (k n) -> p k n", k=8), src_ap, B_ring[s], ring_sem[s])
        return s

    def proj_fm(wblk, s, col0, bank):
        for k in range(8):
            P.op("pe", I("matmul", psbank(bank), lhsT=wblk[:, k, col0:col0 + 128], rhs=hn_t[:, k, :],
                                               start=(k == 0), stop=(k == 7)),
                 reads=([B_ring[s]] + B_hn) if k == 0 else [], writes=[psb[bank]] if k == 0 else [])
        last = P.ops["pe"][-1]
        psb[bank].w = [last]
        B_ring[s].r["pe"] = last
        for b in B_hn:
            b.r["pe"] = last

    ebc = [0]
    for g in range(G):
        for j in range(4):
            prep_tile(tile_src(1 + 4 * g + j), j)
        s = ring_load(wscr[:, :, C_QB:C_QB + 512].rearrange("k p n -> p k n"))
        wblk = Wring[s].rearrange("p (k n) -> p k n", k=8)
        for j in range(4):
            bank = j % 2
            proj_fm(wblk, s, j * 128, bank)
            P.op("dve", I("tensor_scalar", out=QTb[:, j, :], in0=psbank(bank), scalar1=bqb[:, j:j + 1],
                                                                 scalar2=0.125, op0=ALU.add, op1=ALU.mult),
                 reads=[psb[bank], B_cst], writes=[B_QTb] if j == 0 else [])
            if j:
                B_QTb.w.append(P.ops["dve"][-1])
                psb[bank].r["dve"] = P.ops["dve"][-1]
        for tq in range(4):
            b = 1 + 4 * g + tq
            cs = [c for c in (b - 1, b, b + 1) if 0 <= c < NB]
            first_ob = True
            for h8 in range(8):
                jq, half = h8 % 4, h8 // 4
                r0 = half * 64
                sbank = 2 + (ebc[0] % 2)
                es = ebc[0] % 2
                ebc[0] += 1
                n0 = (cs[0] - (b - 1)) * 128
                for i, c in enumerate(cs):
                    jj = c - (b - 1)
                    P.op("pe", I("matmul",
                        ps_t[:, sbank * 512 + jj * 128:sbank * 512 + (jj + 1) * 128],
                        lhsT=KTB[r0:r0 + 64, c * 128:(c + 1) * 128], rhs=QTb[r0:r0 + 64, jq, tq * 128:(tq + 1) * 128],
                        start=True, stop=False), reads=[B_QTb] if i == 0 else [], writes=[psb[sbank]] if i == 0 else [])
                    P.op("pe", I("matmul",
                        ps_t[:, sbank * 512 + jj * 128:sbank * 512 + (jj + 1) * 128],
                        lhsT=ident, rhs=biasB[:, h8, (2 - jj) * 128:(3 - jj) * 128], start=False, stop=True),
                        reads=[B_biasB, B_cst] if i == 0 else [])
                last = P.ops["pe"][-1]
                psb[sbank].w = [last]
                B_QTb.r["pe"] = last
                ncol = len(cs) * 128
                P.op("act", I("activation", out=EB[es][:, n0:n0 + ncol],
                                                                   in_=ps_t[:, sbank * 512 + n0:sbank * 512 + n0 + ncol], func=AF.Exp),
                     reads=[psb[sbank]], writes=[B_EB[es]])
                abank = 4 + (h8 % 2)
                for i, c in enumerate(cs):
                    jj = c - (b - 1)
                    P.op("pe", I("matmul",
                        ps_t[:, abank * 512:abank * 512 + 65], lhsT=EB[es][:, jj * 128:(jj + 1) * 128],
                        rhs=VB[:, c, half * 65:half * 65 + 65], start=(i == 0), stop=(i == len(cs) - 1)),
                        reads=[B_EB[es]] if i == 0 else [], writes=[psb[abank]] if i == 0 else [])
                last = P.ops["pe"][-1]
                psb[abank].w = [last]
                B_EB[es].r["pe"] = last
                P.op("dve", I("tensor_scalar", out=smf[:, h8:h8 + 1], in0=ps_t[:, abank * 512 + 64:abank * 512 + 65],
                                                      scalar1=esink[:, h8:h8 + 1], scalar2=None, op0=ALU.add),
                     reads=[psb[abank], B_cst], writes=[B_smf])
                P.op("dve", I("reciprocal", out=smf[:, 8 + h8:9 + h8], in_=smf[:, h8:h8 + 1]), reads=[B_smf], writes=[B_smf])
                P.op("dve", I("tensor_scalar", out=obt[:, tq, h8 * 64:(h8 + 1) * 64], in0=ps_t[:, abank * 512:abank * 512 + 64],
                                                      scalar1=smf[:, 8 + h8:9 + h8], scalar2=None, op0=ALU.mult),
                     reads=[psb[abank], B_smf], writes=[B_ob[tq]] if first_ob else [])
                if not first_ob:
                    B_ob[tq].w.append(P.ops["dve"][-1])
                    psb[abank].r["dve"] = P.ops["dve"][-1]
                first_ob = False

        def gate(col_base, which):
            s = ring_load(wscr[:, :, col_base:col_base + 512].rearrange("k p n -> p k n"))
            wblk = Wring[s].rearrange("p (k n) -> p k n", k=8)
            for c in range(4):
                bank = c % 2
                ch = col_base // 128 + c
                proj_fm(wblk, s, c * 128, bank)
                P.op("act", I("activation", out=ttile, in_=psbank(bank), func=AF.Tanh,
                                                                   bias=bhalf[:, ch:ch + 1], scale=0.5),
                     reads=[psb[bank], B_cst], writes=[B_t])
                P.op("act", I("activation", out=zf, in_=psbank(bank), func=AF.Identity,
                                                                   bias=bcol[:, ch:ch + 1], scale=1.0),
                     reads=[psb[bank], B_cst], writes=[B_zf])
                P.op("dve", I("scalar_tensor_tensor", out=sz, in0=ttile, scalar=1.0, in1=zf, op0=ALU.add, op1=ALU.mult),
                     reads=[B_t, B_zf], writes=[B_sz])
                if which == "a":
                    P.op("dve", I("tensor_tensor", out=goaT[:, c, :], in0=oat_t[:, c, g * 512:(g + 1) * 512], in1=sz,
                                                               op=ALU.mult), reads=[B_sz], writes=[B_goa] if c == 0 else [])
                    if c:
                        B_goa.w.append(P.ops["dve"][-1])
                        B_sz.r["dve"] = P.ops["dve"][-1]
                else:
                    for tq in range(4):
                        P.op("pe", I("transpose", out=pst_t[:, tq * 128:(tq + 1) * 128],
                                                                    in_=obt[:, tq, c * 128:(c + 1) * 128], identity=ident),
                             reads=[B_ob[tq], B_cst], writes=[B_pst] if tq == 0 else [])
                    lastT = P.ops["pe"][-1]
                    B_pst.w = [lastT]
                    for tq in range(4):
                        B_ob[tq].r["pe"] = lastT
                    P.op("dve", I("tensor_tensor", out=gobT[:, c, :], in0=pst_t[:, 0:512], in1=sz, op=ALU.mult),
                         reads=[B_pst, B_sz], writes=[B_gob] if c == 0 else [])
                    if c:
                        B_gob.w.append(P.ops["dve"][-1])
                        B_sz.r["dve"] = P.ops["dve"][-1]
                        B_pst.r["dve"] = P.ops["dve"][-1]

        gate(C_ZB, "b")
        gate(C_ZA, "a")
        for half in range(2):
            sa = ring_load(wscr[:, :, C_GA + half * 512:C_GA + (half + 1) * 512].rearrange("k p n -> p k n"))
            wa = Wring[sa].rearrange("p (k n) -> p k n", k=8)
            sb = None
            for c4 in range(4):
                fc = half * 4 + c4
                proj_fm(wa, sa, c4 * 128, 0)
                cha = C_GA // 128 + fc
                P.op("act", I("activation", out=ta, in_=psbank(0), func=AF.Tanh, bias=bhalf[:, cha:cha + 1], scale=0.5),
                     reads=[psb[0], B_cst], writes=[B_ta])
                if c4 == 0:
                    sb = ring_load(wscr[:, :, C_GB + half * 512:C_GB + (half + 1) * 512].rearrange("k p n -> p k n"))
                    wb = Wring[sb].rearrange("p (k n) -> p k n", k=8)
                proj_fm(wb, sb, c4 * 128, 1)
                chb = C_GB // 128 + fc
                P.op("act", I("activation", out=tb, in_=psbank(1), func=AF.Tanh, bias=bhalf[:, chb:chb + 1], scale=0.5),
                     reads=[psb[1], B_cst], writes=[B_tb])
                for k in range(4):
                    P.op("pe", I("matmul", psbank(2), lhsT=Woa[:, k, fc * 128:(fc + 1) * 128], rhs=goaT[:, k, :],
                                                              start=(k == 0), stop=(k == 3)),
                         reads=[B_Woa, B_goa] if k == 0 else [], writes=[psb[2]] if k == 0 else [])
                last = P.ops["pe"][-1]
                psb[2].w = [last]
                B_goa.r["pe"] = last
                for k in range(4):
                    P.op("pe", I("matmul", psbank(3), lhsT=Wob[:, k, fc * 128:(fc + 1) * 128], rhs=gobT[:, k, :],
                                                              start=(k == 0), stop=(k == 3)),
                         reads=[B_Wob, B_gob] if k == 0 else [], writes=[psb[3]] if k == 0 else [])
                last = P.ops["pe"][-1]
                psb[3].w = [last]
                B_gob.r["pe"] = last
                P.op("dve", I("scalar_tensor_tensor", out=ta, in0=ta, scalar=1.0, in1=psbank(2), op0=ALU.add, op1=ALU.mult),
                     reads=[psb[2]], writes=[B_ta])
                P.op("dve", I("scalar_tensor_tensor", out=tb, in0=tb, scalar=1.0, in1=psbank(3), op0=ALU.add, op1=ALU.mult),
                     reads=[psb[3]], writes=[B_tb])
                P.op("dve", I("tensor_tensor", out=mixedT[:, fc, :], in0=ta, in1=tb, op=ALU.add),
                     reads=[B_ta, B_tb], writes=[B_mixed] if fc == 0 else [])
                if fc:
                    B_mixed.w.append(P.ops["dve"][-1])
                    B_ta.r["dve"] = P.ops["dve"][-1]
                    B_tb.r["dve"] = P.ops["dve"][-1]
        so = [ring_load(woscr[:, :, hf * 512:(hf + 1) * 512].rearrange("k p n -> p k n")) for hf in range(2)]
        for tq in range(4):
            for hf in range(2):
                wblk = Wring[so[hf]].rearrange("p (k n) -> p k n", k=8)
                bank = 4 + hf
                for k in range(8):
                    P.op("pe", I("matmul",
                        psbank(bank), lhsT=mixedT[:, k, tq * 128:(tq + 1) * 128], rhs=wblk[:, k, :], start=(k == 0), stop=(k == 7)),
                        reads=[B_ring[so[hf]], B_mixed] if k == 0 else [], writes=[psb[bank]] if k == 0 else [])
                last = P.ops["pe"][-1]
                psb[bank].w = [last]
                B_ring[so[hf]].r["pe"] = last
                B_mixed.r["pe"] = last
            P.op("act", I("activation", out=outt[:, 0:512], in_=psbank(4), func=AF.Square, accum_out=smf[:, 0:1]),
                 reads=[psb[4]], writes=[B_out, B_smf])
            P.op("act", I("activation", out=outt[:, 512:1024], in_=psbank(5), func=AF.Square, accum_out=smf[:, 1:2]),
                 reads=[psb[5]], writes=[B_out, B_smf])
            P.op("dve", I("tensor_tensor", out=smf[:, 2:3], in0=smf[:, 0:1], in1=smf[:, 1:2], op=ALU.add), reads=[B_smf], writes=[B_smf])
            P.op("dve", I("tensor_scalar", out=smf[:, 2:3], in0=smf[:, 2:3], scalar1=1.0 / D, scalar2=EPS, op0=ALU.mult, op1=ALU.add),
                 reads=[B_smf], writes=[B_smf])
            P.op("pool", I("tensor_tensor", out=smf[:, 3:4], in0=smf[:, 2:3], in1=nhalf[:, 0:1], op=ALU.pow),
                 reads=[B_smf], writes=[B_smf])
            s = xcnt[0] % 2
            xcnt[0] += 1
            xsrc = x_d[(4 * g + tq) * 128:(4 * g + tq + 1) * 128, :]
            P.op("sp", I("dma_start", out=xb_t[:, s, :], in_=xsrc), writes=[xb[s]], dsem=xb_sem[s])
            for hf in range(2):
                P.op("dve", I("scalar_tensor_tensor", out=outt[:, hf * 512:(hf + 1) * 512], in0=psbank(4 + hf),
                                                                    scalar=smf[:, 3:4], in1=postg[:, hf * 512:(hf + 1) * 512],
                                                                    op0=ALU.mult, op1=ALU.mult),
                     reads=[psb[4 + hf], B_smf, B_postg], writes=[B_out])
            P.op("dve", I("tensor_tensor", out=outt, in0=outt, in1=xb_t[:, s, :], op=ALU.add),
                 reads=[xb[s]], writes=[B_out])
            ysl = y_d[(4 * g + tq) * 128:(4 * g + tq + 1) * 128, :]
            P.op("pool", I("dma_start", out=ysl, in_=outt), reads=[B_out], dsem=out_sem)
    P.barrier()

    with nc.Block() as block:
        P.flush(block)
    return nc


_NC_CACHE = {}


def host_consts(rel_bias, pre_norm_g, b_in, lambda_q1, lambda_k1, lambda_q2, lambda_k2, subln_g, sink, post_norm_g):
    f = np.float32
    rel_bias = np.asarray(rel_bias, f)
    b = np.asarray(b_in, f).reshape(-1)
    kk = np.arange(128)[:, None]
    v = np.arange(1152)[None, :]
    bucket = t5_bucket_np((512 + kk - v).astype(np.int32))
    strips = np.ascontiguousarray(np.transpose(rel_bias[bucket], (2, 0, 1)))
    cfar = np.ascontiguousarray(np.broadcast_to(np.concatenate([rel_bias[15], rel_bias[31]])[None, :], (128, 24)))
    x384 = np.arange(384)[None, :]
    rel = 128 + kk - x384
    maskb = np.where(np.abs(rel) <= 128, 0.0, -30000.0).astype(f)
    gcol = np.ascontiguousarray(np.asarray(pre_norm_g, f).reshape(8, 128).T)
    bcol = np.ascontiguousarray(b.reshape(42, 128).T)
    bq = b[C_QB:C_QB + 512].reshape(2, 4, 64)
    bqb = np.ascontiguousarray(np.transpose(bq, (1, 0, 2)).reshape(4, 128).T)
    brow = np.ascontiguousarray(np.concatenate([b[C_VA:C_VA + 512], b[C_VB:C_VB + 128]])[None, :])
    lamv = np.concatenate([np.asarray(a, f).reshape(-1) for a in (lambda_q1, lambda_k1, lambda_q2, lambda_k2)])
    lamv = np.ascontiguousarray(np.broadcast_to(lamv[None, :], (128, 256)))
    gsub = np.ascontiguousarray(np.asarray(subln_g, f).reshape(128, 1))
    sinkr = np.ascontiguousarray(np.broadcast_to(np.asarray(sink, f).reshape(1, 8), (128, 8)))
    postg = np.ascontiguousarray(np.broadcast_to(np.asarray(post_norm_g, f).reshape(1, D), (128, D)))
    ident = np.eye(128, dtype=f)
    return dict(strips=strips, cfar=cfar, maskb=maskb, gcol=gcol, bcol=bcol, bqb=bqb, brow=brow, lamv=lamv, gsub=gsub,
                sinkr=sinkr, postg=postg, ident=ident)


def kernel(x, meta_tokens, rel_bias, pre_norm_g, w_in, b_in, lambda_q1, lambda_k1, lambda_q2, lambda_k2,
           subln_g, sink, w_out_a, w_out_b, w_out, post_norm_g):
    x = np.asarray(x, np.float32)
    B, S, _ = x.shape
    NT = S // 128
    if NT not in _NC_CACHE:
        _NC_CACHE[NT] = build(NT)
    nc = _NC_CACHE[NT]
    consts = host_consts(rel_bias, pre_norm_g, b_in, lambda_q1, lambda_k1, lambda_q2, lambda_k2, subln_g, sink, post_norm_g)
    meta_pad = np.zeros((128, D), np.float32)
    meta_pad[128 - NMETA:] = np.asarray(meta_tokens, np.float32)
    shared = dict(meta_pad=meta_pad, w_in=np.ascontiguousarray(np.asarray(w_in, np.float32)[0]),
                  w_out_a=np.ascontiguousarray(np.asarray(w_out_a, np.float32)[0]),
                  w_out_b=np.ascontiguousarray(np.asarray(w_out_b, np.float32)[0]),
                  w_out=np.ascontiguousarray(np.asarray(w_out, np.float32)[0]), **consts)
    in_maps = [dict(x=np.ascontiguousarray(x[i]), **shared) for i in range(B)]
    res = run_bass_kernel_spmd(nc, in_maps, core_ids=list(range(B)))
    return np.stack([np.asarray(r["y"], np.float32) for r in res.results], axis=0)
```

```python
import math
import numpy as np
import concourse.bass as bass
import concourse.mybir as mybir
from concourse.bass_utils import run_bass_kernel_spmd

F32 = mybir.dt.float32
BF16 = mybir.dt.bfloat16
AF = mybir.ActivationFunctionType
ALU = mybir.AluOpType
AX = mybir.AxisListType

D = 1024
DIN = 5376
EPS = 1e-6
NMETA = 16
ENGS = ("sp", "act", "dve", "pool", "pe")
C_QA, C_KA, C_VA, C_ZA, C_QB, C_KB, C_VB, C_ZB, C_GA, C_GB = 0, 512, 1024, 1536, 2048, 2560, 2688, 2816, 3328, 4352
LAM_INIT = 0.8 - 0.6 * math.exp(-0.3 * 0)


class Sem:
    def __init__(self, nc, name, group=False):
        self.h = nc.alloc_semaphore(name)
        self.name = name
        self.v = 0
        self.group = group


class Op:
    __slots__ = ("eng", "fn", "deps", "dsem", "sem", "val", "need", "idx")

    def __init__(self, eng, fn, dsem):
        self.eng, self.fn, self.dsem = eng, fn, dsem
        self.deps = []
        self.sem = None
        self.val = 0
        self.need = False


def I(method, *args, **kw):
    return (method, args, kw)


class Buf:
    __slots__ = ("w", "r", "name")

    def __init__(self, name=""):
        self.name = name
        self.w = []
        self.r = {}


class Prog:
    def __init__(self, nc):
        self.nc = nc
        self.ops = {e: [] for e in ENGS}
        self.esem = {e: Sem(nc, "es_" + e) for e in ENGS}
        self.bufs = []
        self.dma_since_barrier = []
        self.marks = []

    def buf(self, name=""):
        b = Buf(name)
        self.bufs.append(b)
        return b

    def bufs_n(self, n, name=""):
        return [self.buf(name + str(i)) for i in range(n)]

    def op(self, eng, fn, reads=(), writes=(), dsem=None, after=()):
        o = Op(eng, fn, dsem)
        deps = {}
        for b in reads:
            for d in b.w:
                deps[id(d)] = d
        for b in writes:
            for d in b.w:
                deps[id(d)] = d
            for d in b.r.values():
                deps[id(d)] = d
        for d in after:
            deps[id(d)] = d
        for d in deps.values():
            if d.eng == "pe" and eng == "pe" and d.dsem is None:
                continue
            if d is o:
                continue
            o.deps.append(d)
            d.need = True
        for b in writes:
            b.w = [o]
            b.r = {}
        for b in reads:
            if b in writes:
                continue
            key = eng if dsem is None else ("dma", id(dsem))
            b.r[key] = o
        self.ops[eng].append(o)
        if dsem is not None:
            self.dma_since_barrier.append(o)
        return o

    def barrier(self):
        self.marks.append(sum(1 for o in self.ops["pe"] if o.fn is not None))
        lasts = []
        for e in ENGS:
            for o in reversed(self.ops[e]):
                if o.fn is not None and o.dsem is None:
                    lasts.append(o)
                    break
        deps = lasts + self.dma_since_barrier
        self.dma_since_barrier = []
        for e in ENGS:
            self.op(e, None, after=deps)
        for b in self.bufs:
            b.w = []
            b.r = {}

    def flush(self, block):
        for e in ENGS:
            for o in self.ops[e]:
                if o.fn is None:
                    continue
                if o.dsem is not None:
                    o.dsem.v += 16
                    o.sem, o.val = o.dsem, o.dsem.v
                elif o.need:
                    s = self.esem[e]
                    s.v += 1
                    o.sem, o.val = s, s.v
        engmap = {"sp": block.sync, "act": block.scalar, "dve": block.vector, "pool": block.gpsimd, "pe": block.tensor}
        for e in ENGS:
            ops = self.ops[e]

            def body(eng, ops=ops):
                waited = {}
                for o in ops:
                    for d in o.deps:
                        v = d.sem.v if d.sem.group else d.val
                        if waited.get(d.sem.name, 0) >= v:
                            continue
                        waited[d.sem.name] = v
                        eng.wait_ge(d.sem.h, v)
                    if o.fn is None:
                        continue
                    m, a, kw = o.fn
                    ins = getattr(eng, m)(*a, **kw)
                    if o.dsem is not None:
                        ins.then_inc(o.sem.h, 16)
                    elif o.need:
                        ins.then_inc(o.sem.h, 1)

            engmap[e](body)


def t5_bucket_np(rel):
    half = 16
    max_exact = 8
    ret = np.where(rel > 0, half, 0)
    n = np.abs(rel)
    nf = np.maximum(n, 1).astype(np.float32)
    large = max_exact + (np.log(nf / max_exact) / math.log(128 / max_exact) * (half - max_exact)).astype(np.int32)
    large = np.minimum(large, half - 1)
    return ret + np.where(n < max_exact, n, large)


def build(NT):
    G = NT // 4
    NB = NT + 1
    nc = bass.Bass("TRN2", target_bir_lowering=False)
    P = Prog(nc)

    def din(name, shape, dt=F32):
        return nc.dram_tensor(name, shape, dt, kind="ExternalInput").ap()

    x_d = din("x", [NT * 128, D])
    meta_d = din("meta_pad", [128, D])
    win_d = din("w_in", [D, DIN])
    woa_d = din("w_out_a", [512, D])
    wob_d = din("w_out_b", [512, D])
    wo_d = din("w_out", [D, D])
    gcol_d = din("gcol", [128, 8])
    bcol_d = din("bcol", [128, 42])
    bqb_d = din("bqb", [128, 4])
    brow_d = din("brow", [1, 640])
    strips_d = din("strips", [12, 128, 1152])
    cfar_d = din("cfar", [128, 24])
    maskb_d = din("maskb", [128, 384])
    lamv_d = din("lamv", [128, 256])
    gsub_d = din("gsub", [128, 1])
    sink_d = din("sinkr", [128, 8])
    postg_d = din("postg", [128, D])
    ident_d = din("ident", [128, 128])
    y_d = nc.dram_tensor("y", [NT * 128, D], F32, kind="ExternalOutput").ap()
    wscr = nc.dram_tensor("wscr", [8, 128, DIN], BF16, kind="Internal").ap()
    woascr = nc.dram_tensor("woascr", [4, 128, D], BF16, kind="Internal").ap()
    wobscr = nc.dram_tensor("wobscr", [4, 128, D], BF16, kind="Internal").ap()
    woscr = nc.dram_tensor("woscr", [8, 128, D], BF16, kind="Internal").ap()
    hscr = nc.dram_tensor("hscr", [G + 1, 128, 4096], BF16, kind="Internal").ap()

    KVW = 2 * NB * 128 + NB * 258 + 64
    KVW += KVW % 2
    KVW = max(KVW, NB * 128 + NB * 130 + 2 + 2 * 8200)
    kv_t = nc.alloc_sbuf_tensor("kvarea", [128, KVW], BF16)
    oat_t = nc.alloc_sbuf_tensor("oat", [128, 4, NT * 128], BF16)
    xb_t = nc.alloc_sbuf_tensor("xb", [128, 2 * D], F32)
    xs_t = nc.alloc_sbuf_tensor("xs", [128, 2, D], BF16)
    hn_t = nc.alloc_sbuf_tensor("hnT", [128, 8, 512], BF16)
    cst_t = nc.alloc_sbuf_tensor("cst", [128, 512], F32)
    cbf_t = nc.alloc_sbuf_tensor("cbf", [128, 1024], BF16)
    ARENA_F32 = 12 * 1024 + 512
    ar_t = nc.alloc_sbuf_tensor("arena", [128, ARENA_F32], F32)
    ps_t = nc.alloc_psum_tensor("psf", [128, 7 * 512], F32)
    pst_t = nc.alloc_psum_tensor("pst", [128, 1024], BF16)

    class Arena:
        def __init__(self, ap_f32_2d, nwords):
            self.t = ap_f32_2d
            self.n = nwords
            self.pos = 0

        def reset(self):
            self.pos = 0

        def f32(self, n):
            a = self.t[:, self.pos:self.pos + n]
            self.pos += n
            assert self.pos <= self.n, ("arena overflow", self.pos, self.n)
            return a

        def bf16(self, n):
            n2 = (n + 1) // 2
            return self.f32(n2).bitcast(BF16)[:, 0:n]

    arena = Arena(ar_t[:, :], ARENA_F32)

    o_ = [0]

    def cslot(n):
        a = cst_t[:, o_[0]:o_[0] + n]
        o_[0] += n
        return a

    gcol = cslot(8)
    bcol = cslot(42)
    bhalf = cslot(42)
    bqb = cslot(4)
    cfar = cslot(24)
    zcol = cslot(1)
    gsub8 = cslot(1)
    esink = cslot(8)
    lamt = cslot(8)
    neglam = cslot(1)
    ssq = cslot(2)
    rstd = cslot(2)
    nhalf = cslot(4)
    assert o_[0] <= 512
    ident = cbf_t[:, 0:128]
    ones_row = cbf_t[0:1, 128:256]
    brow = cbf_t[0:1, 256:896]

    B_cst = P.buf("cst")
    xb = P.bufs_n(2, "xb")
    xsb = P.bufs_n(2, "xs")
    xb_sem = [Sem(nc, "xbs%d" % i) for i in range(2)]
    xs_sem = [Sem(nc, "xss%d" % i) for i in range(2)]
    B_hn = P.bufs_n(4, "hn")
    B_hn2 = P.bufs_n(4, "hn2")
    hn_views = [hn_t[:, :, :], xb_t[:, :].bitcast(BF16).rearrange("p (k n) -> p k n", k=8)]
    B_hnsets = [B_hn, B_hn2]
    hn_sem = [Sem(nc, "hns0"), Sem(nc, "hns1")]
    hst_sem = Sem(nc, "hst")
    B_ssq = P.bufs_n(2, "ssq")
    B_rstd = P.bufs_n(2, "rstd")
    psb = P.bufs_n(7, "ps")
    B_pst = P.buf("pst")
    B_kt = P.buf("kt")
    B_va = P.buf("va")
    B_oat = P.buf("oat")
    setup_sem = Sem(nc, "setup", group=True)
    ld_sem = Sem(nc, "ld")
    ld_sem.group = False

    def psbank(b, n=512):
        return ps_t[:, b * 512:b * 512 + n]

    arena.reset()
    stage_small = arena.f32(1152)
    B_stage_small = P.buf("stage_small")
    lamv = arena.f32(256)
    sinkr = arena.f32(8)
    gsubr = arena.f32(1)
    identf = arena.f32(128)
    browf = arena.f32(640)

    setup_loads = []
    for (o, i) in [(gcol, gcol_d[:, :]), (bcol, bcol_d[:, :]), (bqb, bqb_d[:, :]), (cfar, cfar_d[:, :]),
                   (lamv, lamv_d[:, :]), (sinkr, sink_d[:, :]), (gsubr, gsub_d[:, :]), (identf, ident_d[:, :]),
                   (browf[0:1, :], brow_d[:, :])]:
        op = Op("sp", (I("dma_start", out=o, in_=i)), setup_sem)
        P.ops["sp"].append(op)
        P.dma_since_barrier.append(op)
        setup_loads.append(op)

    def cop(eng, fn):
        return P.op(eng, fn, reads=[], writes=[B_cst], after=setup_loads)

    cop("dve", I("memset", zcol, 0.0))
    cop("dve", I("memset", nhalf, -0.5))
    cop("dve", I("tensor_scalar", out=bhalf, in0=bcol, scalar1=0.5, scalar2=None, op0=ALU.mult))
    cop("dve", I("tensor_scalar", out=gsub8, in0=gsubr, scalar1=(1.0 - LAM_INIT), scalar2=None, op0=ALU.mult))
    cop("dve", I("tensor_copy", out=ident, in_=identf))
    cop("dve", I("memset", ones_row, 1.0))
    cop("dve", I("tensor_copy", out=brow, in_=browf[0:1, :]))
    prod = arena.f32(128)
    cop("dve", I("tensor_tensor", out=prod[:, 0:64], in0=lamv[:, 0:64], in1=lamv[:, 64:128], op=ALU.mult))
    cop("dve", I("tensor_tensor", out=prod[:, 64:128], in0=lamv[:, 128:192], in1=lamv[:, 192:256], op=ALU.mult))
    cop("dve", I("reduce_sum", out=lamt[:, 0:1], in_=prod[:, 0:64], axis=AX.X))
    cop("dve", I("reduce_sum", out=lamt[:, 1:2], in_=prod[:, 64:128], axis=AX.X))
    cop("act", I("activation", out=lamt[:, 2:4], in_=lamt[:, 0:2], func=AF.Exp))
    cop("act", I("activation", out=esink, in_=sinkr, func=AF.Exp))
    cop("dve", I("tensor_tensor", out=lamt[:, 4:5], in0=lamt[:, 3:4], in1=lamt[:, 2:3], op=ALU.subtract))
    cop("dve", I("tensor_scalar", out=neglam, in0=lamt[:, 4:5], scalar1=-LAM_INIT, scalar2=None, op0=ALU.add))

    cvt_i = [0]

    def convert(src_ap, dst_ap, n, scale_ap=None, scale_f=None, perm_qb=False):
        s = cvt_i[0] % 2
        cvt_i[0] += 1
        st = xb_t[:, s * D:s * D + n]
        ot = xs_t[:, s, 0:n]
        P.op("sp", I("dma_start", out=st, in_=src_ap), writes=[xb[s]], dsem=xb_sem[s])
        if perm_qb:
            src_v = xb_t[:, s * D:s * D + 512].rearrange("p (s j d) -> p s j d", s=2, j=4, d=64)
            dst_v = xs_t[:, s, 0:512].rearrange("p (j s d) -> p s j d", s=2, j=4, d=64)
            P.op("dve", I("tensor_scalar", out=dst_v, in0=src_v, scalar1=scale_ap, scalar2=None, op0=ALU.mult),
                 reads=[xb[s], B_cst], writes=[xsb[s]])
            st2 = xb_t[:, s * D + 512:s * D + n]
            ot2 = xs_t[:, s, 512:n]
            o2 = P.op("dve", I("tensor_scalar", out=ot2, in0=st2, scalar1=scale_ap, scalar2=None, op0=ALU.mult),
                      reads=[xb[s], B_cst], writes=[])
            P.op("pool", I("dma_start", out=dst_ap, in_=ot), reads=[xsb[s]], dsem=xs_sem[s], after=[o2])
            xsb[s].r[("dve2")] = o2
        else:
            sc = scale_ap if scale_ap is not None else scale_f
            P.op("dve", I("tensor_scalar", out=ot, in0=st, scalar1=sc, scalar2=None, op0=ALU.mult),
                 reads=[xb[s], B_cst], writes=[xsb[s]])
            P.op("pool", I("dma_start", out=dst_ap, in_=ot), reads=[xsb[s]], dsem=xs_sem[s])

    for k in range(8):
        for cb in range(6):
            c0 = cb * 1024
            n = min(1024, DIN - c0)
            convert(win_d[k * 128:(k + 1) * 128, c0:c0 + n], wscr[k, :, c0:c0 + n], n,
                    scale_ap=gcol[:, k:k + 1], perm_qb=(cb == 2))
    for k in range(4):
        convert(woa_d[k * 128:(k + 1) * 128, :], woascr[k, :, :], 1024, scale_f=0.5)
    for k in range(4):
        convert(wob_d[k * 128:(k + 1) * 128, :], wobscr[k, :, :], 1024, scale_f=0.5)
    for k in range(8):
        convert(wo_d[k * 128:(k + 1) * 128, :], woscr[k, :, :], 1024, scale_f=0.5)
    P.barrier()

    xcnt = [0]

    def prep_tile(src_ap, col):
        s = xcnt[0] % 2
        xcnt[0] += 1
        xt = xb_t[:, s * D:(s + 1) * D]
        xo = xs_t[:, s, :]
        P.op("sp", I("dma_start", out=xt, in_=src_ap), writes=[xb[s]], dsem=xb_sem[s])
        P.op("act", I("activation", out=xo, in_=xt, func=AF.Square, accum_out=ssq[:, s:s + 1]),
             reads=[xb[s]], writes=[xsb[s], B_ssq[s]])
        P.op("dve", I("tensor_scalar", out=rstd[:, s:s + 1], in0=ssq[:, s:s + 1], scalar1=1.0 / D, scalar2=EPS,
                                              op0=ALU.mult, op1=ALU.add), reads=[B_ssq[s]], writes=[B_rstd[s]])
        P.op("pool", I("tensor_tensor", out=rstd[:, s:s + 1], in0=rstd[:, s:s + 1], in1=nhalf[:, 0:1], op=ALU.pow),
             reads=[B_rstd[s], B_cst], writes=[B_rstd[s]])
        P.op("dve", I("tensor_scalar", out=xo, in0=xt, scalar1=rstd[:, s:s + 1], scalar2=None, op0=ALU.mult),
             reads=[xb[s], B_rstd[s]], writes=[xsb[s]])
        for k in range(8):
            P.op("pe", I("transpose", out=pst_t[:, k * 128:(k + 1) * 128], in_=xs_t[:, s, k * 128:(k + 1) * 128],
                                                  identity=ident),
                 reads=[xsb[s]] if k else [xsb[s], B_cst], writes=[B_pst] if k == 0 else [])
        lastT = P.ops["pe"][-1]
        B_pst.w = [lastT]
        xsb[s].r["pe"] = lastT
        P.op("dve", I("tensor_copy", out=hn_t[:, :, col * 128:(col + 1) * 128],
                                            in_=pst_t[:, :].rearrange("p (k t) -> p k t", k=8)),
             reads=[B_pst], writes=[B_hn[col]])

    def tile_src(c):
        return meta_d[:, :] if c == 0 else x_d[(c - 1) * 128:c * 128, :]

    def groups_kv():
        yield [0]
        for g in range(G):
            yield [1 + 4 * g + j for j in range(4)]

    glist = list(groups_kv())

    def hn_store(gi, nt):
        P.op("pool", I("dma_start", out=hscr[gi].rearrange("p (k n) -> p k n", k=8)[:, :, 0:nt * 128], in_=hn_t[:, :, 0:nt * 128]),
             reads=B_hn[0:nt], dsem=hst_sem)

    def hn_load(gi, nt, which):
        P.op("sp", I("dma_start", out=hn_views[which][:, :, 0:nt * 128],
                     in_=hscr[gi].rearrange("p (k n) -> p k n", k=8)[:, :, 0:nt * 128]),
             writes=B_hnsets[which], dsem=hn_sem[which])
        return hn_views[which], B_hnsets[which]

    def hn_iter(first):
        if first:
            for gi, blocks in enumerate(glist):
                for j, c in enumerate(blocks):
                    prep_tile(tile_src(c), j)
                hn_store(gi, len(blocks))
                yield blocks, hn_views[0], B_hn
        else:
            nxt = hn_load(0, len(glist[0]), 0)
            for gi, blocks in enumerate(glist):
                cur = nxt
                if gi + 1 < len(glist):
                    nxt = hn_load(gi + 1, len(glist[gi + 1]), (gi + 1) % 2)
                yield blocks, cur[0], cur[1]

    def load_w(dst_ap, src_ap, buf, sem):
        return P.op("sp", I("dma_start", out=dst_ap, in_=src_ap), writes=[buf], dsem=sem)

    wsem = [Sem(nc, "wsem%d" % i) for i in range(8)]

    for p in range(2):
        arena.reset()
        KT = kv_t[:, 0:2 * NB * 128].rearrange("p (h n) -> p h n", h=2)
        VA = kv_t[:, 2 * NB * 128:2 * NB * 128 + NB * 258].rearrange("p (c n) -> p c n", n=258)
        wk = arena.bf16(8 * 256).rearrange("p (k n) -> p k n", k=8)
        wv = arena.bf16(8 * 256).rearrange("p (k n) -> p k n", k=8)
        B_wk, B_wv = P.buf("wk"), P.buf("wv")
        load_w(wk, wscr[:, :, C_KA + p * 256:C_KA + (p + 1) * 256].rearrange("k p n -> p k n"), B_wk, wsem[0])
        load_w(wv, wscr[:, :, C_VA + p * 256:C_VA + (p + 1) * 256].rearrange("k p n -> p k n"), B_wv, wsem[1])
        B_vcol = P.buf("vcol")
        P.op("pool", I("memset", VA[:, :, 128:129], 1.0), writes=[B_vcol])
        P.op("pool", I("memset", VA[:, :, 257:258], 1.0), writes=[B_vcol])
        B_ktc = {}
        B_vac = {}
        for blocks, HN, BHN in hn_iter(p == 0):
            nt = len(blocks)
            c0 = blocks[0]
            for hh in range(2):
                bank = hh
                for k in range(8):
                    P.op("pe", I("matmul",
                        psbank(bank, nt * 128), lhsT=wk[:, k, hh * 128:(hh + 1) * 128], rhs=HN[:, k, 0:nt * 128],
                        start=(k == 0), stop=(k == 7)),
                        reads=([B_wk] + BHN[0:nt]) if k == 0 else [], writes=[psb[bank]] if k == 0 else [])
                last = P.ops["pe"][-1]
                psb[bank].w = [last]
                for b in BHN[0:nt]:
                    b.r["pe"] = last
                kb = P.buf("ktc")
                B_ktc[(blocks[0], hh)] = kb
                ch = (C_KA + p * 256) // 128 + hh
                P.op("dve", I("tensor_scalar",
                    out=KT[:, hh, c0 * 128:(c0 + nt) * 128], in0=psbank(bank, nt * 128), scalar1=bcol[:, ch:ch + 1],
                    scalar2=None, op0=ALU.add), reads=[psb[bank], B_cst], writes=[kb])
            for j, c in enumerate(blocks):
                bank = 2 + (j % 4)
                for k in range(8):
                    P.op("pe", I("matmul",
                        psbank(bank, 256), lhsT=HN[:, k, j * 128:(j + 1) * 128], rhs=wv[:, k, :],
                        start=(k == 0), stop=False),
                        reads=([B_wv, BHN[j]]) if k == 0 else [], writes=[psb[bank]] if k == 0 else [])
                P.op("pe", I("matmul",
                    psbank(bank, 256), lhsT=ones_row, rhs=brow[0:1, p * 256:(p + 1) * 256], start=False, stop=True),
                    reads=[B_cst])
                last = P.ops["pe"][-1]
                psb[bank].w = [last]
                BHN[j].r["pe"] = last
                vb = P.buf("vac")
                B_vac[c] = vb
                P.op("act", I("activation",
                    out=VA[:, c, :].rearrange("p (h n) -> p h n", h=2)[:, :, 0:128],
                    in_=psbank(bank, 256).rearrange("p (h n) -> p h n", h=2), func=AF.Copy),
                    reads=[psb[bank]], writes=[vb])
                if c == 0:
                    P.op("pool", I("memset", VA[0:128 - NMETA, 0, :], 0.0), reads=[], writes=[vb], after=list(B_vcol.w))
        P.barrier()

        arena.reset()
        wq = arena.bf16(8 * 256).rearrange("p (k n) -> p k n", k=8)
        strips = arena.bf16(2 * 1152).rearrange("p (h n) -> p h n", h=2)
        QT = arena.bf16(2 * 2 * 512).rearrange("p (s h n) -> p s h n", s=2, h=2)
        ET = arena.bf16(3 * 1024).rearrange("p (s n) -> p s n", s=3)
        accs = arena.f32(8 * 129).rearrange("p (a n) -> p a n", a=8)
        ofin = arena.f32(4 * 128).rearrange("p (j n) -> p j n", j=4)
        t1 = arena.f32(128)
        sq = arena.f32(128)
        onb = arena.bf16(4 * 128).rearrange("p (j n) -> p j n", j=4)
        sm = arena.f32(32)
        sstage = arena.f32(1152)
        B_wq, B_strips = P.buf("wq"), P.buf("strips")
        B_QT = P.bufs_n(2, "QT")
        B_ET = P.bufs_n(3, "ET")
        B_accs, B_fin, B_on, B_sstage = P.buf("accs"), P.buf("fin"), P.buf("on"), P.buf("sstage")
        load_w(wq, wscr[:, :, C_QA + p * 256:C_QA + (p + 1) * 256].rearrange("k p n -> p k n"), B_wq, wsem[0])
        for hh in range(2):
            P.op("sp", I("dma_start", out=sstage, in_=strips_d[2 * p + hh, :, :]), writes=[B_sstage], dsem=wsem[2])
            P.op("dve", I("tensor_copy", out=strips[:, hh, :], in_=sstage), reads=[B_sstage], writes=[B_strips])
        ucnt = [0]
        ecnt = [0]
        pstf = pst_t[:, :].bitcast(F32)
        deferred = []

        def defer(n, tag, fn):
            deferred.append([n, tag, fn])

        def tick():
            for d in deferred:
                d[0] -= 1
            due = [d for d in deferred if d[0] <= 0]
            for d in due:
                deferred.remove(d)
                d[2]()

        def flush_deferred(tag=None):
            due = [d for d in deferred if tag is None or d[1] == tag]
            for d in due:
                deferred.remove(d)
                d[2]()

        def qproj_mm(hh, k0, k1, HNq, BHNq):
            for k in range(k0, k1):
                P.op("pe", I("matmul", pstf, lhsT=wq[:, k, hh * 128:(hh + 1) * 128], rhs=HNq[:, k, :],
                             start=(k == 0), stop=(k == 7)),
                     reads=([B_wq] + BHNq) if k == 0 else [], writes=[B_pst] if k == 0 else [])
            if k1 == 8:
                last = P.ops["pe"][-1]
                B_pst.w = [last]
                for b in BHNq:
                    b.r["pe"] = last

        def qproj_evac(hh, qsl):
            ch = (C_QA + p * 256) // 128 + hh
            P.op("dve", I("tensor_scalar", out=QT[:, qsl, hh, :], in0=pstf, scalar1=bcol[:, ch:ch + 1], scalar2=0.125,
                          op0=ALU.add, op1=ALU.mult), reads=[B_pst, B_cst], writes=[B_QT[qsl]] if hh == 0 else [])
            if hh == 1:
                B_QT[qsl].w.append(P.ops["dve"][-1])

        def qproj_schedule(gq, HNq, BHNq, now):
            qsl = gq % 2
            t = 8
            for hh in range(2):
                for k0 in (0, 2, 4, 6):
                    fn = (lambda hh=hh, k0=k0: qproj_mm(hh, k0, k0 + 2, HNq, BHNq))
                    if now:
                        fn()
                    else:
                        defer(t, "q", fn)
                    t += 1
                fn = (lambda hh=hh: qproj_evac(hh, qsl))
                if now:
                    fn()
                else:
                    defer(t, "q", fn)
                t += 2

        def B_ktc_get(c, hh):
            for (c0, h2), b in B_ktc.items():
                if h2 == hh and (c0 == c or (c0 >= 1 and c0 <= c < c0 + 4 and c >= 1)):
                    return b
            raise KeyError((c, hh))

        def emit_scores(c, g, hh, h, qs, b0):
            slot = ucnt[0] % 2
            ucnt[0] += 1
            bk = 2 * slot
            Dd = c - b0
            mixed = (-1 <= Dd <= 4)
            P.op("pe", I("matmul", psbank(bk), lhsT=KT[0:64, hh, c * 128:(c + 1) * 128],
                         rhs=QT[0:64, qs, hh, :], start=True, stop=not mixed),
                 reads=[B_ktc_get(c, hh), B_QT[qs]], writes=[psb[bk], psb[bk + 1]])
            P.op("pe", I("matmul", psbank(bk + 1), lhsT=KT[64:128, hh, c * 128:(c + 1) * 128],
                         rhs=QT[64:128, qs, hh, :], start=True, stop=not mixed))
            if mixed:
                o0 = (4 - Dd) * 128
                P.op("pe", I("matmul", psbank(bk), lhsT=ident, rhs=strips[:, hh, o0:o0 + 512],
                             start=False, stop=True), reads=[B_strips, B_cst])
                P.op("pe", I("matmul", psbank(bk + 1), lhsT=ident, rhs=strips[:, hh, o0:o0 + 512],
                             start=False, stop=True))
            last = P.ops["pe"][-1]
            psb[bk].w = [last]
            psb[bk + 1].w = [last]
            B_QT[qs].r["pe"] = last
            es = ecnt[0] % 3
            ecnt[0] += 1
            if mixed:
                bias_ap = zcol
            elif Dd <= -2:
                bias_ap = cfar[:, h:h + 1]
            else:
                bias_ap = cfar[:, 12 + h:13 + h]
            P.op("act", I("activation", out=ET[:, es, :], in_=ps_t[:, bk * 512:bk * 512 + 1024], func=AF.Exp,
                          bias=bias_ap, scale=1.0),
                 reads=[psb[bk], psb[bk + 1], B_cst], writes=[B_ET[es]])
            return (c, es)

        def emit_av(c, es, hh):
            first = (c == 0)
            lastc = (c == NB - 1)
            for j in range(4):
                for m in range(2):
                    a = 2 * j + m
                    bank = 4 + a // 3
                    off = (a % 3) * 129
                    P.op("pe", I("matmul", ps_t[:, bank * 512 + off:bank * 512 + off + 129],
                                 lhsT=ET[:, es, m * 512 + j * 128:m * 512 + (j + 1) * 128],
                                 rhs=VA[:, c, hh * 129:hh * 129 + 129], start=(first and a % 3 == 0), stop=lastc,
                                 skip_group_check=True),
                         reads=([B_ET[es], B_vac[c]]) if a == 0 else [],
                         writes=[psb[4], psb[5], psb[6]] if (first and a == 0) else [])
            last = P.ops["pe"][-1]
            B_ET[es].r["pe"] = last
            if lastc:
                for b in (psb[4], psb[5], psb[6]):
                    b.w = [last]

        def finalize_part1():
            for bi, na in ((4, 3), (5, 3), (6, 2)):
                a0 = (bi - 4) * 3
                P.op("dve", I("tensor_copy", out=accs[:, a0:a0 + na, :],
                              in_=psbank(bi, na * 129).rearrange("p (a n) -> p a n", a=na)),
                     reads=[psb[bi]], writes=[B_accs] if bi == 4 else [])
                if bi != 4:
                    B_accs.w.append(P.ops["dve"][-1])

            def fin(fn):
                return P.op("dve", fn, reads=[B_accs, B_fin, B_cst], writes=[B_fin])

            fin(I("reciprocal", out=sm[:, 0:8], in_=accs[:, :, 128:129].rearrange("p a o -> p (a o)")))
            fin(I("tensor_scalar", out=sm[:, 8:12], in0=sm[:, 0:8].rearrange("p (j m) -> p j m", m=2)[:, :, 1],
                  scalar1=neglam, scalar2=None, op0=ALU.mult))
            for j in range(4):
                fin(I("tensor_scalar", out=t1, in0=accs[:, 2 * j + 1, 0:128], scalar1=sm[:, 8 + j:9 + j],
                      scalar2=None, op0=ALU.mult))
                fin(I("scalar_tensor_tensor", out=ofin[:, j, :], in0=accs[:, 2 * j, 0:128],
                      scalar=sm[:, 2 * j:2 * j + 1], in1=t1, op0=ALU.mult, op1=ALU.add))
                fin(I("tensor_tensor", out=sq, in0=ofin[:, j, :], in1=ofin[:, j, :], op=ALU.mult))
                fin(I("reduce_sum", out=sm[:, 12 + j:13 + j], in_=sq, axis=AX.X))
            fin(I("tensor_scalar", out=sm[:, 16:20], in0=sm[:, 12:16], scalar1=1.0 / 128, scalar2=EPS,
                  op0=ALU.mult, op1=ALU.add))
            P.op("pool", I("tensor_tensor", out=sm[:, 16:20], in0=sm[:, 16:20], in1=nhalf[:, 0:4], op=ALU.pow),
                 reads=[B_fin, B_cst], writes=[B_fin])
            for j in range(4):
                P.op("dve", I("tensor_scalar", out=onb[:, j, :], in0=ofin[:, j, :], scalar1=sm[:, 16 + j:17 + j],
                              scalar2=None, op0=ALU.mult),
                     reads=[B_fin], writes=[B_on] if j == 0 else [])
                if j:
                    B_on.w.append(P.ops["dve"][-1])

        def finalize_part2(h, g):
            for j in range(4):
                P.op("pe", I("transpose", out=pst_t[:, j * 128:(j + 1) * 128], in_=onb[:, j, :], identity=ident),
                     reads=[B_on, B_cst] if j == 0 else [], writes=[B_pst] if j == 0 else [])
            lastT = P.ops["pe"][-1]
            B_pst.w = [lastT]
            B_on.r["pe"] = lastT
            P.op("dve", I("tensor_scalar", out=oat_t[:, h, g * 512:(g + 1) * 512], in0=pst_t[:, 0:512],
                          scalar1=gsub8, scalar2=None, op0=ALU.mult),
                 reads=[B_pst, B_cst], writes=[])

        nxt = hn_load(1, 4, 0)
        qproj_schedule(0, nxt[0], nxt[1], now=True)
        for g in range(G):
            qs = g % 2
            flush_deferred("q")
            if g + 1 < G:
                nxt = hn_load(2 + g, 4, (g + 1) % 2)
            b0 = 1 + 4 * g
            for hh in range(2):
                h = 2 * p + hh
                if hh == 1 and g + 1 < G:
                    qproj_schedule(g + 1, nxt[0], nxt[1], now=False)
                prev = emit_scores(0, g, hh, h, qs, b0)
                for c in range(1, NB):
                    cur = emit_scores(c, g, hh, h, qs, b0)
                    emit_av(prev[0], prev[1], hh)
                    tick()
                    prev = cur
                emit_av(prev[0], prev[1], hh)
                finalize_part1()
                defer(6, "f", (lambda h=h, g=g: finalize_part2(h, g)))
        flush_deferred()
        P.barrier()

    arena.reset()
    KTB = kv_t[:, 0:NB * 128]
    VB = kv_t[:, NB * 128:NB * 128 + NB * 130].rearrange("p (c n) -> p c n", n=130)
    kvfree0 = NB * 128 + NB * 130
    kvfree0 += kvfree0 % 2
    wkb = arena.bf16(8 * 128).rearrange("p (k n) -> p k n", k=8)
    wvb = arena.bf16(8 * 128).rearrange("p (k n) -> p k n", k=8)
    B_wk, B_wv = P.buf("wkb"), P.buf("wvb")
    load_w(wkb, wscr[:, :, C_KB:C_KB + 128].rearrange("k p n -> p k n"), B_wk, wsem[0])
    load_w(wvb, wscr[:, :, C_VB:C_VB + 128].rearrange("k p n -> p k n"), B_wv, wsem[1])
    B_vcol = P.buf("vcolb")
    P.op("pool", I("memset", VB[:, :, 64:65], 1.0), writes=[B_vcol])
    P.op("pool", I("memset", VB[:, :, 129:130], 1.0), writes=[B_vcol])
    for blocks, HN, BHN in hn_iter(False):
        nt = len(blocks)
        c0 = blocks[0]
        for k in range(8):
            P.op("pe", I("matmul", psbank(0, nt * 128), lhsT=wkb[:, k, :], rhs=HN[:, k, 0:nt * 128],
                                               start=(k == 0), stop=(k == 7)),
                 reads=([B_wk] + BHN[0:nt]) if k == 0 else [], writes=[psb[0]] if k == 0 else [])
        last = P.ops["pe"][-1]
        psb[0].w = [last]
        for b in BHN[0:nt]:
            b.r["pe"] = last
        P.op("dve", I("tensor_scalar",
            out=KTB[:, c0 * 128:(c0 + nt) * 128], in0=psbank(0, nt * 128), scalar1=bcol[:, C_KB // 128:C_KB // 128 + 1],
            scalar2=None, op0=ALU.add), reads=[psb[0], B_cst], writes=[])
        psb[0].r["dve"] = P.ops["dve"][-1]
        for j, c in enumerate(blocks):
            bank = 2 + (j % 4)
            for k in range(8):
                P.op("pe", I("matmul",
                    psbank(bank, 128), lhsT=HN[:, k, j * 128:(j + 1) * 128], rhs=wvb[:, k, :], start=(k == 0), stop=False),
                    reads=([B_wv, BHN[j]]) if k == 0 else [], writes=[psb[bank]] if k == 0 else [])
            P.op("pe", I("matmul", psbank(bank, 128), lhsT=ones_row, rhs=brow[0:1, 512:640],
                                                     start=False, stop=True), reads=[B_cst])
            last = P.ops["pe"][-1]
            psb[bank].w = [last]
            BHN[j].r["pe"] = last
            P.op("act", I("activation",
                out=VB[:, c, :].rearrange("p (h n) -> p h n", h=2)[:, :, 0:64],
                in_=psbank(bank, 128).rearrange("p (h n) -> p h n", h=2), func=AF.Copy),
                reads=[psb[bank]], writes=[])
            psb[bank].r["act"] = P.ops["act"][-1]
            if c == 0:
                P.op("pool", I("memset", VB[0:128 - NMETA, 0, :], 0.0), after=list(B_vcol.w) + [P.ops["act"][-1]])
    P.barrier()

    arena.reset()
    ar2 = Arena(kv_t[:, kvfree0:KVW].bitcast(F32), (KVW - kvfree0) // 2)
    biasB = arena.bf16(8 * 384).rearrange("p (h n) -> p h n", h=8)
    Woa = arena.bf16(4 * 1024).rearrange("p (k n) -> p k n", k=4)
    Wob = arena.bf16(4 * 1024).rearrange("p (k n) -> p k n", k=4)
    Wring = [arena.bf16(4096), arena.bf16(4096)]
    postg = arena.f32(1024)
    ttile = arena.f32(512)
    zf = arena.f32(512)
    sz = arena.f32(512)
    mixedT = ar2.bf16(8 * 512).rearrange("p (k n) -> p k n", k=8)
    goaT = ar2.bf16(4 * 512).rearrange("p (k n) -> p k n", k=4)
    gobT = ar2.bf16(4 * 512).rearrange("p (k n) -> p k n", k=4)
    QTb = ar2.bf16(4 * 512).rearrange("p (k n) -> p k n", k=4)
    obt = ar2.bf16(4 * 512).rearrange("p (t n) -> p t n", t=4)
    EB = [ar2.bf16(384), ar2.bf16(384)]
    outt = ar2.f32(1024)
    ta = arena.f32(512)
    tb = ar2.f32(512)
    smf = ar2.f32(16)
    bstage = zf[:, 0:384]
    maskb = sz[:, 0:384]
    B_biasB, B_Woa, B_Wob, B_postg = P.buf("biasB"), P.buf("Woa"), P.buf("Wob"), P.buf("postg")
    B_ring = P.bufs_n(2, "ring")
    ring_sem = [wsem[3], wsem[4]]
    B_t, B_zf, B_sz, B_mixed, B_goa, B_gob, B_QTb = (P.buf("t"), P.buf("zf"), P.buf("sz"), P.buf("mixed"), P.buf("goa"),
                                                     P.buf("gob"), P.buf("QTb"))
    B_ob = P.bufs_n(4, "ob")
    B_EB = P.bufs_n(2, "EB")
    B_out, B_ta, B_tb, B_smf, B_bstage, B_maskb = P.buf("out"), P.buf("ta"), P.buf("tb"), P.buf("smf"), P.buf("bstage"), P.buf("maskb")
    out_sem = wsem[5]
    load_w(Woa, woascr.rearrange("k p n -> p k n"), B_Woa, wsem[0])
    load_w(Wob, wobscr.rearrange("k p n -> p k n"), B_Wob, wsem[1])
    load_w(postg, postg_d[:, :], B_postg, wsem[2])
    load_w(maskb, maskb_d[:, :], B_maskb, wsem[6])
    for h8 in range(8):
        P.op("sp", I("dma_start", out=bstage, in_=strips_d[4 + h8, :, 384:768]), writes=[B_bstage], dsem=wsem[7])
        P.op("dve", I("tensor_tensor", out=biasB[:, h8, :], in0=bstage, in1=maskb, op=ALU.add),
             reads=[B_bstage, B_maskb], writes=[B_biasB])
    rcnt = [0]

    def ring_load(src_ap):
        s = rcnt[0] % 2
        rcnt[0] += 1
        load_w(Wring[s].rearrange("p (k n) -> p k n", k=8), src_ap, B_ring[s], ring_sem[s])
        return s

    def proj_fm(wblk, s, col0, bank):
        for k in range(8):
            P.op("pe", I("matmul", psbank(bank), lhsT=wblk[:, k, col0:col0 + 128], rhs=hn_t[:, k, :],
                                               start=(k == 0), stop=(k == 7)),
                 reads=([B_ring[s]] + B_hn) if k == 0 else [], writes=[psb[bank]] if k == 0 else [])
        last = P.ops["pe"][-1]
        psb[bank].w = [last]
        B_ring[s].r["pe"] = last
        for b in B_hn:
            b.r["pe"] = last

    ebc = [0]
    for g in range(G):
        hn_load(1 + g, 4, 0)
        s = ring_load(wscr[:, :, C_QB:C_QB + 512].rearrange("k p n -> p k n"))
        wblk = Wring[s].rearrange("p (k n) -> p k n", k=8)
        for j in range(4):
            bank = j % 2
            proj_fm(wblk, s, j * 128, bank)
            P.op("dve", I("tensor_scalar", out=QTb[:, j, :], in0=psbank(bank), scalar1=bqb[:, j:j + 1],
                                                                 scalar2=0.125, op0=ALU.add, op1=ALU.mult),
                 reads=[psb[bank], B_cst], writes=[B_QTb] if j == 0 else [])
            if j:
                B_QTb.w.append(P.ops["dve"][-1])
                psb[bank].r["dve"] = P.ops["dve"][-1]
        for tq in range(4):
            b = 1 + 4 * g + tq
            cs = [c for c in (b - 1, b, b + 1) if 0 <= c < NB]
            first_ob = True
            for h8 in range(8):
                jq, half = h8 % 4, h8 // 4
                r0 = half * 64
                sbank = 2 + (ebc[0] % 2)
                es = ebc[0] % 2
                ebc[0] += 1
                n0 = (cs[0] - (b - 1)) * 128
                for i, c in enumerate(cs):
                    jj = c - (b - 1)
                    P.op("pe", I("matmul",
                        ps_t[:, sbank * 512 + jj * 128:sbank * 512 + (jj + 1) * 128],
                        lhsT=KTB[r0:r0 + 64, c * 128:(c + 1) * 128], rhs=QTb[r0:r0 + 64, jq, tq * 128:(tq + 1) * 128],
                        start=True, stop=False), reads=[B_QTb] if i == 0 else [], writes=[psb[sbank]] if i == 0 else [])
                    P.op("pe", I("matmul",
                        ps_t[:, sbank * 512 + jj * 128:sbank * 512 + (jj + 1) * 128],
                        lhsT=ident, rhs=biasB[:, h8, (2 - jj) * 128:(3 - jj) * 128], start=False, stop=True),
                        reads=[B_biasB, B_cst] if i == 0 else [])
                last = P.ops["pe"][-1]
                psb[sbank].w = [last]
                B_QTb.r["pe"] = last
                ncol = len(cs) * 128
                P.op("act", I("activation", out=EB[es][:, n0:n0 + ncol],
                                                                   in_=ps_t[:, sbank * 512 + n0:sbank * 512 + n0 + ncol], func=AF.Exp),
                     reads=[psb[sbank]], writes=[B_EB[es]])
                abank = 4 + (h8 % 2)
                for i, c in enumerate(cs):
                    jj = c - (b - 1)
                    P.op("pe", I("matmul",
                        ps_t[:, abank * 512:abank * 512 + 65], lhsT=EB[es][:, jj * 128:(jj + 1) * 128],
                        rhs=VB[:, c, half * 65:half * 65 + 65], start=(i == 0), stop=(i == len(cs) - 1)),
                        reads=[B_EB[es]] if i == 0 else [], writes=[psb[abank]] if i == 0 else [])
                last = P.ops["pe"][-1]
                psb[abank].w = [last]
                B_EB[es].r["pe"] = last
                P.op("dve", I("tensor_scalar", out=smf[:, h8:h8 + 1], in0=ps_t[:, abank * 512 + 64:abank * 512 + 65],
                                                      scalar1=esink[:, h8:h8 + 1], scalar2=None, op0=ALU.add),
                     reads=[psb[abank], B_cst], writes=[B_smf])
                P.op("dve", I("reciprocal", out=smf[:, 8 + h8:9 + h8], in_=smf[:, h8:h8 + 1]), reads=[B_smf], writes=[B_smf])
                P.op("dve", I("tensor_scalar", out=obt[:, tq, h8 * 64:(h8 + 1) * 64], in0=ps_t[:, abank * 512:abank * 512 + 64],
                                                      scalar1=smf[:, 8 + h8:9 + h8], scalar2=None, op0=ALU.mult),
                     reads=[psb[abank], B_smf], writes=[B_ob[tq]] if first_ob else [])
                if not first_ob:
                    B_ob[tq].w.append(P.ops["dve"][-1])
                    psb[abank].r["dve"] = P.ops["dve"][-1]
                first_ob = False

        def gate(col_base, which):
            s = ring_load(wscr[:, :, col_base:col_base + 512].rearrange("k p n -> p k n"))
            wblk = Wring[s].rearrange("p (k n) -> p k n", k=8)
            for c in range(4):
                bank = c % 2
                ch = col_base // 128 + c
                proj_fm(wblk, s, c * 128, bank)
                P.op("act", I("activation", out=ttile, in_=psbank(bank), func=AF.Tanh,
                                                                   bias=bhalf[:, ch:ch + 1], scale=0.5),
                     reads=[psb[bank], B_cst], writes=[B_t])
                P.op("act", I("activation", out=zf, in_=psbank(bank), func=AF.Identity,
                                                                   bias=bcol[:, ch:ch + 1], scale=1.0),
                     reads=[psb[bank], B_cst], writes=[B_zf])
                P.op("dve", I("scalar_tensor_tensor", out=sz, in0=ttile, scalar=1.0, in1=zf, op0=ALU.add, op1=ALU.mult),
                     reads=[B_t, B_zf], writes=[B_sz])
                if which == "a":
                    P.op("dve", I("tensor_tensor", out=goaT[:, c, :], in0=oat_t[:, c, g * 512:(g + 1) * 512], in1=sz,
                                                               op=ALU.mult), reads=[B_sz], writes=[B_goa] if c == 0 else [])
                    if c:
                        B_goa.w.append(P.ops["dve"][-1])
                        B_sz.r["dve"] = P.ops["dve"][-1]
                else:
                    for tq in range(4):
                        P.op("pe", I("transpose", out=pst_t[:, tq * 128:(tq + 1) * 128],
                                                                    in_=obt[:, tq, c * 128:(c + 1) * 128], identity=ident),
                             reads=[B_ob[tq], B_cst], writes=[B_pst] if tq == 0 else [])
                    lastT = P.ops["pe"][-1]
                    B_pst.w = [lastT]
                    for tq in range(4):
                        B_ob[tq].r["pe"] = lastT
                    P.op("dve", I("tensor_tensor", out=gobT[:, c, :], in0=pst_t[:, 0:512], in1=sz, op=ALU.mult),
                         reads=[B_pst, B_sz], writes=[B_gob] if c == 0 else [])
                    if c:
                        B_gob.w.append(P.ops["dve"][-1])
                        B_sz.r["dve"] = P.ops["dve"][-1]
                        B_pst.r["dve"] = P.ops["dve"][-1]

        gate(C_ZB, "b")
        gate(C_ZA, "a")
        for half in range(2):
            sa = ring_load(wscr[:, :, C_GA + half * 512:C_GA + (half + 1) * 512].rearrange("k p n -> p k n"))
            wa = Wring[sa].rearrange("p (k n) -> p k n", k=8)
            sb = None
            for c4 in range(4):
                fc = half * 4 + c4
                proj_fm(wa, sa, c4 * 128, 0)
                cha = C_GA // 128 + fc
                P.op("act", I("activation", out=ta, in_=psbank(0), func=AF.Tanh, bias=bhalf[:, cha:cha + 1], scale=0.5),
                     reads=[psb[0], B_cst], writes=[B_ta])
                if c4 == 0:
                    sb = ring_load(wscr[:, :, C_GB + half * 512:C_GB + (half + 1) * 512].rearrange("k p n -> p k n"))
                    wb = Wring[sb].rearrange("p (k n) -> p k n", k=8)
                proj_fm(wb, sb, c4 * 128, 1)
                chb = C_GB // 128 + fc
                P.op("act", I("activation", out=tb, in_=psbank(1), func=AF.Tanh, bias=bhalf[:, chb:chb + 1], scale=0.5),
                     reads=[psb[1], B_cst], writes=[B_tb])
                for k in range(4):
                    P.op("pe", I("matmul", psbank(2), lhsT=Woa[:, k, fc * 128:(fc + 1) * 128], rhs=goaT[:, k, :],
                                                              start=(k == 0), stop=(k == 3)),
                         reads=[B_Woa, B_goa] if k == 0 else [], writes=[psb[2]] if k == 0 else [])
                last = P.ops["pe"][-1]
                psb[2].w = [last]
                B_goa.r["pe"] = last
                for k in range(4):
                    P.op("pe", I("matmul", psbank(3), lhsT=Wob[:, k, fc * 128:(fc + 1) * 128], rhs=gobT[:, k, :],
                                                              start=(k == 0), stop=(k == 3)),
                         reads=[B_Wob, B_gob] if k == 0 else [], writes=[psb[3]] if k == 0 else [])
                last = P.ops["pe"][-1]
                psb[3].w = [last]
                B_gob.r["pe"] = last
                P.op("dve", I("scalar_tensor_tensor", out=ta, in0=ta, scalar=1.0, in1=psbank(2), op0=ALU.add, op1=ALU.mult),
                     reads=[psb[2]], writes=[B_ta])
                P.op("dve", I("scalar_tensor_tensor", out=tb, in0=tb, scalar=1.0, in1=psbank(3), op0=ALU.add, op1=ALU.mult),
                     reads=[psb[3]], writes=[B_tb])
                P.op("dve", I("tensor_tensor", out=mixedT[:, fc, :], in0=ta, in1=tb, op=ALU.add),
                     reads=[B_ta, B_tb], writes=[B_mixed] if fc == 0 else [])
                if fc:
                    B_mixed.w.append(P.ops["dve"][-1])
                    B_ta.r["dve"] = P.ops["dve"][-1]
                    B_tb.r["dve"] = P.ops["dve"][-1]
        so = [ring_load(woscr[:, :, hf * 512:(hf + 1) * 512].rearrange("k p n -> p k n")) for hf in range(2)]
        for tq in range(4):
            for hf in range(2):
                wblk = Wring[so[hf]].rearrange("p (k n) -> p k n", k=8)
                bank = 4 + hf
                for k in range(8):
                    P.op("pe", I("matmul",
                        psbank(bank), lhsT=mixedT[:, k, tq * 128:(tq + 1) * 128], rhs=wblk[:, k, :], start=(k == 0), stop=(k == 7)),
                        reads=[B_ring[so[hf]], B_mixed] if k == 0 else [], writes=[psb[bank]] if k == 0 else [])
                last = P.ops["pe"][-1]
                psb[bank].w = [last]
                B_ring[so[hf]].r["pe"] = last
                B_mixed.r["pe"] = last
            P.op("act", I("activation", out=outt[:, 0:512], in_=psbank(4), func=AF.Square, accum_out=smf[:, 0:1]),
                 reads=[psb[4]], writes=[B_out, B_smf])
            P.op("act", I("activation", out=outt[:, 512:1024], in_=psbank(5), func=AF.Square, accum_out=smf[:, 1:2]),
                 reads=[psb[5]], writes=[B_out, B_smf])
            P.op("dve", I("tensor_tensor", out=smf[:, 2:3], in0=smf[:, 0:1], in1=smf[:, 1:2], op=ALU.add), reads=[B_smf], writes=[B_smf])
            P.op("dve", I("tensor_scalar", out=smf[:, 2:3], in0=smf[:, 2:3], scalar1=1.0 / D, scalar2=EPS, op0=ALU.mult, op1=ALU.add),
                 reads=[B_smf], writes=[B_smf])
            P.op("pool", I("tensor_tensor", out=smf[:, 3:4], in0=smf[:, 2:3], in1=nhalf[:, 0:1], op=ALU.pow),
                 reads=[B_smf], writes=[B_smf])
            s = xcnt[0] % 2
            xcnt[0] += 1
            xsrc = x_d[(4 * g + tq) * 128:(4 * g + tq + 1) * 128, :]
            P.op("sp", I("dma_start", out=xb_t[:, s * D:(s + 1) * D], in_=xsrc), writes=[xb[s]], dsem=xb_sem[s])
            for hf in range(2):
                P.op("dve", I("scalar_tensor_tensor", out=outt[:, hf * 512:(hf + 1) * 512], in0=psbank(4 + hf),
                                                                    scalar=smf[:, 3:4], in1=postg[:, hf * 512:(hf + 1) * 512],
                                                                    op0=ALU.mult, op1=ALU.mult),
                     reads=[psb[4 + hf], B_smf, B_postg], writes=[B_out])
            P.op("dve", I("tensor_tensor", out=outt, in0=outt, in1=xb_t[:, s * D:(s + 1) * D], op=ALU.add),
                 reads=[xb[s]], writes=[B_out])
            ysl = y_d[(4 * g + tq) * 128:(4 * g + tq + 1) * 128, :]
            P.op("pool", I("dma_start", out=ysl, in_=outt), reads=[B_out], dsem=out_sem)
    P.barrier()

    with nc.Block() as block:
        P.flush(block)
    nc._marks = P.marks
    return nc


_NC_CACHE = {}


def host_consts(rel_bias, pre_norm_g, b_in, lambda_q1, lambda_k1, lambda_q2, lambda_k2, subln_g, sink, post_norm_g):
    f = np.float32
    rel_bias = np.asarray(rel_bias, f)
    b = np.asarray(b_in, f).reshape(-1)
    kk = np.arange(128)[:, None]
    v = np.arange(1152)[None, :]
    bucket = t5_bucket_np((512 + kk - v).astype(np.int32))
    strips = np.ascontiguousarray(np.transpose(rel_bias[bucket], (2, 0, 1)))
    cfar = np.ascontiguousarray(np.broadcast_to(np.concatenate([rel_bias[15], rel_bias[31]])[None, :], (128, 24)))
    x384 = np.arange(384)[None, :]
    rel = 128 + kk - x384
    maskb = np.where(np.abs(rel) <= 128, 0.0, -30000.0).astype(f)
    gcol = np.ascontiguousarray(np.asarray(pre_norm_g, f).reshape(8, 128).T)
    bcol = np.ascontiguousarray(b.reshape(42, 128).T)
    bq = b[C_QB:C_QB + 512].reshape(2, 4, 64)
    bqb = np.ascontiguousarray(np.transpose(bq, (1, 0, 2)).reshape(4, 128).T)
    brow = np.ascontiguousarray(np.concatenate([b[C_VA:C_VA + 512], b[C_VB:C_VB + 128]])[None, :])
    lamv = np.concatenate([np.asarray(a, f).reshape(-1) for a in (lambda_q1, lambda_k1, lambda_q2, lambda_k2)])
    lamv = np.ascontiguousarray(np.broadcast_to(lamv[None, :], (128, 256)))
    gsub = np.ascontiguousarray(np.asarray(subln_g, f).reshape(128, 1))
    sinkr = np.ascontiguousarray(np.broadcast_to(np.asarray(sink, f).reshape(1, 8), (128, 8)))
    postg = np.ascontiguousarray(np.broadcast_to(np.asarray(post_norm_g, f).reshape(1, D), (128, D)))
    ident = np.eye(128, dtype=f)
    return dict(strips=strips, cfar=cfar, maskb=maskb, gcol=gcol, bcol=bcol, bqb=bqb, brow=brow, lamv=lamv, gsub=gsub,
                sinkr=sinkr, postg=postg, ident=ident)


def kernel(x, meta_tokens, rel_bias, pre_norm_g, w_in, b_in, lambda_q1, lambda_k1, lambda_q2, lambda_k2,
           subln_g, sink, w_out_a, w_out_b, w_out, post_norm_g):
    x = np.asarray(x, np.float32)
    B, S, _ = x.shape
    NT = S // 128
    if NT not in _NC_CACHE:
        _NC_CACHE[NT] = build(NT)
    nc = _NC_CACHE[NT]
    consts = host_consts(rel_bias, pre_norm_g, b_in, lambda_q1, lambda_k1, lambda_q2, lambda_k2, subln_g, sink, post_norm_g)
    meta_pad = np.zeros((128, D), np.float32)
    meta_pad[128 - NMETA:] = np.asarray(meta_tokens, np.float32)
    shared = dict(meta_pad=meta_pad, w_in=np.ascontiguousarray(np.asarray(w_in, np.float32)[0]),
                  w_out_a=np.ascontiguousarray(np.asarray(w_out_a, np.float32)[0]),
                  w_out_b=np.ascontiguousarray(np.asarray(w_out_b, np.float32)[0]),
                  w_out=np.ascontiguousarray(np.asarray(w_out, np.float32)[0]), **consts)
    in_maps = [dict(x=np.ascontiguousarray(x[i]), **shared) for i in range(B)]
    res = run_bass_kernel_spmd(nc, in_maps, core_ids=list(range(B)))
    return np.stack([np.asarray(r["y"], np.float32) for r in res.results], axis=0)
```

```python
import math
import numpy as np
import concourse.bass as bass
import concourse.mybir as mybir
from concourse.bass_utils import run_bass_kernel_spmd

F32 = mybir.dt.float32
BF16 = mybir.dt.bfloat16
AF = mybir.ActivationFunctionType
ALU = mybir.AluOpType
AX = mybir.AxisListType

D = 1024
DIN = 5376
EPS = 1e-6
NMETA = 16
ENGS = ("sp", "act", "dve", "pool", "pe")
C_QA, C_KA, C_VA, C_ZA, C_QB, C_KB, C_VB, C_ZB, C_GA, C_GB = 0, 512, 1024, 1536, 2048, 2560, 2688, 2816, 3328, 4352
LAM_INIT = 0.8 - 0.6 * math.exp(-0.3 * 0)


class Sem:
    def __init__(self, nc, name, group=False):
        self.h = nc.alloc_semaphore(name)
        self.name = name
        self.v = 0
        self.group = group


class Op:
    __slots__ = ("eng", "fn", "deps", "dsem", "sem", "val", "need", "idx")

    def __init__(self, eng, fn, dsem):
        self.eng, self.fn, self.dsem = eng, fn, dsem
        self.deps = []
        self.sem = None
        self.val = 0
        self.need = False


def I(method, *args, **kw):
    return (method, args, kw)


class Buf:
    __slots__ = ("w", "r", "name")

    def __init__(self, name=""):
        self.name = name
        self.w = []
        self.r = {}


class Prog:
    def __init__(self, nc):
        self.nc = nc
        self.ops = {e: [] for e in ENGS}
        self.esem = {e: Sem(nc, "es_" + e) for e in ENGS}
        self.bufs = []
        self.dma_since_barrier = []
        self.marks = []

    def buf(self, name=""):
        b = Buf(name)
        self.bufs.append(b)
        return b

    def bufs_n(self, n, name=""):
        return [self.buf(name + str(i)) for i in range(n)]

    def op(self, eng, fn, reads=(), writes=(), dsem=None, after=()):
        o = Op(eng, fn, dsem)
        deps = {}
        for b in reads:
            for d in b.w:
                deps[id(d)] = d
        for b in writes:
            for d in b.w:
                deps[id(d)] = d
            for d in b.r.values():
                deps[id(d)] = d
        for d in after:
            deps[id(d)] = d
        for d in deps.values():
            if d.eng == "pe" and eng == "pe" and d.dsem is None:
                continue
            if d is o:
                continue
            o.deps.append(d)
            d.need = True
        for b in writes:
            b.w = [o]
            b.r = {}
        for b in reads:
            if b in writes:
                continue
            key = eng if dsem is None else ("dma", id(dsem))
            b.r[key] = o
        self.ops[eng].append(o)
        if dsem is not None:
            self.dma_since_barrier.append(o)
        return o

    def barrier(self):
        self.marks.append(sum(1 for o in self.ops["pe"] if o.fn is not None))
        lasts = []
        for e in ENGS:
            for o in reversed(self.ops[e]):
                if o.fn is not None and o.dsem is None:
                    lasts.append(o)
                    break
        deps = lasts + self.dma_since_barrier
        self.dma_since_barrier = []
        for e in ENGS:
            self.op(e, None, after=deps)
        for b in self.bufs:
            b.w = []
            b.r = {}

    def flush(self, block):
        for e in ENGS:
            for o in self.ops[e]:
                if o.fn is None:
                    continue
                if o.dsem is not None:
                    o.dsem.v += 16
                    o.sem, o.val = o.dsem, o.dsem.v
                elif o.need:
                    s = self.esem[e]
                    s.v += 1
                    o.sem, o.val = s, s.v
        engmap = {"sp": block.sync, "act": block.scalar, "dve": block.vector, "pool": block.gpsimd, "pe": block.tensor}
        for e in ENGS:
            ops = self.ops[e]

            def body(eng, ops=ops):
                waited = {}
                for o in ops:
                    for d in o.deps:
                        v = d.sem.v if d.sem.group else d.val
                        if waited.get(d.sem.name, 0) >= v:
                            continue
                        waited[d.sem.name] = v
                        eng.wait_ge(d.sem.h, v)
                    if o.fn is None:
                        continue
                    m, a, kw = o.fn
                    ins = getattr(eng, m)(*a, **kw)
                    if o.dsem is not None:
                        ins.then_inc(o.sem.h, 16)
                    elif o.need:
                        ins.then_inc(o.sem.h, 1)

            engmap[e](body)


def t5_bucket_np(rel):
    half = 16
    max_exact = 8
    ret = np.where(rel > 0, half, 0)
    n = np.abs(rel)
    nf = np.maximum(n, 1).astype(np.float32)
    large = max_exact + (np.log(nf / max_exact) / math.log(128 / max_exact) * (half - max_exact)).astype(np.int32)
    large = np.minimum(large, half - 1)
    return ret + np.where(n < max_exact, n, large)


def build(NT):
    G = NT // 4
    NB = NT + 1
    nc = bass.Bass("TRN2", target_bir_lowering=False)
    P = Prog(nc)

    def din(name, shape, dt=F32):
        return nc.dram_tensor(name, shape, dt, kind="ExternalInput").ap()

    x_d = din("x", [NT * 128, D])
    meta_d = din("meta_pad", [128, D])
    win_d = din("w_in", [D, DIN])
    woa_d = din("w_out_a", [512, D])
    wob_d = din("w_out_b", [512, D])
    wo_d = din("w_out", [D, D])
    gcol_d = din("gcol", [128, 8])
    bcol_d = din("bcol", [128, 42])
    bqb_d = din("bqb", [128, 4])
    brow_d = din("brow", [1, 640])
    strips_d = din("strips", [12, 128, 1152])
    cfar_d = din("cfar", [128, 24])
    maskb_d = din("maskb", [128, 384])
    lamv_d = din("lamv", [128, 256])
    gsub_d = din("gsub", [128, 1])
    sink_d = din("sinkr", [128, 8])
    postg_d = din("postg", [128, D])
    ident_d = din("ident", [128, 128])
    y_d = nc.dram_tensor("y", [NT * 128, D], F32, kind="ExternalOutput").ap()
    wscr = nc.dram_tensor("wscr", [8, 128, DIN], BF16, kind="Internal").ap()
    woascr = nc.dram_tensor("woascr", [4, 128, D], BF16, kind="Internal").ap()
    wobscr = nc.dram_tensor("wobscr", [4, 128, D], BF16, kind="Internal").ap()
    woscr = nc.dram_tensor("woscr", [8, 128, D], BF16, kind="Internal").ap()
    hscr = nc.dram_tensor("hscr", [G + 1, 128, 4096], BF16, kind="Internal").ap()

    KVW = 2 * NB * 128 + NB * 258 + 64
    KVW += KVW % 2
    KVW = max(KVW, NB * 128 + NB * 130 + 2 + 2 * 8300)
    kv_t = nc.alloc_sbuf_tensor("kvarea", [128, KVW], BF16)
    oat_t = nc.alloc_sbuf_tensor("oat", [128, 4, NT * 128], BF16)
    xb_t = nc.alloc_sbuf_tensor("xb", [128, 2 * D], F32)
    xs_t = nc.alloc_sbuf_tensor("xs", [128, 2, D], BF16)
    hn_t = nc.alloc_sbuf_tensor("hnT", [128, 8, 512], BF16)
    cst_t = nc.alloc_sbuf_tensor("cst", [128, 512], F32)
    cbf_t = nc.alloc_sbuf_tensor("cbf", [128, 1024], BF16)
    ARENA_F32 = 12 * 1024 + 512
    ar_t = nc.alloc_sbuf_tensor("arena", [128, ARENA_F32], F32)
    ps_t = nc.alloc_psum_tensor("psf", [128, 7 * 512], F32)
    pst_t = nc.alloc_psum_tensor("pst", [128, 1024], BF16)

    class Arena:
        def __init__(self, ap_f32_2d, nwords):
            self.t = ap_f32_2d
            self.n = nwords
            self.pos = 0

        def reset(self):
            self.pos = 0

        def f32(self, n):
            a = self.t[:, self.pos:self.pos + n]
            self.pos += n
            assert self.pos <= self.n, ("arena overflow", self.pos, self.n)
            return a

        def bf16(self, n):
            n2 = (n + 1) // 2
            return self.f32(n2).bitcast(BF16)[:, 0:n]

    arena = Arena(ar_t[:, :], ARENA_F32)

    o_ = [0]

    def cslot(n):
        a = cst_t[:, o_[0]:o_[0] + n]
        o_[0] += n
        return a

    gcol = cslot(8)
    bcol = cslot(42)
    bhalf = cslot(42)
    bqb = cslot(4)
    cfar = cslot(24)
    zcol = cslot(1)
    gsub8 = cslot(1)
    esink = cslot(8)
    lamt = cslot(8)
    neglam = cslot(1)
    ssq = cslot(2)
    rstd = cslot(2)
    nhalf = cslot(4)
    assert o_[0] <= 512
    ident = cbf_t[:, 0:128]
    ones_row = cbf_t[0:1, 128:256]
    brow = cbf_t[0:1, 256:896]

    B_cst = P.buf("cst")
    xb = P.bufs_n(2, "xb")
    xsb = P.bufs_n(2, "xs")
    xb_sem = [Sem(nc, "xbs%d" % i) for i in range(2)]
    xs_sem = [Sem(nc, "xss%d" % i) for i in range(2)]
    B_hn = P.bufs_n(4, "hn")
    B_hn2 = P.bufs_n(4, "hn2")
    hn_views = [hn_t[:, :, :], xb_t[:, :].bitcast(BF16).rearrange("p (k n) -> p k n", k=8)]
    B_hnsets = [B_hn, B_hn2]
    hn_sem = [Sem(nc, "hns0"), Sem(nc, "hns1")]
    hst_sem = Sem(nc, "hst")
    bst_sem = Sem(nc, "bst")
    B_ssq = P.bufs_n(2, "ssq")
    B_rstd = P.bufs_n(2, "rstd")
    psb = P.bufs_n(7, "ps")
    B_pst = P.buf("pst")
    B_kt = P.buf("kt")
    B_va = P.buf("va")
    B_oat = P.buf("oat")
    setup_sem = Sem(nc, "setup", group=True)
    ld_sem = Sem(nc, "ld")
    ld_sem.group = False

    def psbank(b, n=512):
        return ps_t[:, b * 512:b * 512 + n]

    arena.reset()
    stage_small = arena.f32(1152)
    B_stage_small = P.buf("stage_small")
    lamv = arena.f32(256)
    sinkr = arena.f32(8)
    gsubr = arena.f32(1)
    identf = arena.f32(128)
    browf = arena.f32(640)

    setup_loads = []
    for (o, i) in [(gcol, gcol_d[:, :]), (bcol, bcol_d[:, :]), (bqb, bqb_d[:, :]), (cfar, cfar_d[:, :]),
                   (lamv, lamv_d[:, :]), (sinkr, sink_d[:, :]), (gsubr, gsub_d[:, :]), (identf, ident_d[:, :]),
                   (browf[0:1, :], brow_d[:, :])]:
        op = Op("sp", (I("dma_start", out=o, in_=i)), setup_sem)
        P.ops["sp"].append(op)
        P.dma_since_barrier.append(op)
        setup_loads.append(op)

    def cop(eng, fn):
        return P.op(eng, fn, reads=[], writes=[B_cst], after=setup_loads)

    cop("dve", I("memset", zcol, 0.0))
    cop("dve", I("memset", nhalf, -0.5))
    cop("dve", I("tensor_scalar", out=bhalf, in0=bcol, scalar1=0.5, scalar2=None, op0=ALU.mult))
    cop("dve", I("tensor_scalar", out=gsub8, in0=gsubr, scalar1=(1.0 - LAM_INIT), scalar2=None, op0=ALU.mult))
    cop("dve", I("tensor_copy", out=ident, in_=identf))
    cop("dve", I("memset", ones_row, 1.0))
    cop("dve", I("tensor_copy", out=brow, in_=browf[0:1, :]))
    prod = arena.f32(128)
    cop("dve", I("tensor_tensor", out=prod[:, 0:64], in0=lamv[:, 0:64], in1=lamv[:, 64:128], op=ALU.mult))
    cop("dve", I("tensor_tensor", out=prod[:, 64:128], in0=lamv[:, 128:192], in1=lamv[:, 192:256], op=ALU.mult))
    cop("dve", I("reduce_sum", out=lamt[:, 0:1], in_=prod[:, 0:64], axis=AX.X))
    cop("dve", I("reduce_sum", out=lamt[:, 1:2], in_=prod[:, 64:128], axis=AX.X))
    cop("act", I("activation", out=lamt[:, 2:4], in_=lamt[:, 0:2], func=AF.Exp))
    cop("act", I("activation", out=esink, in_=sinkr, func=AF.Exp))
    cop("dve", I("tensor_tensor", out=lamt[:, 4:5], in0=lamt[:, 3:4], in1=lamt[:, 2:3], op=ALU.subtract))
    cop("dve", I("tensor_scalar", out=neglam, in0=lamt[:, 4:5], scalar1=-LAM_INIT, scalar2=None, op0=ALU.add))

    cvt_i = [0]

    def convert(src_ap, dst_ap, n, scale_ap=None, scale_f=None, perm_qb=False):
        s = cvt_i[0] % 2
        cvt_i[0] += 1
        st = xb_t[:, s * D:s * D + n]
        ot = xs_t[:, s, 0:n]
        P.op("sp", I("dma_start", out=st, in_=src_ap), writes=[xb[s]], dsem=xb_sem[s])
        if perm_qb:
            src_v = xb_t[:, s * D:s * D + 512].rearrange("p (s j d) -> p s j d", s=2, j=4, d=64)
            dst_v = xs_t[:, s, 0:512].rearrange("p (j s d) -> p s j d", s=2, j=4, d=64)
            P.op("dve", I("tensor_scalar", out=dst_v, in0=src_v, scalar1=scale_ap, scalar2=None, op0=ALU.mult),
                 reads=[xb[s], B_cst], writes=[xsb[s]])
            st2 = xb_t[:, s * D + 512:s * D + n]
            ot2 = xs_t[:, s, 512:n]
            o2 = P.op("dve", I("tensor_scalar", out=ot2, in0=st2, scalar1=scale_ap, scalar2=None, op0=ALU.mult),
                      reads=[xb[s], B_cst], writes=[])
            P.op("pool", I("dma_start", out=dst_ap, in_=ot), reads=[xsb[s]], dsem=xs_sem[s], after=[o2])
            xsb[s].r[("dve2")] = o2
        else:
            sc = scale_ap if scale_ap is not None else scale_f
            P.op("dve", I("tensor_scalar", out=ot, in0=st, scalar1=sc, scalar2=None, op0=ALU.mult),
                 reads=[xb[s], B_cst], writes=[xsb[s]])
            P.op("pool", I("dma_start", out=dst_ap, in_=ot), reads=[xsb[s]], dsem=xs_sem[s])

    for k in range(8):
        for cb in range(6):
            c0 = cb * 1024
            n = min(1024, DIN - c0)
            convert(win_d[k * 128:(k + 1) * 128, c0:c0 + n], wscr[k, :, c0:c0 + n], n,
                    scale_ap=gcol[:, k:k + 1], perm_qb=(cb == 2))
    for k in range(4):
        convert(woa_d[k * 128:(k + 1) * 128, :], woascr[k, :, :], 1024, scale_f=0.5)
    for k in range(4):
        convert(wob_d[k * 128:(k + 1) * 128, :], wobscr[k, :, :], 1024, scale_f=0.5)
    for k in range(8):
        convert(wo_d[k * 128:(k + 1) * 128, :], woscr[k, :, :], 1024, scale_f=0.5)
    P.barrier()

    xcnt = [0]

    def prep_tile(src_ap, col):
        s = xcnt[0] % 2
        xcnt[0] += 1
        xt = xb_t[:, s * D:(s + 1) * D]
        xo = xs_t[:, s, :]
        P.op("sp", I("dma_start", out=xt, in_=src_ap), writes=[xb[s]], dsem=xb_sem[s])
        P.op("act", I("activation", out=xo, in_=xt, func=AF.Square, accum_out=ssq[:, s:s + 1]),
             reads=[xb[s]], writes=[xsb[s], B_ssq[s]])
        P.op("dve", I("tensor_scalar", out=rstd[:, s:s + 1], in0=ssq[:, s:s + 1], scalar1=1.0 / D, scalar2=EPS,
                                              op0=ALU.mult, op1=ALU.add), reads=[B_ssq[s]], writes=[B_rstd[s]])
        P.op("pool", I("tensor_tensor", out=rstd[:, s:s + 1], in0=rstd[:, s:s + 1], in1=nhalf[:, 0:1], op=ALU.pow),
             reads=[B_rstd[s], B_cst], writes=[B_rstd[s]])
        P.op("dve", I("tensor_scalar", out=xo, in0=xt, scalar1=rstd[:, s:s + 1], scalar2=None, op0=ALU.mult),
             reads=[xb[s], B_rstd[s]], writes=[xsb[s]])
        for k in range(8):
            P.op("pe", I("transpose", out=pst_t[:, k * 128:(k + 1) * 128], in_=xs_t[:, s, k * 128:(k + 1) * 128],
                                                  identity=ident),
                 reads=[xsb[s]] if k else [xsb[s], B_cst], writes=[B_pst] if k == 0 else [])
        lastT = P.ops["pe"][-1]
        B_pst.w = [lastT]
        xsb[s].r["pe"] = lastT
        P.op("dve", I("tensor_copy", out=hn_t[:, :, col * 128:(col + 1) * 128],
                                            in_=pst_t[:, :].rearrange("p (k t) -> p k t", k=8)),
             reads=[B_pst], writes=[B_hn[col]])

    def tile_src(c):
        return meta_d[:, :] if c == 0 else x_d[(c - 1) * 128:c * 128, :]

    def groups_kv():
        yield [0]
        for g in range(G):
            yield [1 + 4 * g + j for j in range(4)]

    glist = list(groups_kv())

    def hn_store(gi, nt):
        P.op("pool", I("dma_start", out=hscr[gi].rearrange("p (k n) -> p k n", k=8)[:, :, 0:nt * 128], in_=hn_t[:, :, 0:nt * 128]),
             reads=B_hn[0:nt], dsem=hst_sem)

    def hn_load(gi, nt, which):
        P.op("sp", I("dma_start", out=hn_views[which][:, :, 0:nt * 128],
                     in_=hscr[gi].rearrange("p (k n) -> p k n", k=8)[:, :, 0:nt * 128]),
             writes=B_hnsets[which], dsem=hn_sem[which])
        return hn_views[which], B_hnsets[which]

    def hn_iter(first):
        if first:
            for gi, blocks in enumerate(glist):
                for j, c in enumerate(blocks):
                    prep_tile(tile_src(c), j)
                hn_store(gi, len(blocks))
                yield blocks, hn_views[0], B_hn
        else:
            nxt = hn_load(0, len(glist[0]), 0)
            for gi, blocks in enumerate(glist):
                cur = nxt
                if gi + 1 < len(glist):
                    nxt = hn_load(gi + 1, len(glist[gi + 1]), (gi + 1) % 2)
                yield blocks, cur[0], cur[1]

    def load_w(dst_ap, src_ap, buf, sem):
        return P.op("sp", I("dma_start", out=dst_ap, in_=src_ap), writes=[buf], dsem=sem)

    wsem = [Sem(nc, "wsem%d" % i) for i in range(8)]

    for p in range(2):
        arena.reset()
        KT = kv_t[:, 0:2 * NB * 128].rearrange("p (h n) -> p h n", h=2)
        VA = kv_t[:, 2 * NB * 128:2 * NB * 128 + NB * 258].rearrange("p (c n) -> p c n", n=258)
        wk = arena.bf16(8 * 256).rearrange("p (k n) -> p k n", k=8)
        wv = arena.bf16(8 * 256).rearrange("p (k n) -> p k n", k=8)
        B_wk, B_wv = P.buf("wk"), P.buf("wv")
        load_w(wk, wscr[:, :, C_KA + p * 256:C_KA + (p + 1) * 256].rearrange("k p n -> p k n"), B_wk, wsem[0])
        load_w(wv, wscr[:, :, C_VA + p * 256:C_VA + (p + 1) * 256].rearrange("k p n -> p k n"), B_wv, wsem[1])
        B_vcol = P.buf("vcol")
        P.op("pool", I("memset", VA[:, :, 128:129], 1.0), writes=[B_vcol])
        P.op("pool", I("memset", VA[:, :, 257:258], 1.0), writes=[B_vcol])
        B_ktc = {}
        B_vac = {}
        for blocks, HN, BHN in hn_iter(p == 0):
            nt = len(blocks)
            c0 = blocks[0]
            for hh in range(2):
                bank = hh
                for k in range(8):
                    P.op("pe", I("matmul",
                        psbank(bank, nt * 128), lhsT=wk[:, k, hh * 128:(hh + 1) * 128], rhs=HN[:, k, 0:nt * 128],
                        start=(k == 0), stop=(k == 7)),
                        reads=([B_wk] + BHN[0:nt]) if k == 0 else [], writes=[psb[bank]] if k == 0 else [])
                last = P.ops["pe"][-1]
                psb[bank].w = [last]
                for b in BHN[0:nt]:
                    b.r["pe"] = last
                kb = P.buf("ktc")
                B_ktc[(blocks[0], hh)] = kb
                ch = (C_KA + p * 256) // 128 + hh
                P.op("dve", I("tensor_scalar",
                    out=KT[:, hh, c0 * 128:(c0 + nt) * 128], in0=psbank(bank, nt * 128), scalar1=bcol[:, ch:ch + 1],
                    scalar2=None, op0=ALU.add), reads=[psb[bank], B_cst], writes=[kb])
            for j, c in enumerate(blocks):
                bank = 2 + (j % 4)
                for k in range(8):
                    P.op("pe", I("matmul",
                        psbank(bank, 256), lhsT=HN[:, k, j * 128:(j + 1) * 128], rhs=wv[:, k, :],
                        start=(k == 0), stop=False),
                        reads=([B_wv, BHN[j]]) if k == 0 else [], writes=[psb[bank]] if k == 0 else [])
                P.op("pe", I("matmul",
                    psbank(bank, 256), lhsT=ones_row, rhs=brow[0:1, p * 256:(p + 1) * 256], start=False, stop=True),
                    reads=[B_cst])
                last = P.ops["pe"][-1]
                psb[bank].w = [last]
                BHN[j].r["pe"] = last
                vb = P.buf("vac")
                B_vac[c] = vb
                P.op("act", I("activation",
                    out=VA[:, c, :].rearrange("p (h n) -> p h n", h=2)[:, :, 0:128],
                    in_=psbank(bank, 256).rearrange("p (h n) -> p h n", h=2), func=AF.Copy),
                    reads=[psb[bank]], writes=[vb])
                if c == 0:
                    P.op("pool", I("memset", VA[0:128 - NMETA, 0, :], 0.0), reads=[], writes=[vb], after=list(B_vcol.w))
        P.barrier()

        arena.reset()
        wq = arena.bf16(8 * 256).rearrange("p (k n) -> p k n", k=8)
        strips = arena.bf16(2 * 1152).rearrange("p (h n) -> p h n", h=2)
        QT = arena.bf16(2 * 2 * 512).rearrange("p (s h n) -> p s h n", s=2, h=2)
        ET = arena.bf16(3 * 1024).rearrange("p (s n) -> p s n", s=3)
        accs = arena.f32(8 * 129).rearrange("p (a n) -> p a n", a=8)
        ofin = arena.f32(4 * 128).rearrange("p (j n) -> p j n", j=4)
        t1 = arena.f32(128)
        sq = arena.f32(128)
        onb = arena.bf16(4 * 128).rearrange("p (j n) -> p j n", j=4)
        sm = arena.f32(32)
        sstage = arena.f32(1152)
        B_wq, B_strips = P.buf("wq"), P.buf("strips")
        B_QT = P.bufs_n(2, "QT")
        B_ET = P.bufs_n(3, "ET")
        B_accs, B_fin, B_on, B_sstage = P.buf("accs"), P.buf("fin"), P.buf("on"), P.buf("sstage")
        load_w(wq, wscr[:, :, C_QA + p * 256:C_QA + (p + 1) * 256].rearrange("k p n -> p k n"), B_wq, wsem[0])
        for hh in range(2):
            P.op("sp", I("dma_start", out=sstage, in_=strips_d[2 * p + hh, :, :]), writes=[B_sstage], dsem=wsem[2])
            P.op("dve", I("tensor_copy", out=strips[:, hh, :], in_=sstage), reads=[B_sstage], writes=[B_strips])
        ucnt = [0]
        ecnt = [0]
        pstf = pst_t[:, :].bitcast(F32)
        deferred = []

        def defer(n, tag, fn):
            deferred.append([n, tag, fn])

        def tick():
            for d in deferred:
                d[0] -= 1
            due = [d for d in deferred if d[0] <= 0]
            for d in due:
                deferred.remove(d)
                d[2]()

        def flush_deferred(tag=None):
            due = [d for d in deferred if tag is None or d[1] == tag]
            for d in due:
                deferred.remove(d)
                d[2]()

        def qproj_mm(hh, k0, k1, HNq, BHNq):
            for k in range(k0, k1):
                P.op("pe", I("matmul", pstf, lhsT=wq[:, k, hh * 128:(hh + 1) * 128], rhs=HNq[:, k, :],
                             start=(k == 0), stop=(k == 7)),
                     reads=([B_wq] + BHNq) if k == 0 else [], writes=[B_pst] if k == 0 else [])
            if k1 == 8:
                last = P.ops["pe"][-1]
                B_pst.w = [last]
                for b in BHNq:
                    b.r["pe"] = last

        def qproj_evac(hh, qsl):
            ch = (C_QA + p * 256) // 128 + hh
            P.op("dve", I("tensor_scalar", out=QT[:, qsl, hh, :], in0=pstf, scalar1=bcol[:, ch:ch + 1], scalar2=0.125,
                          op0=ALU.add, op1=ALU.mult), reads=[B_pst, B_cst], writes=[B_QT[qsl]] if hh == 0 else [])
            if hh == 1:
                B_QT[qsl].w.append(P.ops["dve"][-1])

        def qproj_schedule(gq, HNq, BHNq, now):
            qsl = gq % 2
            t = 22
            for hh in range(2):
                for k0 in (0, 2, 4, 6):
                    fn = (lambda hh=hh, k0=k0: qproj_mm(hh, k0, k0 + 2, HNq, BHNq))
                    if now:
                        fn()
                    else:
                        defer(t, "q", fn)
                    t += 1
                fn = (lambda hh=hh: qproj_evac(hh, qsl))
                if now:
                    fn()
                else:
                    defer(t, "q", fn)
                t += 2

        def B_ktc_get(c, hh):
            for (c0, h2), b in B_ktc.items():
                if h2 == hh and (c0 == c or (c0 >= 1 and c0 <= c < c0 + 4 and c >= 1)):
                    return b
            raise KeyError((c, hh))

        def emit_scores(c, g, hh, h, qs, b0):
            slot = ucnt[0] % 2
            ucnt[0] += 1
            bk = 2 * slot
            Dd = c - b0
            mixed = (-1 <= Dd <= 4)
            P.op("pe", I("matmul", psbank(bk), lhsT=KT[0:64, hh, c * 128:(c + 1) * 128],
                         rhs=QT[0:64, qs, hh, :], start=True, stop=not mixed),
                 reads=[B_ktc_get(c, hh), B_QT[qs]], writes=[psb[bk], psb[bk + 1]])
            P.op("pe", I("matmul", psbank(bk + 1), lhsT=KT[64:128, hh, c * 128:(c + 1) * 128],
                         rhs=QT[64:128, qs, hh, :], start=True, stop=not mixed))
            if mixed:
                o0 = (4 - Dd) * 128
                P.op("pe", I("matmul", psbank(bk), lhsT=ident, rhs=strips[:, hh, o0:o0 + 512],
                             start=False, stop=True), reads=[B_strips, B_cst])
                P.op("pe", I("matmul", psbank(bk + 1), lhsT=ident, rhs=strips[:, hh, o0:o0 + 512],
                             start=False, stop=True))
            last = P.ops["pe"][-1]
            psb[bk].w = [last]
            psb[bk + 1].w = [last]
            B_QT[qs].r["pe"] = last
            es = ecnt[0] % 3
            ecnt[0] += 1
            if mixed:
                bias_ap = zcol
            elif Dd <= -2:
                bias_ap = cfar[:, h:h + 1]
            else:
                bias_ap = cfar[:, 12 + h:13 + h]
            P.op("act", I("activation", out=ET[:, es, :], in_=ps_t[:, bk * 512:bk * 512 + 1024], func=AF.Exp,
                          bias=bias_ap, scale=1.0),
                 reads=[psb[bk], psb[bk + 1], B_cst], writes=[B_ET[es]])
            return (c, es)

        def emit_av(c, es, hh):
            first = (c == 0)
            lastc = (c == NB - 1)
            for j in range(4):
                for m in range(2):
                    a = 2 * j + m
                    bank = 4 + a // 3
                    off = (a % 3) * 129
                    P.op("pe", I("matmul", ps_t[:, bank * 512 + off:bank * 512 + off + 129],
                                 lhsT=ET[:, es, m * 512 + j * 128:m * 512 + (j + 1) * 128],
                                 rhs=VA[:, c, hh * 129:hh * 129 + 129], start=(first and a % 3 == 0), stop=lastc,
                                 skip_group_check=True),
                         reads=([B_ET[es], B_vac[c]]) if a == 0 else [],
                         writes=[psb[4], psb[5], psb[6]] if (first and a == 0) else [])
            last = P.ops["pe"][-1]
            B_ET[es].r["pe"] = last
            if lastc:
                for b in (psb[4], psb[5], psb[6]):
                    b.w = [last]

        def finalize_part1():
            for bi, na in ((4, 3), (5, 3), (6, 2)):
                a0 = (bi - 4) * 3
                P.op("dve", I("tensor_copy", out=accs[:, a0:a0 + na, :],
                              in_=psbank(bi, na * 129).rearrange("p (a n) -> p a n", a=na)),
                     reads=[psb[bi]], writes=[B_accs] if bi == 4 else [])
                if bi != 4:
                    B_accs.w.append(P.ops["dve"][-1])

            def fin(fn):
                return P.op("dve", fn, reads=[B_accs, B_fin, B_cst], writes=[B_fin])

            fin(I("reciprocal", out=sm[:, 0:8], in_=accs[:, :, 128:129].rearrange("p a o -> p (a o)")))
            fin(I("tensor_scalar", out=sm[:, 8:12], in0=sm[:, 0:8].rearrange("p (j m) -> p j m", m=2)[:, :, 1],
                  scalar1=neglam, scalar2=None, op0=ALU.mult))
            for j in range(4):
                fin(I("tensor_scalar", out=t1, in0=accs[:, 2 * j + 1, 0:128], scalar1=sm[:, 8 + j:9 + j],
                      scalar2=None, op0=ALU.mult))
                fin(I("scalar_tensor_tensor", out=ofin[:, j, :], in0=accs[:, 2 * j, 0:128],
                      scalar=sm[:, 2 * j:2 * j + 1], in1=t1, op0=ALU.mult, op1=ALU.add))
                fin(I("tensor_tensor", out=sq, in0=ofin[:, j, :], in1=ofin[:, j, :], op=ALU.mult))
                fin(I("reduce_sum", out=sm[:, 12 + j:13 + j], in_=sq, axis=AX.X))
            fin(I("tensor_scalar", out=sm[:, 16:20], in0=sm[:, 12:16], scalar1=1.0 / 128, scalar2=EPS,
                  op0=ALU.mult, op1=ALU.add))
            P.op("pool", I("tensor_tensor", out=sm[:, 16:20], in0=sm[:, 16:20], in1=nhalf[:, 0:4], op=ALU.pow),
                 reads=[B_fin, B_cst], writes=[B_fin])
            for j in range(4):
                P.op("dve", I("tensor_scalar", out=onb[:, j, :], in0=ofin[:, j, :], scalar1=sm[:, 16 + j:17 + j],
                              scalar2=None, op0=ALU.mult),
                     reads=[B_fin], writes=[B_on] if j == 0 else [])
                if j:
                    B_on.w.append(P.ops["dve"][-1])

        def finalize_part2(h, g):
            for j in range(4):
                P.op("pe", I("transpose", out=pst_t[:, j * 128:(j + 1) * 128], in_=onb[:, j, :], identity=ident),
                     reads=[B_on, B_cst] if j == 0 else [], writes=[B_pst] if j == 0 else [])
            lastT = P.ops["pe"][-1]
            B_pst.w = [lastT]
            B_on.r["pe"] = lastT
            P.op("dve", I("tensor_scalar", out=oat_t[:, h, g * 512:(g + 1) * 512], in0=pst_t[:, 0:512],
                          scalar1=gsub8, scalar2=None, op0=ALU.mult),
                 reads=[B_pst, B_cst], writes=[])

        nxt = hn_load(1, 4, 0)
        qproj_schedule(0, nxt[0], nxt[1], now=True)
        for g in range(G):
            qs = g % 2
            flush_deferred("q")
            if g + 1 < G:
                nxt = hn_load(2 + g, 4, (g + 1) % 2)
            b0 = 1 + 4 * g
            for hh in range(2):
                h = 2 * p + hh
                if hh == 1 and g + 1 < G:
                    qproj_schedule(g + 1, nxt[0], nxt[1], now=False)
                prev = emit_scores(0, g, hh, h, qs, b0)
                for c in range(1, NB):
                    cur = emit_scores(c, g, hh, h, qs, b0)
                    emit_av(prev[0], prev[1], hh)
                    tick()
                    prev = cur
                emit_av(prev[0], prev[1], hh)
                flush_deferred("f")
                finalize_part1()
                defer(20, "f", (lambda h=h, g=g: finalize_part2(h, g)))
        flush_deferred()
        P.barrier()

    arena.reset()
    KTB = kv_t[:, 0:NB * 128]
    VB = kv_t[:, NB * 128:NB * 128 + NB * 130].rearrange("p (c n) -> p c n", n=130)
    kvfree0 = NB * 128 + NB * 130
    kvfree0 += kvfree0 % 2
    wkb = arena.bf16(8 * 128).rearrange("p (k n) -> p k n", k=8)
    wvb = arena.bf16(8 * 128).rearrange("p (k n) -> p k n", k=8)
    B_wk, B_wv = P.buf("wkb"), P.buf("wvb")
    load_w(wkb, wscr[:, :, C_KB:C_KB + 128].rearrange("k p n -> p k n"), B_wk, wsem[0])
    load_w(wvb, wscr[:, :, C_VB:C_VB + 128].rearrange("k p n -> p k n"), B_wv, wsem[1])
    B_vcol = P.buf("vcolb")
    P.op("pool", I("memset", VB[:, :, 64:65], 1.0), writes=[B_vcol])
    P.op("pool", I("memset", VB[:, :, 129:130], 1.0), writes=[B_vcol])
    for blocks, HN, BHN in hn_iter(False):
        nt = len(blocks)
        c0 = blocks[0]
        for k in range(8):
            P.op("pe", I("matmul", psbank(0, nt * 128), lhsT=wkb[:, k, :], rhs=HN[:, k, 0:nt * 128],
                                               start=(k == 0), stop=(k == 7)),
                 reads=([B_wk] + BHN[0:nt]) if k == 0 else [], writes=[psb[0]] if k == 0 else [])
        last = P.ops["pe"][-1]
        psb[0].w = [last]
        for b in BHN[0:nt]:
            b.r["pe"] = last
        P.op("dve", I("tensor_scalar",
            out=KTB[:, c0 * 128:(c0 + nt) * 128], in0=psbank(0, nt * 128), scalar1=bcol[:, C_KB // 128:C_KB // 128 + 1],
            scalar2=None, op0=ALU.add), reads=[psb[0], B_cst], writes=[])
        psb[0].r["dve"] = P.ops["dve"][-1]
        for j, c in enumerate(blocks):
            bank = 2 + (j % 4)
            for k in range(8):
                P.op("pe", I("matmul",
                    psbank(bank, 128), lhsT=HN[:, k, j * 128:(j + 1) * 128], rhs=wvb[:, k, :], start=(k == 0), stop=False),
                    reads=([B_wv, BHN[j]]) if k == 0 else [], writes=[psb[bank]] if k == 0 else [])
            P.op("pe", I("matmul", psbank(bank, 128), lhsT=ones_row, rhs=brow[0:1, 512:640],
                                                     start=False, stop=True), reads=[B_cst])
            last = P.ops["pe"][-1]
            psb[bank].w = [last]
            BHN[j].r["pe"] = last
            P.op("act", I("activation",
                out=VB[:, c, :].rearrange("p (h n) -> p h n", h=2)[:, :, 0:64],
                in_=psbank(bank, 128).rearrange("p (h n) -> p h n", h=2), func=AF.Copy),
                reads=[psb[bank]], writes=[])
            psb[bank].r["act"] = P.ops["act"][-1]
            if c == 0:
                P.op("pool", I("memset", VB[0:128 - NMETA, 0, :], 0.0), after=list(B_vcol.w) + [P.ops["act"][-1]])
    P.barrier()

    arena.reset()
    ar2 = Arena(kv_t[:, kvfree0:KVW].bitcast(F32), (KVW - kvfree0) // 2)
    biasB = arena.bf16(8 * 384).rearrange("p (h n) -> p h n", h=8)
    Woa = arena.bf16(4 * 1024).rearrange("p (k n) -> p k n", k=4)
    Wob = arena.bf16(4 * 1024).rearrange("p (k n) -> p k n", k=4)
    Wring = [arena.bf16(4096), arena.bf16(4096)]
    postg = arena.f32(1024)
    ttile = arena.f32(512)
    sz = arena.f32(512)
    ta = arena.f32(512)
    EBs = [ar2.bf16(1536), ar2.bf16(1536)]
    mixedT = ar2.bf16(8 * 512).rearrange("p (k n) -> p k n", k=8)
    goaT = ar2.bf16(4 * 512).rearrange("p (k n) -> p k n", k=4)
    gobT = ar2.bf16(4 * 512).rearrange("p (k n) -> p k n", k=4)
    QTb = ar2.bf16(4 * 512).rearrange("p (k n) -> p k n", k=4)
    obt = ar2.bf16(4 * 512).rearrange("p (t n) -> p t n", t=4)
    tb = ar2.f32(512)
    smf = ar2.f32(16)
    smf2 = ar2.f32(8)
    bstage = ttile[:, 0:384]
    maskb = sz[:, 0:384]
    pstf = pst_t[:, :].bitcast(F32)
    B_biasB, B_Woa, B_Wob, B_postg = P.buf("biasB"), P.buf("Woa"), P.buf("Wob"), P.buf("postg")
    B_ring = P.bufs_n(2, "ring")
    ring_sem = [wsem[3], wsem[4]]
    B_t, B_sz, B_mixed, B_goa, B_gob, B_QTb = (P.buf("t"), P.buf("sz"), P.buf("mixed"), P.buf("goa"), P.buf("gob"), P.buf("QTb"))
    B_ob = P.bufs_n(4, "ob")
    B_EB = P.bufs_n(2, "EB")
    B_ta, B_tb = P.buf("ta"), P.buf("tb")
    B_smfs = P.bufs_n(2, "smf")
    B_smf2 = P.bufs_n(2, "smf2")
    out_sem = [wsem[5], wsem[6]]
    load_w(Woa, woascr.rearrange("k p n -> p k n"), B_Woa, wsem[0])
    load_w(Wob, wobscr.rearrange("k p n -> p k n"), B_Wob, wsem[1])
    load_w(postg, postg_d[:, :], B_postg, wsem[2])
    load_w(maskb, maskb_d[:, :], B_sz, wsem[7])
    for h8 in range(8):
        P.op("sp", I("dma_start", out=bstage, in_=strips_d[4 + h8, :, 384:768]), writes=[B_t], dsem=bst_sem)
        P.op("dve", I("tensor_tensor", out=biasB[:, h8, :], in0=bstage, in1=maskb, op=ALU.add),
             reads=[B_t, B_sz], writes=[B_biasB])
    rcnt = [0]

    def ring_load(src_ap):
        s = rcnt[0] % 2
        rcnt[0] += 1
        load_w(Wring[s].rearrange("p (k n) -> p k n", k=8), src_ap, B_ring[s], ring_sem[s])
        return s

    def proj_fm(wblk, s, col0, bank):
        for k in range(8):
            P.op("pe", I("matmul", psbank(bank), lhsT=wblk[:, k, col0:col0 + 128], rhs=hn_t[:, k, :],
                         start=(k == 0), stop=(k == 7)),
                 reads=([B_ring[s]] + B_hn) if k == 0 else [], writes=[psb[bank]] if k == 0 else [])
        last = P.ops["pe"][-1]
        psb[bank].w = [last]
        B_ring[s].r["pe"] = last
        for b in B_hn:
            b.r["pe"] = last

    ebc = [0]
    fcnt = [0]
    ACCS = [(ps_t[:, 6 * 512:6 * 512 + 260], psb[6]), (pstf[:, 0:260], B_pst)]
    for g in range(G):
        hn_load(1 + g, 4, 0)
        s = ring_load(wscr[:, :, C_QB:C_QB + 512].rearrange("k p n -> p k n"))
        wblk = Wring[s].rearrange("p (k n) -> p k n", k=8)
        for j in range(4):
            bank = j % 2
            proj_fm(wblk, s, j * 128, bank)
            P.op("dve", I("tensor_scalar", out=QTb[:, j, :], in0=psbank(bank), scalar1=bqb[:, j:j + 1],
                          scalar2=0.125, op0=ALU.add, op1=ALU.mult),
                 reads=[psb[bank], B_cst], writes=[B_QTb] if j == 0 else [])
            if j:
                B_QTb.w.append(P.ops["dve"][-1])
        for tq in range(4):
            b = 1 + 4 * g + tq
            cs = [c for c in (b - 1, b, b + 1) if 0 <= c < NB]
            jj0 = cs[0] - (b - 1)
            ncs = len(cs)
            for half in range(2):
                sl = ebc[0] % 2
                ebc[0] += 1
                r0 = half * 64
                h0 = half * 4
                banks = [3 * sl + (c - (b - 1)) for c in cs]
                for i, c in enumerate(cs):
                    jj = c - (b - 1)
                    bk = 3 * sl + jj
                    P.op("pe", I("matmul", psbank(bk), lhsT=KTB[r0:r0 + 64, c * 128:(c + 1) * 128],
                                 rhs=QTb[r0:r0 + 64, :, tq * 128:(tq + 1) * 128], start=True, stop=False),
                         reads=[B_QTb] if i == 0 else [], writes=[psb[x] for x in banks] if i == 0 else [])
                    P.op("pe", I("matmul", psbank(bk), lhsT=ident, rhs=biasB[:, h0:h0 + 4, (2 - jj) * 128:(3 - jj) * 128],
                                 start=False, stop=True), reads=[B_biasB, B_cst] if i == 0 else [])
                last = P.ops["pe"][-1]
                for x in banks:
                    psb[x].w = [last]
                B_QTb.r["pe"] = last
                lo = (3 * sl + jj0) * 512
                P.op("act", I("activation", out=EBs[sl][:, jj0 * 512:(jj0 + ncs) * 512], in_=ps_t[:, lo:lo + ncs * 512], func=AF.Exp),
                     reads=[psb[x] for x in banks], writes=[B_EB[sl]])
                acc_ap, acc_buf = ACCS[sl]
                firstmm = True
                for i4 in range(4):
                    for i, c in enumerate(cs):
                        jj = c - (b - 1)
                        P.op("pe", I("matmul", acc_ap[:, i4 * 65:(i4 + 1) * 65],
                                     lhsT=EBs[sl][:, jj * 512 + i4 * 128:jj * 512 + (i4 + 1) * 128],
                                     rhs=VB[:, c, half * 65:half * 65 + 65], start=firstmm, stop=(i == ncs - 1),
                                     skip_group_check=True),
                             reads=[B_EB[sl]] if firstmm else [], writes=[acc_buf] if firstmm else [])
                        firstmm = False
                last = P.ops["pe"][-1]
                acc_buf.w = [last]
                B_EB[sl].r["pe"] = last
                acc3 = acc_ap.rearrange("p (i n) -> p i n", n=65)
                sms = smf[:, sl * 8:sl * 8 + 8]
                P.op("dve", I("tensor_tensor", out=sms[:, 0:4], in0=acc3[:, :, 64:65].rearrange("p i o -> p (i o)"),
                              in1=esink[:, h0:h0 + 4], op=ALU.add),
                     reads=[acc_buf, B_cst], writes=[B_smfs[sl]])
                P.op("dve", I("reciprocal", out=sms[:, 4:8], in_=sms[:, 0:4]), reads=[B_smfs[sl]], writes=[B_smfs[sl]])
                for i4 in range(4):
                    P.op("dve", I("tensor_scalar", out=obt[:, tq, (h0 + i4) * 64:(h0 + i4 + 1) * 64], in0=acc3[:, i4, 0:64],
                                  scalar1=sms[:, 4 + i4:5 + i4], scalar2=None, op0=ALU.mult),
                         reads=[acc_buf, B_smfs[sl]], writes=[B_ob[tq]] if (half == 0 and i4 == 0) else [])
                    if half or i4:
                        B_ob[tq].w.append(P.ops["dve"][-1])

        def gate(col_base, which):
            s = ring_load(wscr[:, :, col_base:col_base + 512].rearrange("k p n -> p k n"))
            wblk = Wring[s].rearrange("p (k n) -> p k n", k=8)
            for c in range(4):
                bank = c % 2
                ch = col_base // 128 + c
                proj_fm(wblk, s, c * 128, bank)
                P.op("act", I("activation", out=ttile, in_=psbank(bank), func=AF.Tanh, bias=bhalf[:, ch:ch + 1], scale=0.5),
                     reads=[psb[bank], B_cst], writes=[B_t])
                P.op("dve", I("scalar_tensor_tensor", out=sz, in0=psbank(bank), scalar=bcol[:, ch:ch + 1], in1=ttile,
                              op0=ALU.add, op1=ALU.mult), reads=[psb[bank], B_t, B_cst], writes=[B_sz])
                P.op("dve", I("scalar_tensor_tensor", out=sz, in0=psbank(bank), scalar=bcol[:, ch:ch + 1], in1=sz,
                              op0=ALU.add, op1=ALU.add), reads=[psb[bank], B_cst], writes=[B_sz])
                if which == "a":
                    P.op("dve", I("tensor_tensor", out=goaT[:, c, :], in0=oat_t[:, c, g * 512:(g + 1) * 512], in1=sz, op=ALU.mult),
                         reads=[B_sz], writes=[B_goa] if c == 0 else [])
                    if c:
                        B_goa.w.append(P.ops["dve"][-1])
                else:
                    for tq in range(4):
                        P.op("pe", I("transpose", out=pst_t[:, tq * 128:(tq + 1) * 128], in_=obt[:, tq, c * 128:(c + 1) * 128],
                                     identity=ident), reads=[B_ob[tq], B_cst], writes=[B_pst] if tq == 0 else [])
                    lastT = P.ops["pe"][-1]
                    B_pst.w = [lastT]
                    for tq in range(4):
                        B_ob[tq].r["pe"] = lastT
                    P.op("dve", I("tensor_tensor", out=gobT[:, c, :], in0=pst_t[:, 0:512], in1=sz, op=ALU.mult),
                         reads=[B_pst, B_sz], writes=[B_gob] if c == 0 else [])
                    if c:
                        B_gob.w.append(P.ops["dve"][-1])

        gate(C_ZB, "b")
        gate(C_ZA, "a")
        for half in range(2):
            sa = ring_load(wscr[:, :, C_GA + half * 512:C_GA + (half + 1) * 512].rearrange("k p n -> p k n"))
            wa = Wring[sa].rearrange("p (k n) -> p k n", k=8)
            sb = None
            for c4 in range(4):
                fc = half * 4 + c4
                proj_fm(wa, sa, c4 * 128, 0)
                cha = C_GA // 128 + fc
                P.op("act", I("activation", out=ta, in_=psbank(0), func=AF.Tanh, bias=bhalf[:, cha:cha + 1], scale=0.5),
                     reads=[psb[0], B_cst], writes=[B_ta])
                if c4 == 0:
                    sb = ring_load(wscr[:, :, C_GB + half * 512:C_GB + (half + 1) * 512].rearrange("k p n -> p k n"))
                    wb = Wring[sb].rearrange("p (k n) -> p k n", k=8)
                proj_fm(wb, sb, c4 * 128, 1)
                chb = C_GB // 128 + fc
                P.op("act", I("activation", out=tb, in_=psbank(1), func=AF.Tanh, bias=bhalf[:, chb:chb + 1], scale=0.5),
                     reads=[psb[1], B_cst], writes=[B_tb])
                for k in range(4):
                    P.op("pe", I("matmul", psbank(2), lhsT=Woa[:, k, fc * 128:(fc + 1) * 128], rhs=goaT[:, k, :],
                                 start=(k == 0), stop=(k == 3)),
                         reads=[B_Woa, B_goa] if k == 0 else [], writes=[psb[2]] if k == 0 else [])
                last = P.ops["pe"][-1]
                psb[2].w = [last]
                B_goa.r["pe"] = last
                for k in range(4):
                    P.op("pe", I("matmul", psbank(3), lhsT=Wob[:, k, fc * 128:(fc + 1) * 128], rhs=gobT[:, k, :],
                                 start=(k == 0), stop=(k == 3)),
                         reads=[B_Wob, B_gob] if k == 0 else [], writes=[psb[3]] if k == 0 else [])
                last = P.ops["pe"][-1]
                psb[3].w = [last]
                B_gob.r["pe"] = last
                P.op("dve", I("scalar_tensor_tensor", out=ta, in0=ta, scalar=1.0, in1=psbank(2), op0=ALU.add, op1=ALU.mult),
                     reads=[psb[2]], writes=[B_ta])
                P.op("dve", I("scalar_tensor_tensor", out=tb, in0=tb, scalar=1.0, in1=psbank(3), op0=ALU.add, op1=ALU.mult),
                     reads=[psb[3]], writes=[B_tb])
                P.op("dve", I("tensor_tensor", out=mixedT[:, fc, :], in0=ta, in1=tb, op=ALU.add),
                     reads=[B_ta, B_tb], writes=[B_mixed] if fc == 0 else [])
                if fc:
                    B_mixed.w.append(P.ops["dve"][-1])
        so = [ring_load(woscr[:, :, hf * 512:(hf + 1) * 512].rearrange("k p n -> p k n")) for hf in range(2)]
        for tq in range(4):
            fs = fcnt[0] % 2
            fcnt[0] += 1
            fb = (4, 5) if fs == 0 else (2, 3)
            sm2 = smf2[:, fs * 4:fs * 4 + 4]
            for hf in range(2):
                wblk = Wring[so[hf]].rearrange("p (k n) -> p k n", k=8)
                bank = fb[hf]
                for k in range(8):
                    P.op("pe", I("matmul", psbank(bank), lhsT=mixedT[:, k, tq * 128:(tq + 1) * 128], rhs=wblk[:, k, :],
                                 start=(k == 0), stop=(k == 7)),
                         reads=[B_ring[so[hf]], B_mixed] if k == 0 else [], writes=[psb[bank]] if k == 0 else [])
                last = P.ops["pe"][-1]
                psb[bank].w = [last]
                B_ring[so[hf]].r["pe"] = last
                B_mixed.r["pe"] = last
            P.op("act", I("activation", out=ttile, in_=psbank(fb[0]), func=AF.Square, accum_out=sm2[:, 0:1]),
                 reads=[psb[fb[0]]], writes=[B_t, B_smf2[fs]])
            P.op("act", I("activation", out=ttile, in_=psbank(fb[1]), func=AF.Square, accum_out=sm2[:, 1:2]),
                 reads=[psb[fb[1]]], writes=[B_t, B_smf2[fs]])
            P.op("dve", I("tensor_tensor", out=sm2[:, 2:3], in0=sm2[:, 0:1], in1=sm2[:, 1:2], op=ALU.add),
                 reads=[B_smf2[fs]], writes=[B_smf2[fs]])
            P.op("dve", I("tensor_scalar", out=sm2[:, 2:3], in0=sm2[:, 2:3], scalar1=1.0 / D, scalar2=EPS, op0=ALU.mult, op1=ALU.add),
                 reads=[B_smf2[fs]], writes=[B_smf2[fs]])
            P.op("pool", I("tensor_tensor", out=sm2[:, 3:4], in0=sm2[:, 2:3], in1=nhalf[:, 0:1], op=ALU.pow),
                 reads=[B_smf2[fs], B_cst], writes=[B_smf2[fs]])
            s = xcnt[0] % 2
            xcnt[0] += 1
            xsl = xb_t[:, s * D:(s + 1) * D]
            xsrc = x_d[(4 * g + tq) * 128:(4 * g + tq + 1) * 128, :]
            P.op("sp", I("dma_start", out=xsl, in_=xsrc), writes=[xb[s]], dsem=xb_sem[s])
            for hf, (tmp, B_tmp) in enumerate(((sz, B_sz), (ta, B_ta))):
                P.op("dve", I("scalar_tensor_tensor", out=tmp, in0=psbank(fb[hf]), scalar=sm2[:, 3:4],
                              in1=postg[:, hf * 512:(hf + 1) * 512], op0=ALU.mult, op1=ALU.mult),
                     reads=[psb[fb[hf]], B_smf2[fs], B_postg], writes=[B_tmp])
                P.op("dve", I("tensor_tensor", out=xsl[:, hf * 512:(hf + 1) * 512], in0=xsl[:, hf * 512:(hf + 1) * 512], in1=tmp,
                              op=ALU.add), reads=[B_tmp, xb[s]], writes=[xb[s]] if hf == 0 else [])
                if hf:
                    xb[s].w.append(P.ops["dve"][-1])
            ysl = y_d[(4 * g + tq) * 128:(4 * g + tq + 1) * 128, :]
            P.op("pool", I("dma_start", out=ysl, in_=xsl), reads=[xb[s]], dsem=out_sem[s])
    P.barrier()

    with nc.Block() as block:
        P.flush(block)
    nc._marks = P.marks
    return nc


_NC_CACHE = {}


def host_consts(rel_bias, pre_norm_g, b_in, lambda_q1, lambda_k1, lambda_q2, lambda_k2, subln_g, sink, post_norm_g):
    f = np.float32
    rel_bias = np.asarray(rel_bias, f)
    b = np.asarray(b_in, f).reshape(-1)
    kk = np.arange(128)[:, None]
    v = np.arange(1152)[None, :]
    bucket = t5_bucket_np((512 + kk - v).astype(np.int32))
    strips = np.ascontiguousarray(np.transpose(rel_bias[bucket], (2, 0, 1)))
    cfar = np.ascontiguousarray(np.broadcast_to(np.concatenate([rel_bias[15], rel_bias[31]])[None, :], (128, 24)))
    x384 = np.arange(384)[None, :]
    rel = 128 + kk - x384
    maskb = np.where(np.abs(rel) <= 128, 0.0, -30000.0).astype(f)
    gcol = np.ascontiguousarray(np.asarray(pre_norm_g, f).reshape(8, 128).T)
    bcol = np.ascontiguousarray(b.reshape(42, 128).T)
    bq = b[C_QB:C_QB + 512].reshape(2, 4, 64)
    bqb = np.ascontiguousarray(np.transpose(bq, (1, 0, 2)).reshape(4, 128).T)
    brow = np.ascontiguousarray(np.concatenate([b[C_VA:C_VA + 512], b[C_VB:C_VB + 128]])[None, :])
    lamv = np.concatenate([np.asarray(a, f).reshape(-1) for a in (lambda_q1, lambda_k1, lambda_q2, lambda_k2)])
    lamv = np.ascontiguousarray(np.broadcast_to(lamv[None, :], (128, 256)))
    gsub = np.ascontiguousarray(np.asarray(subln_g, f).reshape(128, 1))
    sinkr = np.ascontiguousarray(np.broadcast_to(np.asarray(sink, f).reshape(1, 8), (128, 8)))
    postg = np.ascontiguousarray(np.broadcast_to(np.asarray(post_norm_g, f).reshape(1, D), (128, D)))
    ident = np.eye(128, dtype=f)
    return dict(strips=strips, cfar=cfar, maskb=maskb, gcol=gcol, bcol=bcol, bqb=bqb, brow=brow, lamv=lamv, gsub=gsub,
                sinkr=sinkr, postg=postg, ident=ident)


def kernel(x, meta_tokens, rel_bias, pre_norm_g, w_in, b_in, lambda_q1, lambda_k1, lambda_q2, lambda_k2,
           subln_g, sink, w_out_a, w_out_b, w_out, post_norm_g):
    x = np.asarray(x, np.float32)
    B, S, _ = x.shape
    NT = S // 128
    if NT not in _NC_CACHE:
        _NC_CACHE[NT] = build(NT)
    nc = _NC_CACHE[NT]
    consts = host_consts(rel_bias, pre_norm_g, b_in, lambda_q1, lambda_k1, lambda_q2, lambda_k2, subln_g, sink, post_norm_g)
    meta_pad = np.zeros((128, D), np.float32)
    meta_pad[128 - NMETA:] = np.asarray(meta_tokens, np.float32)
    shared = dict(meta_pad=meta_pad, w_in=np.ascontiguousarray(np.asarray(w_in, np.float32)[0]),
                  w_out_a=np.ascontiguousarray(np.asarray(w_out_a, np.float32)[0]),
                  w_out_b=np.ascontiguousarray(np.asarray(w_out_b, np.float32)[0]),
                  w_out=np.ascontiguousarray(np.asarray(w_out, np.float32)[0]), **consts)
    in_maps = [dict(x=np.ascontiguousarray(x[i]), **shared) for i in range(B)]
    res = run_bass_kernel_spmd(nc, in_maps, core_ids=list(range(B)))
    return np.stack([np.asarray(r["y"], np.float32) for r in res.results], axis=0)
```

```python
import math
import numpy as np
import concourse.bass as bass
import concourse.mybir as mybir
from concourse.bass_utils import run_bass_kernel_spmd

F32 = mybir.dt.float32
BF16 = mybir.dt.bfloat16
AF = mybir.ActivationFunctionType
ALU = mybir.AluOpType
AX = mybir.AxisListType

D = 1024
DIN = 5376
EPS = 1e-6
NMETA = 16
ENGS = ("sp", "act", "dve", "pool", "pe")
C_QA, C_KA, C_VA, C_ZA, C_QB, C_KB, C_VB, C_ZB, C_GA, C_GB = 0, 512, 1024, 1536, 2048, 2560, 2688, 2816, 3328, 4352
LAM_INIT = 0.8 - 0.6 * math.exp(-0.3 * 0)


class Sem:
    def __init__(self, nc, name, group=False):
        self.h = nc.alloc_semaphore(name)
        self.name = name
        self.v = 0
        self.group = group


class Op:
    __slots__ = ("eng", "fn", "deps", "dsem", "sem", "val", "need", "idx")

    def __init__(self, eng, fn, dsem):
        self.eng, self.fn, self.dsem = eng, fn, dsem
        self.deps = []
        self.sem = None
        self.val = 0
        self.need = False


def I(method, *args, **kw):
    return (method, args, kw)


class Buf:
    __slots__ = ("w", "r", "name")

    def __init__(self, name=""):
        self.name = name
        self.w = []
        self.r = {}


class Prog:
    def __init__(self, nc):
        self.nc = nc
        self.ops = {e: [] for e in ENGS}
        self.esem = {e: Sem(nc, "es_" + e) for e in ENGS}
        self.bufs = []
        self.dma_since_barrier = []
        self.marks = []

    def buf(self, name=""):
        b = Buf(name)
        self.bufs.append(b)
        return b

    def bufs_n(self, n, name=""):
        return [self.buf(name + str(i)) for i in range(n)]

    def op(self, eng, fn, reads=(), writes=(), dsem=None, after=()):
        o = Op(eng, fn, dsem)
        deps = {}
        for b in reads:
            for d in b.w:
                deps[id(d)] = d
        for b in writes:
            for d in b.w:
                deps[id(d)] = d
            for d in b.r.values():
                deps[id(d)] = d
        for d in after:
            deps[id(d)] = d
        for d in deps.values():
            if d.eng == "pe" and eng == "pe" and d.dsem is None:
                continue
            if d is o:
                continue
            o.deps.append(d)
            d.need = True
        for b in writes:
            b.w = [o]
            b.r = {}
        for b in reads:
            if b in writes:
                continue
            key = eng if dsem is None else ("dma", id(dsem))
            b.r[key] = o
        self.ops[eng].append(o)
        if dsem is not None:
            self.dma_since_barrier.append(o)
        return o

    def barrier(self):
        self.marks.append(sum(1 for o in self.ops["pe"] if o.fn is not None))
        lasts = []
        for e in ENGS:
            for o in reversed(self.ops[e]):
                if o.fn is not None and o.dsem is None:
                    lasts.append(o)
                    break
        deps = lasts + self.dma_since_barrier
        self.dma_since_barrier = []
        for e in ENGS:
            self.op(e, None, after=deps)
        for b in self.bufs:
            b.w = []
            b.r = {}

    def flush(self, block):
        for e in ENGS:
            for o in self.ops[e]:
                if o.fn is None:
                    continue
                if o.dsem is not None:
                    o.dsem.v += 16
                    o.sem, o.val = o.dsem, o.dsem.v
                elif o.need:
                    s = self.esem[e]
                    s.v += 1
                    o.sem, o.val = s, s.v
        engmap = {"sp": block.sync, "act": block.scalar, "dve": block.vector, "pool": block.gpsimd, "pe": block.tensor}
        for e in ENGS:
            ops = self.ops[e]

            def body(eng, ops=ops):
                waited = {}
                for o in ops:
                    for d in o.deps:
                        v = d.sem.v if d.sem.group else d.val
                        if waited.get(d.sem.name, 0) >= v:
                            continue
                        waited[d.sem.name] = v
                        eng.wait_ge(d.sem.h, v)
                    if o.fn is None:
                        continue
                    m, a, kw = o.fn
                    ins = getattr(eng, m)(*a, **kw)
                    if o.dsem is not None:
                        ins.then_inc(o.sem.h, 16)
                    elif o.need:
                        ins.then_inc(o.sem.h, 1)

            engmap[e](body)


def t5_bucket_np(rel):
    half = 16
    max_exact = 8
    ret = np.where(rel > 0, half, 0)
    n = np.abs(rel)
    nf = np.maximum(n, 1).astype(np.float32)
    large = max_exact + (np.log(nf / max_exact) / math.log(128 / max_exact) * (half - max_exact)).astype(np.int32)
    large = np.minimum(large, half - 1)
    return ret + np.where(n < max_exact, n, large)


def build(NT):
    G = NT // 4
    NB = NT + 1
    nc = bass.Bass("TRN2", target_bir_lowering=False)
    P = Prog(nc)

    def din(name, shape, dt=F32):
        return nc.dram_tensor(name, shape, dt, kind="ExternalInput").ap()

    x_d = din("x", [NT * 128, D])
    meta_d = din("meta_pad", [128, D])
    win_d = din("w_in", [D, DIN])
    woa_d = din("w_out_a", [512, D])
    wob_d = din("w_out_b", [512, D])
    wo_d = din("w_out", [D, D])
    gcol_d = din("gcol", [128, 8])
    bcol_d = din("bcol", [128, 42])
    bqb_d = din("bqb", [128, 4])
    brow_d = din("brow", [1, 640])
    strips_d = din("strips", [12, 128, 1152])
    cfar_d = din("cfar", [128, 24])
    maskb_d = din("maskb", [128, 384])
    lamv_d = din("lamv", [128, 256])
    gsub_d = din("gsub", [128, 1])
    sink_d = din("sinkr", [128, 8])
    postg_d = din("postg", [128, D])
    ident_d = din("ident", [128, 128])
    y_d = nc.dram_tensor("y", [NT * 128, D], F32, kind="ExternalOutput").ap()
    wscr = nc.dram_tensor("wscr", [8, 128, DIN], BF16, kind="Internal").ap()
    woascr = nc.dram_tensor("woascr", [4, 128, D], BF16, kind="Internal").ap()
    wobscr = nc.dram_tensor("wobscr", [4, 128, D], BF16, kind="Internal").ap()
    woscr = nc.dram_tensor("woscr", [8, 128, D], BF16, kind="Internal").ap()
    hscr = nc.dram_tensor("hscr", [G + 1, 128, 4096], BF16, kind="Internal").ap()

    KVW = 2 * NB * 128 + NB * 258 + 64
    KVW += KVW % 2
    KVW = max(KVW, NB * 128 + NB * 130 + 2 + 2 * 8300)
    kv_t = nc.alloc_sbuf_tensor("kvarea", [128, KVW], BF16)
    oat_t = nc.alloc_sbuf_tensor("oat", [128, 4, NT * 128], BF16)
    xb_t = nc.alloc_sbuf_tensor("xb", [128, 2 * D], F32)
    xs_t = nc.alloc_sbuf_tensor("xs", [128, 2, D], BF16)
    hn_t = nc.alloc_sbuf_tensor("hnT", [128, 8, 512], BF16)
    cst_t = nc.alloc_sbuf_tensor("cst", [128, 512], F32)
    cbf_t = nc.alloc_sbuf_tensor("cbf", [128, 1024], BF16)
    ARENA_F32 = 12 * 1024 + 512
    ar_t = nc.alloc_sbuf_tensor("arena", [128, ARENA_F32], F32)
    ps_t = nc.alloc_psum_tensor("psf", [128, 7 * 512], F32)
    pst_t = nc.alloc_psum_tensor("pst", [128, 1024], BF16)

    class Arena:
        def __init__(self, ap_f32_2d, nwords):
            self.t = ap_f32_2d
            self.n = nwords
            self.pos = 0

        def reset(self):
            self.pos = 0

        def f32(self, n):
            a = self.t[:, self.pos:self.pos + n]
            self.pos += n
            assert self.pos <= self.n, ("arena overflow", self.pos, self.n)
            return a

        def bf16(self, n):
            n2 = (n + 1) // 2
            return self.f32(n2).bitcast(BF16)[:, 0:n]

    arena = Arena(ar_t[:, :], ARENA_F32)

    o_ = [0]

    def cslot(n):
        a = cst_t[:, o_[0]:o_[0] + n]
        o_[0] += n
        return a

    gcol = cslot(8)
    bcol = cslot(42)
    bhalf = cslot(42)
    bqb = cslot(4)
    cfar = cslot(24)
    zcol = cslot(1)
    gsub8 = cslot(1)
    esink = cslot(8)
    lamt = cslot(8)
    neglam = cslot(1)
    ssq = cslot(2)
    rstd = cslot(2)
    nhalf = cslot(4)
    assert o_[0] <= 512
    ident = cbf_t[:, 0:128]
    ones_row = cbf_t[0:1, 128:256]
    brow = cbf_t[0:1, 256:896]

    B_cst = P.buf("cst")
    xb = P.bufs_n(2, "xb")
    xsb = P.bufs_n(2, "xs")
    xb_sem = [Sem(nc, "xbs%d" % i) for i in range(2)]
    xs_sem = [Sem(nc, "xss%d" % i) for i in range(2)]
    B_hn = P.bufs_n(4, "hn")
    B_hn2 = P.bufs_n(4, "hn2")
    hn_views = [hn_t[:, :, :], xb_t[:, :].bitcast(BF16).rearrange("p (k n) -> p k n", k=8)]
    B_hnsets = [B_hn, B_hn2]
    hn_sem = [Sem(nc, "hns0"), Sem(nc, "hns1")]
    hst_sem = Sem(nc, "hst")
    bst_sem = Sem(nc, "bst")
    B_ssq = P.bufs_n(2, "ssq")
    B_rstd = P.bufs_n(2, "rstd")
    psb = P.bufs_n(7, "ps")
    B_pst = P.buf("pst")
    B_kt = P.buf("kt")
    B_va = P.buf("va")
    B_oat = P.buf("oat")
    setup_sem = Sem(nc, "setup", group=True)
    ld_sem = Sem(nc, "ld")
    ld_sem.group = False

    def psbank(b, n=512):
        return ps_t[:, b * 512:b * 512 + n]

    arena.reset()
    stage_small = arena.f32(1152)
    B_stage_small = P.buf("stage_small")
    lamv = arena.f32(256)
    sinkr = arena.f32(8)
    gsubr = arena.f32(1)
    identf = arena.f32(128)
    browf = arena.f32(640)

    setup_loads = []
    for (o, i) in [(gcol, gcol_d[:, :]), (bcol, bcol_d[:, :]), (bqb, bqb_d[:, :]), (cfar, cfar_d[:, :]),
                   (lamv, lamv_d[:, :]), (sinkr, sink_d[:, :]), (gsubr, gsub_d[:, :]), (identf, ident_d[:, :]),
                   (browf[0:1, :], brow_d[:, :])]:
        op = Op("sp", (I("dma_start", out=o, in_=i)), setup_sem)
        P.ops["sp"].append(op)
        P.dma_since_barrier.append(op)
        setup_loads.append(op)

    def cop(eng, fn):
        return P.op(eng, fn, reads=[], writes=[B_cst], after=setup_loads)

    cop("dve", I("memset", zcol, 0.0))
    cop("dve", I("memset", nhalf, -0.5))
    cop("dve", I("tensor_scalar", out=bhalf, in0=bcol, scalar1=0.5, scalar2=None, op0=ALU.mult))
    cop("dve", I("tensor_scalar", out=gsub8, in0=gsubr, scalar1=(1.0 - LAM_INIT), scalar2=None, op0=ALU.mult))
    cop("dve", I("tensor_copy", out=ident, in_=identf))
    cop("dve", I("memset", ones_row, 1.0))
    cop("dve", I("tensor_copy", out=brow, in_=browf[0:1, :]))
    prod = arena.f32(128)
    cop("dve", I("tensor_tensor", out=prod[:, 0:64], in0=lamv[:, 0:64], in1=lamv[:, 64:128], op=ALU.mult))
    cop("dve", I("tensor_tensor", out=prod[:, 64:128], in0=lamv[:, 128:192], in1=lamv[:, 192:256], op=ALU.mult))
    cop("dve", I("reduce_sum", out=lamt[:, 0:1], in_=prod[:, 0:64], axis=AX.X))
    cop("dve", I("reduce_sum", out=lamt[:, 1:2], in_=prod[:, 64:128], axis=AX.X))
    cop("act", I("activation", out=lamt[:, 2:4], in_=lamt[:, 0:2], func=AF.Exp))
    cop("act", I("activation", out=esink, in_=sinkr, func=AF.Exp))
    cop("dve", I("tensor_tensor", out=lamt[:, 4:5], in0=lamt[:, 3:4], in1=lamt[:, 2:3], op=ALU.subtract))
    cop("dve", I("tensor_scalar", out=neglam, in0=lamt[:, 4:5], scalar1=-LAM_INIT, scalar2=None, op0=ALU.add))

    cvt_i = [0]

    def convert(src_ap, dst_ap, n, scale_ap=None, scale_f=None, perm_qb=False):
        s = cvt_i[0] % 2
        cvt_i[0] += 1
        st = xb_t[:, s * D:s * D + n]
        ot = xs_t[:, s, 0:n]
        P.op("sp", I("dma_start", out=st, in_=src_ap), writes=[xb[s]], dsem=xb_sem[s])
        if perm_qb:
            src_v = xb_t[:, s * D:s * D + 512].rearrange("p (s j d) -> p s j d", s=2, j=4, d=64)
            dst_v = xs_t[:, s, 0:512].rearrange("p (j s d) -> p s j d", s=2, j=4, d=64)
            P.op("dve", I("tensor_scalar", out=dst_v, in0=src_v, scalar1=scale_ap, scalar2=None, op0=ALU.mult),
                 reads=[xb[s], B_cst], writes=[xsb[s]])
            st2 = xb_t[:, s * D + 512:s * D + n]
            ot2 = xs_t[:, s, 512:n]
            o2 = P.op("dve", I("tensor_scalar", out=ot2, in0=st2, scalar1=scale_ap, scalar2=None, op0=ALU.mult),
                      reads=[xb[s], B_cst], writes=[])
            P.op("pool", I("dma_start", out=dst_ap, in_=ot), reads=[xsb[s]], dsem=xs_sem[s], after=[o2])
            xsb[s].r[("dve2")] = o2
        else:
            sc = scale_ap if scale_ap is not None else scale_f
            P.op("dve", I("tensor_scalar", out=ot, in0=st, scalar1=sc, scalar2=None, op0=ALU.mult),
                 reads=[xb[s], B_cst], writes=[xsb[s]])
            P.op("pool", I("dma_start", out=dst_ap, in_=ot), reads=[xsb[s]], dsem=xs_sem[s])

    for k in range(8):
        for cb in range(6):
            c0 = cb * 1024
            n = min(1024, DIN - c0)
            convert(win_d[k * 128:(k + 1) * 128, c0:c0 + n], wscr[k, :, c0:c0 + n], n,
                    scale_ap=gcol[:, k:k + 1], perm_qb=(cb == 2))
    for k in range(4):
        convert(woa_d[k * 128:(k + 1) * 128, :], woascr[k, :, :], 1024, scale_f=0.5)
    for k in range(4):
        convert(wob_d[k * 128:(k + 1) * 128, :], wobscr[k, :, :], 1024, scale_f=0.5)
    for k in range(8):
        convert(wo_d[k * 128:(k + 1) * 128, :], woscr[k, :, :], 1024, scale_f=0.5)
    P.barrier()

    xcnt = [0]

    def prep_tile(src_ap, col):
        s = xcnt[0] % 2
        xcnt[0] += 1
        xt = xb_t[:, s * D:(s + 1) * D]
        xo = xs_t[:, s, :]
        P.op("sp", I("dma_start", out=xt, in_=src_ap), writes=[xb[s]], dsem=xb_sem[s])
        P.op("act", I("activation", out=xo, in_=xt, func=AF.Square, accum_out=ssq[:, s:s + 1]),
             reads=[xb[s]], writes=[xsb[s], B_ssq[s]])
        P.op("dve", I("tensor_scalar", out=rstd[:, s:s + 1], in0=ssq[:, s:s + 1], scalar1=1.0 / D, scalar2=EPS,
                                              op0=ALU.mult, op1=ALU.add), reads=[B_ssq[s]], writes=[B_rstd[s]])
        P.op("pool", I("tensor_tensor", out=rstd[:, s:s + 1], in0=rstd[:, s:s + 1], in1=nhalf[:, 0:1], op=ALU.pow),
             reads=[B_rstd[s], B_cst], writes=[B_rstd[s]])
        P.op("dve", I("tensor_scalar", out=xo, in0=xt, scalar1=rstd[:, s:s + 1], scalar2=None, op0=ALU.mult),
             reads=[xb[s], B_rstd[s]], writes=[xsb[s]])
        for k in range(8):
            P.op("pe", I("transpose", out=pst_t[:, k * 128:(k + 1) * 128], in_=xs_t[:, s, k * 128:(k + 1) * 128],
                                                  identity=ident),
                 reads=[xsb[s]] if k else [xsb[s], B_cst], writes=[B_pst] if k == 0 else [])
        lastT = P.ops["pe"][-1]
        B_pst.w = [lastT]
        xsb[s].r["pe"] = lastT
        P.op("dve", I("tensor_copy", out=hn_t[:, :, col * 128:(col + 1) * 128],
                                            in_=pst_t[:, :].rearrange("p (k t) -> p k t", k=8)),
             reads=[B_pst], writes=[B_hn[col]])

    def tile_src(c):
        return meta_d[:, :] if c == 0 else x_d[(c - 1) * 128:c * 128, :]

    def groups_kv():
        yield [0]
        for g in range(G):
            yield [1 + 4 * g + j for j in range(4)]

    glist = list(groups_kv())

    def hn_store(gi, nt):
        P.op("pool", I("dma_start", out=hscr[gi].rearrange("p (k n) -> p k n", k=8)[:, :, 0:nt * 128], in_=hn_t[:, :, 0:nt * 128]),
             reads=B_hn[0:nt], dsem=hst_sem)

    def hn_load(gi, nt, which):
        P.op("sp", I("dma_start", out=hn_views[which][:, :, 0:nt * 128],
                     in_=hscr[gi].rearrange("p (k n) -> p k n", k=8)[:, :, 0:nt * 128]),
             writes=B_hnsets[which], dsem=hn_sem[which])
        return hn_views[which], B_hnsets[which]

    def hn_iter(first):
        if first:
            for gi, blocks in enumerate(glist):
                for j, c in enumerate(blocks):
                    prep_tile(tile_src(c), j)
                hn_store(gi, len(blocks))
                yield blocks, hn_views[0], B_hn
        else:
            nxt = hn_load(0, len(glist[0]), 0)
            for gi, blocks in enumerate(glist):
                cur = nxt
                if gi + 1 < len(glist):
                    nxt = hn_load(gi + 1, len(glist[gi + 1]), (gi + 1) % 2)
                yield blocks, cur[0], cur[1]

    def load_w(dst_ap, src_ap, buf, sem):
        return P.op("sp", I("dma_start", out=dst_ap, in_=src_ap), writes=[buf], dsem=sem)

    wsem = [Sem(nc, "wsem%d" % i) for i in range(8)]

    for p in range(2):
        arena.reset()
        KT = kv_t[:, 0:2 * NB * 128].rearrange("p (h n) -> p h n", h=2)
        VA = kv_t[:, 2 * NB * 128:2 * NB * 128 + NB * 258].rearrange("p (c n) -> p c n", n=258)
        wk = arena.bf16(8 * 256).rearrange("p (k n) -> p k n", k=8)
        wv = arena.bf16(8 * 256).rearrange("p (k n) -> p k n", k=8)
        B_wk, B_wv = P.buf("wk"), P.buf("wv")
        load_w(wk, wscr[:, :, C_KA + p * 256:C_KA + (p + 1) * 256].rearrange("k p n -> p k n"), B_wk, wsem[0])
        load_w(wv, wscr[:, :, C_VA + p * 256:C_VA + (p + 1) * 256].rearrange("k p n -> p k n"), B_wv, wsem[1])
        B_vcol = P.buf("vcol")
        P.op("pool", I("memset", VA[:, :, 128:129], 1.0), writes=[B_vcol])
        P.op("pool", I("memset", VA[:, :, 257:258], 1.0), writes=[B_vcol])
        B_ktc = {}
        B_vac = {}
        for blocks, HN, BHN in hn_iter(p == 0):
            nt = len(blocks)
            c0 = blocks[0]
            for hh in range(2):
                bank = hh
                for k in range(8):
                    P.op("pe", I("matmul",
                        psbank(bank, nt * 128), lhsT=wk[:, k, hh * 128:(hh + 1) * 128], rhs=HN[:, k, 0:nt * 128],
                        start=(k == 0), stop=(k == 7)),
                        reads=([B_wk] + BHN[0:nt]) if k == 0 else [], writes=[psb[bank]] if k == 0 else [])
                last = P.ops["pe"][-1]
                psb[bank].w = [last]
                for b in BHN[0:nt]:
                    b.r["pe"] = last
                kb = P.buf("ktc")
                B_ktc[(blocks[0], hh)] = kb
                ch = (C_KA + p * 256) // 128 + hh
                P.op("dve", I("tensor_scalar",
                    out=KT[:, hh, c0 * 128:(c0 + nt) * 128], in0=psbank(bank, nt * 128), scalar1=bcol[:, ch:ch + 1],
                    scalar2=None, op0=ALU.add), reads=[psb[bank], B_cst], writes=[kb])
            for j, c in enumerate(blocks):
                bank = 2 + (j % 4)
                for k in range(8):
                    P.op("pe", I("matmul",
                        psbank(bank, 256), lhsT=HN[:, k, j * 128:(j + 1) * 128], rhs=wv[:, k, :],
                        start=(k == 0), stop=False),
                        reads=([B_wv, BHN[j]]) if k == 0 else [], writes=[psb[bank]] if k == 0 else [])
                P.op("pe", I("matmul",
                    psbank(bank, 256), lhsT=ones_row, rhs=brow[0:1, p * 256:(p + 1) * 256], start=False, stop=True),
                    reads=[B_cst])
                last = P.ops["pe"][-1]
                psb[bank].w = [last]
                BHN[j].r["pe"] = last
                vb = P.buf("vac")
                B_vac[c] = vb
                P.op("act", I("activation",
                    out=VA[:, c, :].rearrange("p (h n) -> p h n", h=2)[:, :, 0:128],
                    in_=psbank(bank, 256).rearrange("p (h n) -> p h n", h=2), func=AF.Copy),
                    reads=[psb[bank]], writes=[vb])
                if c == 0:
                    P.op("pool", I("memset", VA[0:128 - NMETA, 0, :], 0.0), reads=[], writes=[vb], after=list(B_vcol.w))
        P.barrier()

        arena.reset()
        wq = arena.bf16(8 * 256).rearrange("p (k n) -> p k n", k=8)
        strips = arena.bf16(2 * 1152).rearrange("p (h n) -> p h n", h=2)
        QT = arena.bf16(2 * 2 * 512).rearrange("p (s h n) -> p s h n", s=2, h=2)
        ET = arena.bf16(3 * 1024).rearrange("p (s n) -> p s n", s=3)
        accs = arena.f32(8 * 129).rearrange("p (a n) -> p a n", a=8)
        ofin = arena.f32(4 * 128).rearrange("p (j n) -> p j n", j=4)
        t1 = arena.f32(128)
        sq = arena.f32(128)
        onb = arena.bf16(4 * 128).rearrange("p (j n) -> p j n", j=4)
        sm = arena.f32(32)
        sstage = arena.f32(1152)
        B_wq, B_strips = P.buf("wq"), P.buf("strips")
        B_QT = P.bufs_n(2, "QT")
        B_ET = P.bufs_n(3, "ET")
        B_accs, B_fin, B_on, B_sstage = P.buf("accs"), P.buf("fin"), P.buf("on"), P.buf("sstage")
        load_w(wq, wscr[:, :, C_QA + p * 256:C_QA + (p + 1) * 256].rearrange("k p n -> p k n"), B_wq, wsem[0])
        for hh in range(2):
            P.op("sp", I("dma_start", out=sstage, in_=strips_d[2 * p + hh, :, :]), writes=[B_sstage], dsem=wsem[2])
            P.op("dve", I("tensor_copy", out=strips[:, hh, :], in_=sstage), reads=[B_sstage], writes=[B_strips])
        ucnt = [0]
        ecnt = [0]
        pstf = pst_t[:, :].bitcast(F32)
        deferred = []

        def defer(n, tag, fn):
            deferred.append([n, tag, fn])

        def tick():
            for d in deferred:
                d[0] -= 1
            due = [d for d in deferred if d[0] <= 0]
            for d in due:
                deferred.remove(d)
                d[2]()

        def flush_deferred(tag=None):
            due = [d for d in deferred if tag is None or d[1] == tag]
            for d in due:
                deferred.remove(d)
                d[2]()

        def qproj_mm(hh, k0, k1, HNq, BHNq):
            for k in range(k0, k1):
                P.op("pe", I("matmul", pstf, lhsT=wq[:, k, hh * 128:(hh + 1) * 128], rhs=HNq[:, k, :],
                             start=(k == 0), stop=(k == 7)),
                     reads=([B_wq] + BHNq) if k == 0 else [], writes=[B_pst] if k == 0 else [])
            if k1 == 8:
                last = P.ops["pe"][-1]
                B_pst.w = [last]
                for b in BHNq:
                    b.r["pe"] = last

        def qproj_evac(hh, qsl):
            ch = (C_QA + p * 256) // 128 + hh
            P.op("dve", I("tensor_scalar", out=QT[:, qsl, hh, :], in0=pstf, scalar1=bcol[:, ch:ch + 1], scalar2=0.125,
                          op0=ALU.add, op1=ALU.mult), reads=[B_pst, B_cst], writes=[B_QT[qsl]] if hh == 0 else [])
            if hh == 1:
                B_QT[qsl].w.append(P.ops["dve"][-1])

        def qproj_schedule(gq, HNq, BHNq, now):
            qsl = gq % 2
            t = 22
            for hh in range(2):
                for k0 in (0, 2, 4, 6):
                    fn = (lambda hh=hh, k0=k0: qproj_mm(hh, k0, k0 + 2, HNq, BHNq))
                    if now:
                        fn()
                    else:
                        defer(t, "q", fn)
                    t += 1
                fn = (lambda hh=hh: qproj_evac(hh, qsl))
                if now:
                    fn()
                else:
                    defer(t, "q", fn)
                t += 2

        def B_ktc_get(c, hh):
            for (c0, h2), b in B_ktc.items():
                if h2 == hh and (c0 == c or (c0 >= 1 and c0 <= c < c0 + 4 and c >= 1)):
                    return b
            raise KeyError((c, hh))

        def emit_scores(c, g, hh, h, qs, b0):
            slot = ucnt[0] % 2
            ucnt[0] += 1
            bk = 2 * slot
            Dd = c - b0
            mixed = (-1 <= Dd <= 4)
            P.op("pe", I("matmul", psbank(bk), lhsT=KT[0:64, hh, c * 128:(c + 1) * 128],
                         rhs=QT[0:64, qs, hh, :], start=True, stop=not mixed),
                 reads=[B_ktc_get(c, hh), B_QT[qs]], writes=[psb[bk], psb[bk + 1]])
            P.op("pe", I("matmul", psbank(bk + 1), lhsT=KT[64:128, hh, c * 128:(c + 1) * 128],
                         rhs=QT[64:128, qs, hh, :], start=True, stop=not mixed))
            if mixed:
                o0 = (4 - Dd) * 128
                P.op("pe", I("matmul", psbank(bk), lhsT=ident, rhs=strips[:, hh, o0:o0 + 512],
                             start=False, stop=True), reads=[B_strips, B_cst])
                P.op("pe", I("matmul", psbank(bk + 1), lhsT=ident, rhs=strips[:, hh, o0:o0 + 512],
                             start=False, stop=True))
            last = P.ops["pe"][-1]
            psb[bk].w = [last]
            psb[bk + 1].w = [last]
            B_QT[qs].r["pe"] = last
            es = ecnt[0] % 3
            ecnt[0] += 1
            if mixed:
                bias_ap = zcol
            elif Dd <= -2:
                bias_ap = cfar[:, h:h + 1]
            else:
                bias_ap = cfar[:, 12 + h:13 + h]
            P.op("act", I("activation", out=ET[:, es, :], in_=ps_t[:, bk * 512:bk * 512 + 1024], func=AF.Exp,
                          bias=bias_ap, scale=1.0),
                 reads=[psb[bk], psb[bk + 1], B_cst], writes=[B_ET[es]])
            return (c, es)

        def emit_av(c, es, hh):
            first = (c == 0)
            lastc = (c == NB - 1)
            for j in range(4):
                for m in range(2):
                    a = 2 * j + m
                    bank = 4 + a // 3
                    off = (a % 3) * 129
                    P.op("pe", I("matmul", ps_t[:, bank * 512 + off:bank * 512 + off + 129],
                                 lhsT=ET[:, es, m * 512 + j * 128:m * 512 + (j + 1) * 128],
                                 rhs=VA[:, c, hh * 129:hh * 129 + 129], start=(first and a % 3 == 0), stop=lastc,
                                 skip_group_check=True),
                         reads=([B_ET[es], B_vac[c]]) if a == 0 else [],
                         writes=[psb[4], psb[5], psb[6]] if (first and a == 0) else [])
            last = P.ops["pe"][-1]
            B_ET[es].r["pe"] = last
            if lastc:
                for b in (psb[4], psb[5], psb[6]):
                    b.w = [last]

        def finalize_part1():
            for bi, na in ((4, 3), (5, 3), (6, 2)):
                a0 = (bi - 4) * 3
                P.op("dve", I("tensor_copy", out=accs[:, a0:a0 + na, :],
                              in_=psbank(bi, na * 129).rearrange("p (a n) -> p a n", a=na)),
                     reads=[psb[bi]], writes=[B_accs] if bi == 4 else [])
                if bi != 4:
                    B_accs.w.append(P.ops["dve"][-1])

            def fin(fn):
                return P.op("dve", fn, reads=[B_accs, B_fin, B_cst], writes=[B_fin])

            fin(I("reciprocal", out=sm[:, 0:8], in_=accs[:, :, 128:129].rearrange("p a o -> p (a o)")))
            fin(I("tensor_scalar", out=sm[:, 8:12], in0=sm[:, 0:8].rearrange("p (j m) -> p j m", m=2)[:, :, 1],
                  scalar1=neglam, scalar2=None, op0=ALU.mult))
            for j in range(4):
                fin(I("tensor_scalar", out=t1, in0=accs[:, 2 * j + 1, 0:128], scalar1=sm[:, 8 + j:9 + j],
                      scalar2=None, op0=ALU.mult))
                fin(I("scalar_tensor_tensor", out=ofin[:, j, :], in0=accs[:, 2 * j, 0:128],
                      scalar=sm[:, 2 * j:2 * j + 1], in1=t1, op0=ALU.mult, op1=ALU.add))
                fin(I("tensor_tensor", out=sq, in0=ofin[:, j, :], in1=ofin[:, j, :], op=ALU.mult))
                fin(I("reduce_sum", out=sm[:, 12 + j:13 + j], in_=sq, axis=AX.X))
            fin(I("tensor_scalar", out=sm[:, 16:20], in0=sm[:, 12:16], scalar1=1.0 / 128, scalar2=EPS,
                  op0=ALU.mult, op1=ALU.add))
            P.op("pool", I("tensor_tensor", out=sm[:, 16:20], in0=sm[:, 16:20], in1=nhalf[:, 0:4], op=ALU.pow),
                 reads=[B_fin, B_cst], writes=[B_fin])
            for j in range(4):
                P.op("dve", I("tensor_scalar", out=onb[:, j, :], in0=ofin[:, j, :], scalar1=sm[:, 16 + j:17 + j],
                              scalar2=None, op0=ALU.mult),
                     reads=[B_fin], writes=[B_on] if j == 0 else [])
                if j:
                    B_on.w.append(P.ops["dve"][-1])

        def finalize_part2(h, g):
            for j in range(4):
                P.op("pe", I("transpose", out=pst_t[:, j * 128:(j + 1) * 128], in_=onb[:, j, :], identity=ident),
                     reads=[B_on, B_cst] if j == 0 else [], writes=[B_pst] if j == 0 else [])
            lastT = P.ops["pe"][-1]
            B_pst.w = [lastT]
            B_on.r["pe"] = lastT
            P.op("dve", I("tensor_scalar", out=oat_t[:, h, g * 512:(g + 1) * 512], in0=pst_t[:, 0:512],
                          scalar1=gsub8, scalar2=None, op0=ALU.mult),
                 reads=[B_pst, B_cst], writes=[])

        nxt = hn_load(1, 4, 0)
        qproj_schedule(0, nxt[0], nxt[1], now=True)
        for g in range(G):
            qs = g % 2
            flush_deferred("q")
            if g + 1 < G:
                nxt = hn_load(2 + g, 4, (g + 1) % 2)
            b0 = 1 + 4 * g
            for hh in range(2):
                h = 2 * p + hh
                if hh == 1 and g + 1 < G:
                    qproj_schedule(g + 1, nxt[0], nxt[1], now=False)
                pend = [emit_scores(0, g, hh, h, qs, b0)]
                if NB > 1:
                    pend.append(emit_scores(1, g, hh, h, qs, b0))
                for c in range(2, NB):
                    cur = emit_scores(c, g, hh, h, qs, b0)
                    pv = pend.pop(0)
                    emit_av(pv[0], pv[1], hh)
                    tick()
                    pend.append(cur)
                while pend:
                    pv = pend.pop(0)
                    emit_av(pv[0], pv[1], hh)
                    tick()
                flush_deferred("f")
                finalize_part1()
                defer(20, "f", (lambda h=h, g=g: finalize_part2(h, g)))
        flush_deferred()
        P.barrier()

    arena.reset()
    KTB = kv_t[:, 0:NB * 128]
    VB = kv_t[:, NB * 128:NB * 128 + NB * 130].rearrange("p (c n) -> p c n", n=130)
    kvfree0 = NB * 128 + NB * 130
    kvfree0 += kvfree0 % 2
    wkb = arena.bf16(8 * 128).rearrange("p (k n) -> p k n", k=8)
    wvb = arena.bf16(8 * 128).rearrange("p (k n) -> p k n", k=8)
    B_wk, B_wv = P.buf("wkb"), P.buf("wvb")
    load_w(wkb, wscr[:, :, C_KB:C_KB + 128].rearrange("k p n -> p k n"), B_wk, wsem[0])
    load_w(wvb, wscr[:, :, C_VB:C_VB + 128].rearrange("k p n -> p k n"), B_wv, wsem[1])
    B_vcol = P.buf("vcolb")
    P.op("pool", I("memset", VB[:, :, 64:65], 1.0), writes=[B_vcol])
    P.op("pool", I("memset", VB[:, :, 129:130], 1.0), writes=[B_vcol])
    for blocks, HN, BHN in hn_iter(False):
        nt = len(blocks)
        c0 = blocks[0]
        for k in range(8):
            P.op("pe", I("matmul", psbank(0, nt * 128), lhsT=wkb[:, k, :], rhs=HN[:, k, 0:nt * 128],
                                               start=(k == 0), stop=(k == 7)),
                 reads=([B_wk] + BHN[0:nt]) if k == 0 else [], writes=[psb[0]] if k == 0 else [])
        last = P.ops["pe"][-1]
        psb[0].w = [last]
        for b in BHN[0:nt]:
            b.r["pe"] = last
        P.op("dve", I("tensor_scalar",
            out=KTB[:, c0 * 128:(c0 + nt) * 128], in0=psbank(0, nt * 128), scalar1=bcol[:, C_KB // 128:C_KB // 128 + 1],
            scalar2=None, op0=ALU.add), reads=[psb[0], B_cst], writes=[])
        psb[0].r["dve"] = P.ops["dve"][-1]
        for j, c in enumerate(blocks):
            bank = 2 + (j % 4)
            for k in range(8):
                P.op("pe", I("matmul",
                    psbank(bank, 128), lhsT=HN[:, k, j * 128:(j + 1) * 128], rhs=wvb[:, k, :], start=(k == 0), stop=False),
                    reads=([B_wv, BHN[j]]) if k == 0 else [], writes=[psb[bank]] if k == 0 else [])
            P.op("pe", I("matmul", psbank(bank, 128), lhsT=ones_row, rhs=brow[0:1, 512:640],
                                                     start=False, stop=True), reads=[B_cst])
            last = P.ops["pe"][-1]
            psb[bank].w = [last]
            BHN[j].r["pe"] = last
            P.op("act", I("activation",
                out=VB[:, c, :].rearrange("p (h n) -> p h n", h=2)[:, :, 0:64],
                in_=psbank(bank, 128).rearrange("p (h n) -> p h n", h=2), func=AF.Copy),
                reads=[psb[bank]], writes=[])
            psb[bank].r["act"] = P.ops["act"][-1]
            if c == 0:
                P.op("pool", I("memset", VB[0:128 - NMETA, 0, :], 0.0), after=list(B_vcol.w) + [P.ops["act"][-1]])
    P.barrier()

    arena.reset()
    ar2 = Arena(kv_t[:, kvfree0:KVW].bitcast(F32), (KVW - kvfree0) // 2)
    biasB = arena.bf16(8 * 384).rearrange("p (h n) -> p h n", h=8)
    Woa = arena.bf16(4 * 1024).rearrange("p (k n) -> p k n", k=4)
    Wob = arena.bf16(4 * 1024).rearrange("p (k n) -> p k n", k=4)
    Wring = [arena.bf16(4096), arena.bf16(4096)]
    postg = arena.f32(1024)
    ttile = arena.f32(512)
    sz = arena.f32(512)
    ta = arena.f32(512)
    EBs = [ar2.bf16(1536), ar2.bf16(1536)]
    mixedT = ar2.bf16(8 * 512).rearrange("p (k n) -> p k n", k=8)
    goaT = ar2.bf16(4 * 512).rearrange("p (k n) -> p k n", k=4)
    gobT = ar2.bf16(4 * 512).rearrange("p (k n) -> p k n", k=4)
    QTb = ar2.bf16(4 * 512).rearrange("p (k n) -> p k n", k=4)
    obt = ar2.bf16(4 * 512).rearrange("p (t n) -> p t n", t=4)
    tb = ar2.f32(512)
    smf = ar2.f32(16)
    smf2 = ar2.f32(8)
    bstage = ttile[:, 0:384]
    maskb = sz[:, 0:384]
    pstf = pst_t[:, :].bitcast(F32)
    B_biasB, B_Woa, B_Wob, B_postg = P.buf("biasB"), P.buf("Woa"), P.buf("Wob"), P.buf("postg")
    B_ring = P.bufs_n(2, "ring")
    ring_sem = [wsem[3], wsem[4]]
    B_t, B_sz, B_mixed, B_goa, B_gob, B_QTb = (P.buf("t"), P.buf("sz"), P.buf("mixed"), P.buf("goa"), P.buf("gob"), P.buf("QTb"))
    B_ob = P.bufs_n(4, "ob")
    B_EB = P.bufs_n(2, "EB")
    B_ta, B_tb = P.buf("ta"), P.buf("tb")
    B_smfs = P.bufs_n(2, "smf")
    B_smf2 = P.bufs_n(2, "smf2")
    out_sem = [wsem[5], wsem[6]]
    load_w(Woa, woascr.rearrange("k p n -> p k n"), B_Woa, wsem[0])
    load_w(Wob, wobscr.rearrange("k p n -> p k n"), B_Wob, wsem[1])
    load_w(postg, postg_d[:, :], B_postg, wsem[2])
    load_w(maskb, maskb_d[:, :], B_sz, wsem[7])
    for h8 in range(8):
        P.op("sp", I("dma_start", out=bstage, in_=strips_d[4 + h8, :, 384:768]), writes=[B_t], dsem=bst_sem)
        P.op("dve", I("tensor_tensor", out=biasB[:, h8, :], in0=bstage, in1=maskb, op=ALU.add),
             reads=[B_t, B_sz], writes=[B_biasB])
    rcnt = [0]

    def ring_load(src_ap):
        s = rcnt[0] % 2
        rcnt[0] += 1
        load_w(Wring[s].rearrange("p (k n) -> p k n", k=8), src_ap, B_ring[s], ring_sem[s])
        return s

    def proj_fm(wblk, s, col0, bank):
        for k in range(8):
            P.op("pe", I("matmul", psbank(bank), lhsT=wblk[:, k, col0:col0 + 128], rhs=hn_t[:, k, :],
                         start=(k == 0), stop=(k == 7)),
                 reads=([B_ring[s]] + B_hn) if k == 0 else [], writes=[psb[bank]] if k == 0 else [])
        last = P.ops["pe"][-1]
        psb[bank].w = [last]
        B_ring[s].r["pe"] = last
        for b in B_hn:
            b.r["pe"] = last

    ebc = [0]
    fcnt = [0]
    ACCS = [(ps_t[:, 6 * 512:6 * 512 + 260], psb[6]), (pstf[:, 0:260], B_pst)]
    for g in range(G):
        hn_load(1 + g, 4, 0)
        s = ring_load(wscr[:, :, C_QB:C_QB + 512].rearrange("k p n -> p k n"))
        wblk = Wring[s].rearrange("p (k n) -> p k n", k=8)
        for j in range(4):
            bank = j % 2
            proj_fm(wblk, s, j * 128, bank)
            P.op("dve", I("tensor_scalar", out=QTb[:, j, :], in0=psbank(bank), scalar1=bqb[:, j:j + 1],
                          scalar2=0.125, op0=ALU.add, op1=ALU.mult),
                 reads=[psb[bank], B_cst], writes=[B_QTb] if j == 0 else [])
            if j:
                B_QTb.w.append(P.ops["dve"][-1])
        for tq in range(4):
            b = 1 + 4 * g + tq
            cs = [c for c in (b - 1, b, b + 1) if 0 <= c < NB]
            jj0 = cs[0] - (b - 1)
            ncs = len(cs)
            for half in range(2):
                sl = ebc[0] % 2
                ebc[0] += 1
                r0 = half * 64
                h0 = half * 4
                banks = [3 * sl + (c - (b - 1)) for c in cs]
                for i, c in enumerate(cs):
                    jj = c - (b - 1)
                    bk = 3 * sl + jj
                    P.op("pe", I("matmul", psbank(bk), lhsT=KTB[r0:r0 + 64, c * 128:(c + 1) * 128],
                                 rhs=QTb[r0:r0 + 64, :, tq * 128:(tq + 1) * 128], start=True, stop=False),
                         reads=[B_QTb] if i == 0 else [], writes=[psb[x] for x in banks] if i == 0 else [])
                    P.op("pe", I("matmul", psbank(bk), lhsT=ident, rhs=biasB[:, h0:h0 + 4, (2 - jj) * 128:(3 - jj) * 128],
                                 start=False, stop=True), reads=[B_biasB, B_cst] if i == 0 else [])
                last = P.ops["pe"][-1]
                for x in banks:
                    psb[x].w = [last]
                B_QTb.r["pe"] = last
                lo = (3 * sl + jj0) * 512
                P.op("act", I("activation", out=EBs[sl][:, jj0 * 512:(jj0 + ncs) * 512], in_=ps_t[:, lo:lo + ncs * 512], func=AF.Exp),
                     reads=[psb[x] for x in banks], writes=[B_EB[sl]])
                acc_ap, acc_buf = ACCS[sl]
                firstmm = True
                for i4 in range(4):
                    for i, c in enumerate(cs):
                        jj = c - (b - 1)
                        P.op("pe", I("matmul", acc_ap[:, i4 * 65:(i4 + 1) * 65],
                                     lhsT=EBs[sl][:, jj * 512 + i4 * 128:jj * 512 + (i4 + 1) * 128],
                                     rhs=VB[:, c, half * 65:half * 65 + 65], start=firstmm, stop=(i == ncs - 1),
                                     skip_group_check=True),
                             reads=[B_EB[sl]] if firstmm else [], writes=[acc_buf] if firstmm else [])
                        firstmm = False
                last = P.ops["pe"][-1]
                acc_buf.w = [last]
                B_EB[sl].r["pe"] = last
                acc3 = acc_ap.rearrange("p (i n) -> p i n", n=65)
                sms = smf[:, sl * 8:sl * 8 + 8]
                P.op("dve", I("tensor_tensor", out=sms[:, 0:4], in0=acc3[:, :, 64:65].rearrange("p i o -> p (i o)"),
                              in1=esink[:, h0:h0 + 4], op=ALU.add),
                     reads=[acc_buf, B_cst], writes=[B_smfs[sl]])
                P.op("dve", I("reciprocal", out=sms[:, 4:8], in_=sms[:, 0:4]), reads=[B_smfs[sl]], writes=[B_smfs[sl]])
                for i4 in range(4):
                    P.op("dve", I("tensor_scalar", out=obt[:, tq, (h0 + i4) * 64:(h0 + i4 + 1) * 64], in0=acc3[:, i4, 0:64],
                                  scalar1=sms[:, 4 + i4:5 + i4], scalar2=None, op0=ALU.mult),
                         reads=[acc_buf, B_smfs[sl]], writes=[B_ob[tq]] if (half == 0 and i4 == 0) else [])
                    if half or i4:
                        B_ob[tq].w.append(P.ops["dve"][-1])

        def gate(col_base, which):
            s = ring_load(wscr[:, :, col_base:col_base + 512].rearrange("k p n -> p k n"))
            wblk = Wring[s].rearrange("p (k n) -> p k n", k=8)
            for c in range(4):
                bank = c % 2
                ch = col_base // 128 + c
                proj_fm(wblk, s, c * 128, bank)
                P.op("act", I("activation", out=ttile, in_=psbank(bank), func=AF.Tanh, bias=bhalf[:, ch:ch + 1], scale=0.5),
                     reads=[psb[bank], B_cst], writes=[B_t])
                P.op("dve", I("scalar_tensor_tensor", out=sz, in0=psbank(bank), scalar=bcol[:, ch:ch + 1], in1=ttile,
                              op0=ALU.add, op1=ALU.mult), reads=[psb[bank], B_t, B_cst], writes=[B_sz])
                P.op("dve", I("scalar_tensor_tensor", out=sz, in0=psbank(bank), scalar=bcol[:, ch:ch + 1], in1=sz,
                              op0=ALU.add, op1=ALU.add), reads=[psb[bank], B_cst], writes=[B_sz])
                if which == "a":
                    P.op("dve", I("tensor_tensor", out=goaT[:, c, :], in0=oat_t[:, c, g * 512:(g + 1) * 512], in1=sz, op=ALU.mult),
                         reads=[B_sz], writes=[B_goa] if c == 0 else [])
                    if c:
                        B_goa.w.append(P.ops["dve"][-1])
                else:
                    for tq in range(4):
                        P.op("pe", I("transpose", out=pst_t[:, tq * 128:(tq + 1) * 128], in_=obt[:, tq, c * 128:(c + 1) * 128],
                                     identity=ident), reads=[B_ob[tq], B_cst], writes=[B_pst] if tq == 0 else [])
                    lastT = P.ops["pe"][-1]
                    B_pst.w = [lastT]
                    for tq in range(4):
                        B_ob[tq].r["pe"] = lastT
                    P.op("dve", I("tensor_tensor", out=gobT[:, c, :], in0=pst_t[:, 0:512], in1=sz, op=ALU.mult),
                         reads=[B_pst, B_sz], writes=[B_gob] if c == 0 else [])
                    if c:
                        B_gob.w.append(P.ops["dve"][-1])

        gate(C_ZB, "b")
        gate(C_ZA, "a")
        for half in range(2):
            sa = ring_load(wscr[:, :, C_GA + half * 512:C_GA + (half + 1) * 512].rearrange("k p n -> p k n"))
            wa = Wring[sa].rearrange("p (k n) -> p k n", k=8)
            sb = None
            for c4 in range(4):
                fc = half * 4 + c4
                proj_fm(wa, sa, c4 * 128, 0)
                cha = C_GA // 128 + fc
                P.op("act", I("activation", out=ta, in_=psbank(0), func=AF.Tanh, bias=bhalf[:, cha:cha + 1], scale=0.5),
                     reads=[psb[0], B_cst], writes=[B_ta])
                if c4 == 0:
                    sb = ring_load(wscr[:, :, C_GB + half * 512:C_GB + (half + 1) * 512].rearrange("k p n -> p k n"))
                    wb = Wring[sb].rearrange("p (k n) -> p k n", k=8)
                proj_fm(wb, sb, c4 * 128, 1)
                chb = C_GB // 128 + fc
                P.op("act", I("activation", out=tb, in_=psbank(1), func=AF.Tanh, bias=bhalf[:, chb:chb + 1], scale=0.5),
                     reads=[psb[1], B_cst], writes=[B_tb])
                for k in range(4):
                    P.op("pe", I("matmul", psbank(2), lhsT=Woa[:, k, fc * 128:(fc + 1) * 128], rhs=goaT[:, k, :],
                                 start=(k == 0), stop=(k == 3)),
                         reads=[B_Woa, B_goa] if k == 0 else [], writes=[psb[2]] if k == 0 else [])
                last = P.ops["pe"][-1]
                psb[2].w = [last]
                B_goa.r["pe"] = last
                for k in range(4):
                    P.op("pe", I("matmul", psbank(3), lhsT=Wob[:, k, fc * 128:(fc + 1) * 128], rhs=gobT[:, k, :],
                                 start=(k == 0), stop=(k == 3)),
                         reads=[B_Wob, B_gob] if k == 0 else [], writes=[psb[3]] if k == 0 else [])
                last = P.ops["pe"][-1]
                psb[3].w = [last]
                B_gob.r["pe"] = last
                P.op("dve", I("scalar_tensor_tensor", out=ta, in0=ta, scalar=1.0, in1=psbank(2), op0=ALU.add, op1=ALU.mult),
                     reads=[psb[2]], writes=[B_ta])
                P.op("dve", I("scalar_tensor_tensor", out=tb, in0=tb, scalar=1.0, in1=psbank(3), op0=ALU.add, op1=ALU.mult),
                     reads=[psb[3]], writes=[B_tb])
                P.op("dve", I("tensor_tensor", out=mixedT[:, fc, :], in0=ta, in1=tb, op=ALU.add),
                     reads=[B_ta, B_tb], writes=[B_mixed] if fc == 0 else [])
                if fc:
                    B_mixed.w.append(P.ops["dve"][-1])
        so = [ring_load(woscr[:, :, hf * 512:(hf + 1) * 512].rearrange("k p n -> p k n")) for hf in range(2)]
        for tq in range(4):
            fs = fcnt[0] % 2
            fcnt[0] += 1
            fb = (4, 5) if fs == 0 else (2, 3)
            sm2 = smf2[:, fs * 4:fs * 4 + 4]
            for hf in range(2):
                wblk = Wring[so[hf]].rearrange("p (k n) -> p k n", k=8)
                bank = fb[hf]
                for k in range(8):
                    P.op("pe", I("matmul", psbank(bank), lhsT=mixedT[:, k, tq * 128:(tq + 1) * 128], rhs=wblk[:, k, :],
                                 start=(k == 0), stop=(k == 7)),
                         reads=[B_ring[so[hf]], B_mixed] if k == 0 else [], writes=[psb[bank]] if k == 0 else [])
                last = P.ops["pe"][-1]
                psb[bank].w = [last]
                B_ring[so[hf]].r["pe"] = last
                B_mixed.r["pe"] = last
            P.op("act", I("activation", out=ttile, in_=psbank(fb[0]), func=AF.Square, accum_out=sm2[:, 0:1]),
                 reads=[psb[fb[0]]], writes=[B_t, B_smf2[fs]])
            P.op("act", I("activation", out=ttile, in_=psbank(fb[1]), func=AF.Square, accum_out=sm2[:, 1:2]),
                 reads=[psb[fb[1]]], writes=[B_t, B_smf2[fs]])
            P.op("dve", I("tensor_tensor", out=sm2[:, 2:3], in0=sm2[:, 0:1], in1=sm2[:, 1:2], op=ALU.add),
                 reads=[B_smf2[fs]], writes=[B_smf2[fs]])
            P.op("dve", I("tensor_scalar", out=sm2[:, 2:3], in0=sm2[:, 2:3], scalar1=1.0 / D, scalar2=EPS, op0=ALU.mult, op1=ALU.add),
                 reads=[B_smf2[fs]], writes=[B_smf2[fs]])
            P.op("pool", I("tensor_tensor", out=sm2[:, 3:4], in0=sm2[:, 2:3], in1=nhalf[:, 0:1], op=ALU.pow),
                 reads=[B_smf2[fs], B_cst], writes=[B_smf2[fs]])
            s = xcnt[0] % 2
            xcnt[0] += 1
            xsl = xb_t[:, s * D:(s + 1) * D]
            xsrc = x_d[(4 * g + tq) * 128:(4 * g + tq + 1) * 128, :]
            P.op("sp", I("dma_start", out=xsl, in_=xsrc), writes=[xb[s]], dsem=xb_sem[s])
            for hf, (tmp, B_tmp) in enumerate(((sz, B_sz), (ta, B_ta))):
                P.op("dve", I("scalar_tensor_tensor", out=tmp, in0=psbank(fb[hf]), scalar=sm2[:, 3:4],
                              in1=postg[:, hf * 512:(hf + 1) * 512], op0=ALU.mult, op1=ALU.mult),
                     reads=[psb[fb[hf]], B_smf2[fs], B_postg], writes=[B_tmp])
                P.op("dve", I("tensor_tensor", out=xsl[:, hf * 512:(hf + 1) * 512], in0=xsl[:, hf * 512:(hf + 1) * 512], in1=tmp,
                              op=ALU.add), reads=[B_tmp, xb[s]], writes=[xb[s]] if hf == 0 else [])
                if hf:
                    xb[s].w.append(P.ops["dve"][-1])
            ysl = y_d[(4 * g + tq) * 128:(4 * g + tq + 1) * 128, :]
            P.op("pool", I("dma_start", out=ysl, in_=xsl), reads=[xb[s]], dsem=out_sem[s])
    P.barrier()

    with nc.Block() as block:
        P.flush(block)
    nc._marks = P.marks
    return nc


_NC_CACHE = {}


def host_consts(rel_bias, pre_norm_g, b_in, lambda_q1, lambda_k1, lambda_q2, lambda_k2, subln_g, sink, post_norm_g):
    f = np.float32
    rel_bias = np.asarray(rel_bias, f)
    b = np.asarray(b_in, f).reshape(-1)
    kk = np.arange(128)[:, None]
    v = np.arange(1152)[None, :]
    bucket = t5_bucket_np((512 + kk - v).astype(np.int32))
    strips = np.ascontiguousarray(np.transpose(rel_bias[bucket], (2, 0, 1)))
    cfar = np.ascontiguousarray(np.broadcast_to(np.concatenate([rel_bias[15], rel_bias[31]])[None, :], (128, 24)))
    x384 = np.arange(384)[None, :]
    rel = 128 + kk - x384
    maskb = np.where(np.abs(rel) <= 128, 0.0, -30000.0).astype(f)
    gcol = np.ascontiguousarray(np.asarray(pre_norm_g, f).reshape(8, 128).T)
    bcol = np.ascontiguousarray(b.reshape(42, 128).T)
    bq = b[C_QB:C_QB + 512].reshape(2, 4, 64)
    bqb = np.ascontiguousarray(np.transpose(bq, (1, 0, 2)).reshape(4, 128).T)
    brow = np.ascontiguousarray(np.concatenate([b[C_VA:C_VA + 512], b[C_VB:C_VB + 128]])[None, :])
    lamv = np.concatenate([np.asarray(a, f).reshape(-1) for a in (lambda_q1, lambda_k1, lambda_q2, lambda_k2)])
    lamv = np.ascontiguousarray(np.broadcast_to(lamv[None, :], (128, 256)))
    gsub = np.ascontiguousarray(np.asarray(subln_g, f).reshape(128, 1))
    sinkr = np.ascontiguousarray(np.broadcast_to(np.asarray(sink, f).reshape(1, 8), (128, 8)))
    postg = np.ascontiguousarray(np.broadcast_to(np.asarray(post_norm_g, f).reshape(1, D), (128, D)))
    ident = np.eye(128, dtype=f)
    return dict(strips=strips, cfar=cfar, maskb=maskb, gcol=gcol, bcol=bcol, bqb=bqb, brow=brow, lamv=lamv, gsub=gsub,
                sinkr=sinkr, postg=postg, ident=ident)


def kernel(x, meta_tokens, rel_bias, pre_norm_g, w_in, b_in, lambda_q1, lambda_k1, lambda_q2, lambda_k2,
           subln_g, sink, w_out_a, w_out_b, w_out, post_norm_g):
    x = np.asarray(x, np.float32)
    B, S, _ = x.shape
    NT = S // 128
    if NT not in _NC_CACHE:
        _NC_CACHE[NT] = build(NT)
    nc = _NC_CACHE[NT]
    consts = host_consts(rel_bias, pre_norm_g, b_in, lambda_q1, lambda_k1, lambda_q2, lambda_k2, subln_g, sink, post_norm_g)
    meta_pad = np.zeros((128, D), np.float32)
    meta_pad[128 - NMETA:] = np.asarray(meta_tokens, np.float32)
    shared = dict(meta_pad=meta_pad, w_in=np.ascontiguousarray(np.asarray(w_in, np.float32)[0]),
                  w_out_a=np.ascontiguousarray(np.asarray(w_out_a, np.float32)[0]),
                  w_out_b=np.ascontiguousarray(np.asarray(w_out_b, np.float32)[0]),
                  w_out=np.ascontiguousarray(np.asarray(w_out, np.float32)[0]), **consts)
    in_maps = [dict(x=np.ascontiguousarray(x[i]), **shared) for i in range(B)]
    res = run_bass_kernel_spmd(nc, in_maps, core_ids=list(range(B)))
    return np.stack([np.asarray(r["y"], np.float32) for r in res.results], axis=0)
```

```python
import math
import numpy as np
import concourse.bass as bass
import concourse.mybir as mybir
from concourse.bass_utils import run_bass_kernel_spmd

F32 = mybir.dt.float32
BF16 = mybir.dt.bfloat16
AF = mybir.ActivationFunctionType
ALU = mybir.AluOpType
AX = mybir.AxisListType

D = 1024
DIN = 5376
EPS = 1e-6
NMETA = 16
ENGS = ("sp", "act", "dve", "pool", "pe")
C_QA, C_KA, C_VA, C_ZA, C_QB, C_KB, C_VB, C_ZB, C_GA, C_GB = 0, 512, 1024, 1536, 2048, 2560, 2688, 2816, 3328, 4352
LAM_INIT = 0.8 - 0.6 * math.exp(-0.3 * 0)


class Sem:
    def __init__(self, nc, name, group=False):
        self.h = nc.alloc_semaphore(name)
        self.name = name
        self.v = 0
        self.group = group


class Op:
    __slots__ = ("eng", "fn", "deps", "dsem", "sem", "val", "need", "idx")

    def __init__(self, eng, fn, dsem):
        self.eng, self.fn, self.dsem = eng, fn, dsem
        self.deps = []
        self.sem = None
        self.val = 0
        self.need = False


def I(method, *args, **kw):
    return (method, args, kw)


class Buf:
    __slots__ = ("w", "r", "name")

    def __init__(self, name=""):
        self.name = name
        self.w = []
        self.r = {}


class Prog:
    def __init__(self, nc):
        self.nc = nc
        self.ops = {e: [] for e in ENGS}
        self.esem = {e: Sem(nc, "es_" + e) for e in ENGS}
        self.bufs = []
        self.dma_since_barrier = []
        self.marks = []

    def buf(self, name=""):
        b = Buf(name)
        self.bufs.append(b)
        return b

    def bufs_n(self, n, name=""):
        return [self.buf(name + str(i)) for i in range(n)]

    def op(self, eng, fn, reads=(), writes=(), dsem=None, after=()):
        o = Op(eng, fn, dsem)
        deps = {}
        for b in reads:
            for d in b.w:
                deps[id(d)] = d
        for b in writes:
            for d in b.w:
                deps[id(d)] = d
            for d in b.r.values():
                deps[id(d)] = d
        for d in after:
            deps[id(d)] = d
        for d in deps.values():
            if d.eng == "pe" and eng == "pe" and d.dsem is None:
                continue
            if d is o:
                continue
            o.deps.append(d)
            d.need = True
        for b in writes:
            b.w = [o]
            b.r = {}
        for b in reads:
            if b in writes:
                continue
            key = eng if dsem is None else ("dma", id(dsem))
            b.r[key] = o
        self.ops[eng].append(o)
        if dsem is not None:
            self.dma_since_barrier.append(o)
        return o

    def barrier(self):
        self.marks.append(sum(1 for o in self.ops["pe"] if o.fn is not None))
        lasts = []
        for e in ENGS:
            for o in reversed(self.ops[e]):
                if o.fn is not None and o.dsem is None:
                    lasts.append(o)
                    break
        deps = lasts + self.dma_since_barrier
        self.dma_since_barrier = []
        for e in ENGS:
            self.op(e, None, after=deps)
        for b in self.bufs:
            b.w = []
            b.r = {}

    def flush(self, block):
        for e in ENGS:
            for o in self.ops[e]:
                if o.fn is None:
                    continue
                if o.dsem is not None:
                    o.dsem.v += 16
                    o.sem, o.val = o.dsem, o.dsem.v
                elif o.need:
                    s = self.esem[e]
                    s.v += 1
                    o.sem, o.val = s, s.v
        engmap = {"sp": block.sync, "act": block.scalar, "dve": block.vector, "pool": block.gpsimd, "pe": block.tensor}
        for e in ENGS:
            ops = self.ops[e]

            def body(eng, ops=ops):
                waited = {}
                for o in ops:
                    need = {}
                    for d in o.deps:
                        v = d.sem.v if d.sem.group else d.val
                        if v > need.get(d.sem.name, (0, None))[0]:
                            need[d.sem.name] = (v, d.sem)
                    for nm, (v, sm_) in need.items():
                        if waited.get(nm, 0) >= v:
                            continue
                        waited[nm] = v
                        eng.wait_ge(sm_.h, v)
                    if o.fn is None:
                        continue
                    m, a, kw = o.fn
                    ins = getattr(eng, m)(*a, **kw)
                    if o.dsem is not None:
                        ins.then_inc(o.sem.h, 16)
                    elif o.need:
                        ins.then_inc(o.sem.h, 1)

            engmap[e](body)


def t5_bucket_np(rel):
    half = 16
    max_exact = 8
    ret = np.where(rel > 0, half, 0)
    n = np.abs(rel)
    nf = np.maximum(n, 1).astype(np.float32)
    large = max_exact + (np.log(nf / max_exact) / math.log(128 / max_exact) * (half - max_exact)).astype(np.int32)
    large = np.minimum(large, half - 1)
    return ret + np.where(n < max_exact, n, large)


def build(NT):
    G = NT // 4
    NB = NT + 1
    nc = bass.Bass("TRN2", target_bir_lowering=False)
    P = Prog(nc)

    def din(name, shape, dt=F32):
        return nc.dram_tensor(name, shape, dt, kind="ExternalInput").ap()

    x_d = din("x", [NT * 128, D])
    meta_d = din("meta_pad", [128, D])
    win_d = din("w_in", [D, DIN])
    woa_d = din("w_out_a", [512, D])
    wob_d = din("w_out_b", [512, D])
    wo_d = din("w_out", [D, D])
    gcol_d = din("gcol", [128, 8])
    bcol_d = din("bcol", [128, 42])
    bqb_d = din("bqb", [128, 4])
    brow_d = din("brow", [1, 640])
    strips_d = din("strips", [12, 128, 1152])
    cfar_d = din("cfar", [128, 24])
    maskb_d = din("maskb", [128, 384])
    lamv_d = din("lamv", [128, 256])
    gsub_d = din("gsub", [128, 1])
    sink_d = din("sinkr", [128, 8])
    postg_d = din("postg", [128, D])
    ident_d = din("ident", [128, 128])
    y_d = nc.dram_tensor("y", [NT * 128, D], F32, kind="ExternalOutput").ap()
    wscr = nc.dram_tensor("wscr", [8, 128, DIN], BF16, kind="Internal").ap()
    woascr = nc.dram_tensor("woascr", [4, 128, D], BF16, kind="Internal").ap()
    wobscr = nc.dram_tensor("wobscr", [4, 128, D], BF16, kind="Internal").ap()
    woscr = nc.dram_tensor("woscr", [8, 128, D], BF16, kind="Internal").ap()
    hscr = nc.dram_tensor("hscr", [G + 1, 128, 4096], BF16, kind="Internal").ap()

    KVW = 2 * NB * 128 + NB * 258 + 64
    KVW += KVW % 2
    KVW = max(KVW, NB * 128 + NB * 130 + 2 + 2 * 8300)
    kv_t = nc.alloc_sbuf_tensor("kvarea", [128, KVW], BF16)
    oat_t = nc.alloc_sbuf_tensor("oat", [128, 4, NT * 128], BF16)
    xb_t = nc.alloc_sbuf_tensor("xb", [128, 2 * D], F32)
    xs_t = nc.alloc_sbuf_tensor("xs", [128, 2, D], BF16)
    hn_t = nc.alloc_sbuf_tensor("hnT", [128, 8, 512], BF16)
    cst_t = nc.alloc_sbuf_tensor("cst", [128, 512], F32)
    cbf_t = nc.alloc_sbuf_tensor("cbf", [128, 1024], BF16)
    ARENA_F32 = 12 * 1024 + 512
    ar_t = nc.alloc_sbuf_tensor("arena", [128, ARENA_F32], F32)
    ps_t = nc.alloc_psum_tensor("psf", [128, 7 * 512], F32)
    pst_t = nc.alloc_psum_tensor("pst", [128, 1024], BF16)

    class Arena:
        def __init__(self, ap_f32_2d, nwords):
            self.t = ap_f32_2d
            self.n = nwords
            self.pos = 0

        def reset(self):
            self.pos = 0

        def f32(self, n):
            a = self.t[:, self.pos:self.pos + n]
            self.pos += n
            assert self.pos <= self.n, ("arena overflow", self.pos, self.n)
            return a

        def bf16(self, n):
            n2 = (n + 1) // 2
            return self.f32(n2).bitcast(BF16)[:, 0:n]

    arena = Arena(ar_t[:, :], ARENA_F32)

    o_ = [0]

    def cslot(n):
        a = cst_t[:, o_[0]:o_[0] + n]
        o_[0] += n
        return a

    gcol = cslot(8)
    bcol = cslot(42)
    bhalf = cslot(42)
    bqb = cslot(4)
    cfar = cslot(24)
    zcol = cslot(1)
    gsub8 = cslot(1)
    esink = cslot(8)
    lamt = cslot(8)
    neglam = cslot(1)
    ssq = cslot(2)
    rstd = cslot(2)
    nhalf = cslot(4)
    assert o_[0] <= 512
    ident = cbf_t[:, 0:128]
    ones_row = cbf_t[0:1, 128:256]
    brow = cbf_t[0:1, 256:896]

    B_cst = P.buf("cst")
    xb = P.bufs_n(2, "xb")
    xsb = P.bufs_n(2, "xs")
    xb_sem = [Sem(nc, "xbs%d" % i) for i in range(2)]
    xs_sem = [Sem(nc, "xss%d" % i) for i in range(2)]
    B_hn = P.bufs_n(4, "hn")
    B_hn2 = P.bufs_n(4, "hn2")
    hn_views = [hn_t[:, :, :], xb_t[:, :].bitcast(BF16).rearrange("p (k n) -> p k n", k=8)]
    B_hnsets = [B_hn, B_hn2]
    hn_sem = [Sem(nc, "hns0"), Sem(nc, "hns1")]
    hst_sem = Sem(nc, "hst")
    bst_sem = Sem(nc, "bst")
    B_ssq = P.bufs_n(2, "ssq")
    B_rstd = P.bufs_n(2, "rstd")
    psb = P.bufs_n(7, "ps")
    B_pst = P.buf("pst")
    B_kt = P.buf("kt")
    B_va = P.buf("va")
    B_oat = P.buf("oat")
    setup_sem = Sem(nc, "setup", group=True)
    ld_sem = Sem(nc, "ld")
    ld_sem.group = False

    def psbank(b, n=512):
        return ps_t[:, b * 512:b * 512 + n]

    arena.reset()
    stage_small = arena.f32(1152)
    B_stage_small = P.buf("stage_small")
    lamv = arena.f32(256)
    sinkr = arena.f32(8)
    gsubr = arena.f32(1)
    identf = arena.f32(128)
    browf = arena.f32(640)

    setup_loads = []
    for (o, i) in [(gcol, gcol_d[:, :]), (bcol, bcol_d[:, :]), (bqb, bqb_d[:, :]), (cfar, cfar_d[:, :]),
                   (lamv, lamv_d[:, :]), (sinkr, sink_d[:, :]), (gsubr, gsub_d[:, :]), (identf, ident_d[:, :]),
                   (browf[0:1, :], brow_d[:, :])]:
        op = Op("sp", (I("dma_start", out=o, in_=i)), setup_sem)
        P.ops["sp"].append(op)
        P.dma_since_barrier.append(op)
        setup_loads.append(op)

    def cop(eng, fn):
        return P.op(eng, fn, reads=[], writes=[B_cst], after=setup_loads)

    cop("dve", I("memset", zcol, 0.0))
    cop("dve", I("memset", nhalf, -0.5))
    cop("dve", I("tensor_scalar", out=bhalf, in0=bcol, scalar1=0.5, scalar2=None, op0=ALU.mult))
    cop("dve", I("tensor_scalar", out=gsub8, in0=gsubr, scalar1=(1.0 - LAM_INIT), scalar2=None, op0=ALU.mult))
    cop("dve", I("tensor_copy", out=ident, in_=identf))
    cop("dve", I("memset", ones_row, 1.0))
    cop("dve", I("tensor_copy", out=brow, in_=browf[0:1, :]))
    prod = arena.f32(128)
    cop("dve", I("tensor_tensor", out=prod[:, 0:64], in0=lamv[:, 0:64], in1=lamv[:, 64:128], op=ALU.mult))
    cop("dve", I("tensor_tensor", out=prod[:, 64:128], in0=lamv[:, 128:192], in1=lamv[:, 192:256], op=ALU.mult))
    cop("dve", I("reduce_sum", out=lamt[:, 0:1], in_=prod[:, 0:64], axis=AX.X))
    cop("dve", I("reduce_sum", out=lamt[:, 1:2], in_=prod[:, 64:128], axis=AX.X))
    cop("act", I("activation", out=lamt[:, 2:4], in_=lamt[:, 0:2], func=AF.Exp))
    cop("act", I("activation", out=esink, in_=sinkr, func=AF.Exp))
    cop("dve", I("tensor_tensor", out=lamt[:, 4:5], in0=lamt[:, 3:4], in1=lamt[:, 2:3], op=ALU.subtract))
    cop("dve", I("tensor_scalar", out=neglam, in0=lamt[:, 4:5], scalar1=-LAM_INIT, scalar2=None, op0=ALU.add))

    cvt_i = [0]

    def convert(src_ap, dst_ap, n, scale_ap=None, scale_f=None, perm_qb=False):
        s = cvt_i[0] % 2
        cvt_i[0] += 1
        st = xb_t[:, s * D:s * D + n]
        ot = xs_t[:, s, 0:n]
        P.op("sp", I("dma_start", out=st, in_=src_ap), writes=[xb[s]], dsem=xb_sem[s])
        if perm_qb:
            src_v = xb_t[:, s * D:s * D + 512].rearrange("p (s j d) -> p s j d", s=2, j=4, d=64)
            dst_v = xs_t[:, s, 0:512].rearrange("p (j s d) -> p s j d", s=2, j=4, d=64)
            P.op("dve", I("tensor_scalar", out=dst_v, in0=src_v, scalar1=scale_ap, scalar2=None, op0=ALU.mult),
                 reads=[xb[s], B_cst], writes=[xsb[s]])
            st2 = xb_t[:, s * D + 512:s * D + n]
            ot2 = xs_t[:, s, 512:n]
            o2 = P.op("dve", I("tensor_scalar", out=ot2, in0=st2, scalar1=scale_ap, scalar2=None, op0=ALU.mult),
                      reads=[xb[s], B_cst], writes=[])
            P.op("pool", I("dma_start", out=dst_ap, in_=ot), reads=[xsb[s]], dsem=xs_sem[s], after=[o2])
            xsb[s].r[("dve2")] = o2
        else:
            sc = scale_ap if scale_ap is not None else scale_f
            P.op("dve", I("tensor_scalar", out=ot, in0=st, scalar1=sc, scalar2=None, op0=ALU.mult),
                 reads=[xb[s], B_cst], writes=[xsb[s]])
            P.op("pool", I("dma_start", out=dst_ap, in_=ot), reads=[xsb[s]], dsem=xs_sem[s])

    for k in range(8):
        for cb in range(6):
            c0 = cb * 1024
            n = min(1024, DIN - c0)
            convert(win_d[k * 128:(k + 1) * 128, c0:c0 + n], wscr[k, :, c0:c0 + n], n,
                    scale_ap=gcol[:, k:k + 1], perm_qb=(cb == 2))
    for k in range(4):
        convert(woa_d[k * 128:(k + 1) * 128, :], woascr[k, :, :], 1024, scale_f=0.5)
    for k in range(4):
        convert(wob_d[k * 128:(k + 1) * 128, :], wobscr[k, :, :], 1024, scale_f=0.5)
    for k in range(8):
        convert(wo_d[k * 128:(k + 1) * 128, :], woscr[k, :, :], 1024, scale_f=0.5)
    P.barrier()

    xcnt = [0]

    def prep_tile(src_ap, col):
        s = xcnt[0] % 2
        xcnt[0] += 1
        xt = xb_t[:, s * D:(s + 1) * D]
        xo = xs_t[:, s, :]
        P.op("sp", I("dma_start", out=xt, in_=src_ap), writes=[xb[s]], dsem=xb_sem[s])
        P.op("act", I("activation", out=xo, in_=xt, func=AF.Square, accum_out=ssq[:, s:s + 1]),
             reads=[xb[s]], writes=[xsb[s], B_ssq[s]])
        P.op("dve", I("tensor_scalar", out=rstd[:, s:s + 1], in0=ssq[:, s:s + 1], scalar1=1.0 / D, scalar2=EPS,
                                              op0=ALU.mult, op1=ALU.add), reads=[B_ssq[s]], writes=[B_rstd[s]])
        P.op("pool", I("tensor_tensor", out=rstd[:, s:s + 1], in0=rstd[:, s:s + 1], in1=nhalf[:, 0:1], op=ALU.pow),
             reads=[B_rstd[s], B_cst], writes=[B_rstd[s]])
        P.op("dve", I("tensor_scalar", out=xo, in0=xt, scalar1=rstd[:, s:s + 1], scalar2=None, op0=ALU.mult),
             reads=[xb[s], B_rstd[s]], writes=[xsb[s]])
        for k in range(8):
            P.op("pe", I("transpose", out=pst_t[:, k * 128:(k + 1) * 128], in_=xs_t[:, s, k * 128:(k + 1) * 128],
                                                  identity=ident),
                 reads=[xsb[s]] if k else [xsb[s], B_cst], writes=[B_pst] if k == 0 else [])
        lastT = P.ops["pe"][-1]
        B_pst.w = [lastT]
        xsb[s].r["pe"] = lastT
        P.op("dve", I("tensor_copy", out=hn_t[:, :, col * 128:(col + 1) * 128],
                                            in_=pst_t[:, :].rearrange("p (k t) -> p k t", k=8)),
             reads=[B_pst], writes=[B_hn[col]])

    def tile_src(c):
        return meta_d[:, :] if c == 0 else x_d[(c - 1) * 128:c * 128, :]

    def groups_kv():
        yield [0]
        for g in range(G):
            yield [1 + 4 * g + j for j in range(4)]

    glist = list(groups_kv())

    def hn_store(gi, nt):
        P.op("pool", I("dma_start", out=hscr[gi].rearrange("p (k n) -> p k n", k=8)[:, :, 0:nt * 128], in_=hn_t[:, :, 0:nt * 128]),
             reads=B_hn[0:nt], dsem=hst_sem)

    def hn_load(gi, nt, which):
        P.op("sp", I("dma_start", out=hn_views[which][:, :, 0:nt * 128],
                     in_=hscr[gi].rearrange("p (k n) -> p k n", k=8)[:, :, 0:nt * 128]),
             writes=B_hnsets[which], dsem=hn_sem[which])
        return hn_views[which], B_hnsets[which]

    def hn_iter(first):
        if first:
            for gi, blocks in enumerate(glist):
                for j, c in enumerate(blocks):
                    prep_tile(tile_src(c), j)
                hn_store(gi, len(blocks))
                yield blocks, hn_views[0], B_hn
        else:
            nxt = hn_load(0, len(glist[0]), 0)
            for gi, blocks in enumerate(glist):
                cur = nxt
                if gi + 1 < len(glist):
                    nxt = hn_load(gi + 1, len(glist[gi + 1]), (gi + 1) % 2)
                yield blocks, cur[0], cur[1]

    def load_w(dst_ap, src_ap, buf, sem):
        return P.op("sp", I("dma_start", out=dst_ap, in_=src_ap), writes=[buf], dsem=sem)

    wsem = [Sem(nc, "wsem%d" % i) for i in range(8)]

    for p in range(2):
        arena.reset()
        KT = kv_t[:, 0:2 * NB * 128].rearrange("p (h n) -> p h n", h=2)
        VA = kv_t[:, 2 * NB * 128:2 * NB * 128 + NB * 258].rearrange("p (c n) -> p c n", n=258)
        wk = arena.bf16(8 * 256).rearrange("p (k n) -> p k n", k=8)
        wv = arena.bf16(8 * 256).rearrange("p (k n) -> p k n", k=8)
        B_wk, B_wv = P.buf("wk"), P.buf("wv")
        load_w(wk, wscr[:, :, C_KA + p * 256:C_KA + (p + 1) * 256].rearrange("k p n -> p k n"), B_wk, wsem[0])
        load_w(wv, wscr[:, :, C_VA + p * 256:C_VA + (p + 1) * 256].rearrange("k p n -> p k n"), B_wv, wsem[1])
        B_vcol = P.buf("vcol")
        P.op("pool", I("memset", VA[:, :, 128:129], 1.0), writes=[B_vcol])
        P.op("pool", I("memset", VA[:, :, 257:258], 1.0), writes=[B_vcol])
        B_ktc = {}
        B_vac = {}
        for blocks, HN, BHN in hn_iter(p == 0):
            nt = len(blocks)
            c0 = blocks[0]
            for hh in range(2):
                bank = hh
                for k in range(8):
                    P.op("pe", I("matmul",
                        psbank(bank, nt * 128), lhsT=wk[:, k, hh * 128:(hh + 1) * 128], rhs=HN[:, k, 0:nt * 128],
                        start=(k == 0), stop=(k == 7)),
                        reads=([B_wk] + BHN[0:nt]) if k == 0 else [], writes=[psb[bank]] if k == 0 else [])
                last = P.ops["pe"][-1]
                psb[bank].w = [last]
                for b in BHN[0:nt]:
                    b.r["pe"] = last
                kb = P.buf("ktc")
                B_ktc[(blocks[0], hh)] = kb
                ch = (C_KA + p * 256) // 128 + hh
                P.op("dve", I("tensor_scalar",
                    out=KT[:, hh, c0 * 128:(c0 + nt) * 128], in0=psbank(bank, nt * 128), scalar1=bcol[:, ch:ch + 1],
                    scalar2=None, op0=ALU.add), reads=[psb[bank], B_cst], writes=[kb])
            for j, c in enumerate(blocks):
                bank = 2 + (j % 4)
                for k in range(8):
                    P.op("pe", I("matmul",
                        psbank(bank, 256), lhsT=HN[:, k, j * 128:(j + 1) * 128], rhs=wv[:, k, :],
                        start=(k == 0), stop=False),
                        reads=([B_wv, BHN[j]]) if k == 0 else [], writes=[psb[bank]] if k == 0 else [])
                P.op("pe", I("matmul",
                    psbank(bank, 256), lhsT=ones_row, rhs=brow[0:1, p * 256:(p + 1) * 256], start=False, stop=True),
                    reads=[B_cst])
                last = P.ops["pe"][-1]
                psb[bank].w = [last]
                BHN[j].r["pe"] = last
                vb = P.buf("vac")
                B_vac[c] = vb
                P.op("act", I("activation",
                    out=VA[:, c, :].rearrange("p (h n) -> p h n", h=2)[:, :, 0:128],
                    in_=psbank(bank, 256).rearrange("p (h n) -> p h n", h=2), func=AF.Copy),
                    reads=[psb[bank]], writes=[vb])
                if c == 0:
                    P.op("pool", I("memset", VA[0:128 - NMETA, 0, :], 0.0), reads=[], writes=[vb], after=list(B_vcol.w))
        P.barrier()

        arena.reset()
        wq = arena.bf16(8 * 256).rearrange("p (k n) -> p k n", k=8)
        strips = arena.bf16(2 * 1152).rearrange("p (h n) -> p h n", h=2)
        QT = arena.bf16(2 * 2 * 512).rearrange("p (s h n) -> p s h n", s=2, h=2)
        ET = arena.bf16(3 * 1024).rearrange("p (s n) -> p s n", s=3)
        accs = arena.f32(8 * 129).rearrange("p (a n) -> p a n", a=8)
        ofin = arena.f32(4 * 128).rearrange("p (j n) -> p j n", j=4)
        t1 = arena.f32(128)
        sq = arena.f32(128)
        onb = arena.bf16(4 * 128).rearrange("p (j n) -> p j n", j=4)
        sm = arena.f32(32)
        sstage = arena.f32(1152)
        B_wq, B_strips = P.buf("wq"), P.buf("strips")
        B_QT = P.bufs_n(2, "QT")
        B_ET = P.bufs_n(3, "ET")
        B_accs, B_fin, B_on, B_sstage = P.buf("accs"), P.buf("fin"), P.buf("on"), P.buf("sstage")
        load_w(wq, wscr[:, :, C_QA + p * 256:C_QA + (p + 1) * 256].rearrange("k p n -> p k n"), B_wq, wsem[0])
        for hh in range(2):
            P.op("sp", I("dma_start", out=sstage, in_=strips_d[2 * p + hh, :, :]), writes=[B_sstage], dsem=wsem[2])
            P.op("dve", I("tensor_copy", out=strips[:, hh, :], in_=sstage), reads=[B_sstage], writes=[B_strips])
        ucnt = [0]
        ecnt = [0]
        pstf = pst_t[:, :].bitcast(F32)
        deferred = []

        def defer(n, tag, fn):
            deferred.append([n, tag, fn])

        def tick():
            for d in deferred:
                d[0] -= 1
            due = [d for d in deferred if d[0] <= 0]
            for d in due:
                deferred.remove(d)
                d[2]()

        def flush_deferred(tag=None):
            due = [d for d in deferred if tag is None or d[1] == tag]
            for d in due:
                deferred.remove(d)
                d[2]()

        def qproj_mm(hh, k0, k1, HNq, BHNq):
            for k in range(k0, k1):
                P.op("pe", I("matmul", pstf, lhsT=wq[:, k, hh * 128:(hh + 1) * 128], rhs=HNq[:, k, :],
                             start=(k == 0), stop=(k == 7)),
                     reads=([B_wq] + BHNq) if k == 0 else [], writes=[B_pst] if k == 0 else [])
            if k1 == 8:
                last = P.ops["pe"][-1]
                B_pst.w = [last]
                for b in BHNq:
                    b.r["pe"] = last

        def qproj_evac(hh, qsl):
            ch = (C_QA + p * 256) // 128 + hh
            P.op("dve", I("tensor_scalar", out=QT[:, qsl, hh, :], in0=pstf, scalar1=bcol[:, ch:ch + 1], scalar2=0.125,
                          op0=ALU.add, op1=ALU.mult), reads=[B_pst, B_cst], writes=[B_QT[qsl]] if hh == 0 else [])
            if hh == 1:
                B_QT[qsl].w.append(P.ops["dve"][-1])

        def qproj_schedule(gq, HNq, BHNq, now):
            qsl = gq % 2
            t = 22
            for hh in range(2):
                for k0 in (0, 2, 4, 6):
                    fn = (lambda hh=hh, k0=k0: qproj_mm(hh, k0, k0 + 2, HNq, BHNq))
                    if now:
                        fn()
                    else:
                        defer(t, "q", fn)
                    t += 1
                fn = (lambda hh=hh: qproj_evac(hh, qsl))
                if now:
                    fn()
                else:
                    defer(t, "q", fn)
                t += 2

        def B_ktc_get(c, hh):
            for (c0, h2), b in B_ktc.items():
                if h2 == hh and (c0 == c or (c0 >= 1 and c0 <= c < c0 + 4 and c >= 1)):
                    return b
            raise KeyError((c, hh))

        def emit_scores(c, g, hh, h, qs, b0):
            slot = ucnt[0] % 2
            ucnt[0] += 1
            bk = 2 * slot
            Dd = c - b0
            mixed = (-1 <= Dd <= 4)
            P.op("pe", I("matmul", psbank(bk), lhsT=KT[0:64, hh, c * 128:(c + 1) * 128],
                         rhs=QT[0:64, qs, hh, :], start=True, stop=not mixed),
                 reads=[B_ktc_get(c, hh), B_QT[qs]], writes=[psb[bk], psb[bk + 1]])
            P.op("pe", I("matmul", psbank(bk + 1), lhsT=KT[64:128, hh, c * 128:(c + 1) * 128],
                         rhs=QT[64:128, qs, hh, :], start=True, stop=not mixed))
            if mixed:
                o0 = (4 - Dd) * 128
                P.op("pe", I("matmul", psbank(bk), lhsT=ident, rhs=strips[:, hh, o0:o0 + 512],
                             start=False, stop=True), reads=[B_strips, B_cst])
                P.op("pe", I("matmul", psbank(bk + 1), lhsT=ident, rhs=strips[:, hh, o0:o0 + 512],
                             start=False, stop=True))
            last = P.ops["pe"][-1]
            psb[bk].w = [last]
            psb[bk + 1].w = [last]
            B_QT[qs].r["pe"] = last
            es = ecnt[0] % 3
            ecnt[0] += 1
            if mixed:
                bias_ap = zcol
            elif Dd <= -2:
                bias_ap = cfar[:, h:h + 1]
            else:
                bias_ap = cfar[:, 12 + h:13 + h]
            P.op("act", I("activation", out=ET[:, es, :], in_=ps_t[:, bk * 512:bk * 512 + 1024], func=AF.Exp,
                          bias=bias_ap, scale=1.0),
                 reads=[psb[bk], psb[bk + 1], B_cst], writes=[B_ET[es]])
            return (c, es)

        def emit_av(c, es, hh):
            first = (c == 0)
            lastc = (c == NB - 1)
            for j in range(4):
                for m in range(2):
                    a = 2 * j + m
                    bank = 4 + a // 3
                    off = (a % 3) * 129
                    P.op("pe", I("matmul", ps_t[:, bank * 512 + off:bank * 512 + off + 129],
                                 lhsT=ET[:, es, m * 512 + j * 128:m * 512 + (j + 1) * 128],
                                 rhs=VA[:, c, hh * 129:hh * 129 + 129], start=(first and a % 3 == 0), stop=lastc,
                                 skip_group_check=True),
                         reads=([B_ET[es], B_vac[c]]) if a == 0 else [],
                         writes=[psb[4], psb[5], psb[6]] if (first and a == 0) else [])
            last = P.ops["pe"][-1]
            B_ET[es].r["pe"] = last
            if lastc:
                for b in (psb[4], psb[5], psb[6]):
                    b.w = [last]

        def finalize_part1():
            for bi, na in ((4, 3), (5, 3), (6, 2)):
                a0 = (bi - 4) * 3
                P.op("dve", I("tensor_copy", out=accs[:, a0:a0 + na, :],
                              in_=psbank(bi, na * 129).rearrange("p (a n) -> p a n", a=na)),
                     reads=[psb[bi]], writes=[B_accs] if bi == 4 else [])
                if bi != 4:
                    B_accs.w.append(P.ops["dve"][-1])

            def fin(fn):
                return P.op("dve", fn, reads=[B_accs, B_fin, B_cst], writes=[B_fin])

            fin(I("reciprocal", out=sm[:, 0:8], in_=accs[:, :, 128:129].rearrange("p a o -> p (a o)")))
            fin(I("tensor_scalar", out=sm[:, 8:12], in0=sm[:, 0:8].rearrange("p (j m) -> p j m", m=2)[:, :, 1],
                  scalar1=neglam, scalar2=None, op0=ALU.mult))
            for j in range(4):
                fin(I("tensor_scalar", out=t1, in0=accs[:, 2 * j + 1, 0:128], scalar1=sm[:, 8 + j:9 + j],
                      scalar2=None, op0=ALU.mult))
                fin(I("scalar_tensor_tensor", out=ofin[:, j, :], in0=accs[:, 2 * j, 0:128],
                      scalar=sm[:, 2 * j:2 * j + 1], in1=t1, op0=ALU.mult, op1=ALU.add))
                fin(I("tensor_tensor", out=sq, in0=ofin[:, j, :], in1=ofin[:, j, :], op=ALU.mult))
                fin(I("reduce_sum", out=sm[:, 12 + j:13 + j], in_=sq, axis=AX.X))
            fin(I("tensor_scalar", out=sm[:, 16:20], in0=sm[:, 12:16], scalar1=1.0 / 128, scalar2=EPS,
                  op0=ALU.mult, op1=ALU.add))
            P.op("pool", I("tensor_tensor", out=sm[:, 16:20], in0=sm[:, 16:20], in1=nhalf[:, 0:4], op=ALU.pow),
                 reads=[B_fin, B_cst], writes=[B_fin])
            for j in range(4):
                P.op("dve", I("tensor_scalar", out=onb[:, j, :], in0=ofin[:, j, :], scalar1=sm[:, 16 + j:17 + j],
                              scalar2=None, op0=ALU.mult),
                     reads=[B_fin], writes=[B_on] if j == 0 else [])
                if j:
                    B_on.w.append(P.ops["dve"][-1])

        def finalize_part2(h, g):
            for j in range(4):
                P.op("pe", I("transpose", out=pst_t[:, j * 128:(j + 1) * 128], in_=onb[:, j, :], identity=ident),
                     reads=[B_on, B_cst] if j == 0 else [], writes=[B_pst] if j == 0 else [])
            lastT = P.ops["pe"][-1]
            B_pst.w = [lastT]
            B_on.r["pe"] = lastT
            P.op("dve", I("tensor_scalar", out=oat_t[:, h, g * 512:(g + 1) * 512], in0=pst_t[:, 0:512],
                          scalar1=gsub8, scalar2=None, op0=ALU.mult),
                 reads=[B_pst, B_cst], writes=[])

        nxt = hn_load(1, 4, 0)
        qproj_schedule(0, nxt[0], nxt[1], now=True)
        for g in range(G):
            qs = g % 2
            flush_deferred("q")
            if g + 1 < G:
                nxt = hn_load(2 + g, 4, (g + 1) % 2)
            b0 = 1 + 4 * g
            for hh in range(2):
                h = 2 * p + hh
                if hh == 1 and g + 1 < G:
                    qproj_schedule(g + 1, nxt[0], nxt[1], now=False)
                pend = [emit_scores(0, g, hh, h, qs, b0)]
                if NB > 1:
                    pend.append(emit_scores(1, g, hh, h, qs, b0))
                for c in range(2, NB):
                    cur = emit_scores(c, g, hh, h, qs, b0)
                    pv = pend.pop(0)
                    emit_av(pv[0], pv[1], hh)
                    tick()
                    pend.append(cur)
                while pend:
                    pv = pend.pop(0)
                    emit_av(pv[0], pv[1], hh)
                    tick()
                flush_deferred("f")
                finalize_part1()
                defer(20, "f", (lambda h=h, g=g: finalize_part2(h, g)))
        flush_deferred()
        P.barrier()

    arena.reset()
    KTB = kv_t[:, 0:NB * 128]
    VB = kv_t[:, NB * 128:NB * 128 + NB * 130].rearrange("p (c n) -> p c n", n=130)
    kvfree0 = NB * 128 + NB * 130
    kvfree0 += kvfree0 % 2
    wkb = arena.bf16(8 * 128).rearrange("p (k n) -> p k n", k=8)
    wvb = arena.bf16(8 * 128).rearrange("p (k n) -> p k n", k=8)
    B_wk, B_wv = P.buf("wkb"), P.buf("wvb")
    load_w(wkb, wscr[:, :, C_KB:C_KB + 128].rearrange("k p n -> p k n"), B_wk, wsem[0])
    load_w(wvb, wscr[:, :, C_VB:C_VB + 128].rearrange("k p n -> p k n"), B_wv, wsem[1])
    B_vcol = P.buf("vcolb")
    P.op("pool", I("memset", VB[:, :, 64:65], 1.0), writes=[B_vcol])
    P.op("pool", I("memset", VB[:, :, 129:130], 1.0), writes=[B_vcol])
    for blocks, HN, BHN in hn_iter(False):
        nt = len(blocks)
        c0 = blocks[0]
        for k in range(8):
            P.op("pe", I("matmul", psbank(0, nt * 128), lhsT=wkb[:, k, :], rhs=HN[:, k, 0:nt * 128],
                                               start=(k == 0), stop=(k == 7)),
                 reads=([B_wk] + BHN[0:nt]) if k == 0 else [], writes=[psb[0]] if k == 0 else [])
        last = P.ops["pe"][-1]
        psb[0].w = [last]
        for b in BHN[0:nt]:
            b.r["pe"] = last
        P.op("dve", I("tensor_scalar",
            out=KTB[:, c0 * 128:(c0 + nt) * 128], in0=psbank(0, nt * 128), scalar1=bcol[:, C_KB // 128:C_KB // 128 + 1],
            scalar2=None, op0=ALU.add), reads=[psb[0], B_cst], writes=[])
        psb[0].r["dve"] = P.ops["dve"][-1]
        for j, c in enumerate(blocks):
            bank = 2 + (j % 4)
            for k in range(8):
                P.op("pe", I("matmul",
                    psbank(bank, 128), lhsT=HN[:, k, j * 128:(j + 1) * 128], rhs=wvb[:, k, :], start=(k == 0), stop=False),
                    reads=([B_wv, BHN[j]]) if k == 0 else [], writes=[psb[bank]] if k == 0 else [])
            P.op("pe", I("matmul", psbank(bank, 128), lhsT=ones_row, rhs=brow[0:1, 512:640],
                                                     start=False, stop=True), reads=[B_cst])
            last = P.ops["pe"][-1]
            psb[bank].w = [last]
            BHN[j].r["pe"] = last
            P.op("act", I("activation",
                out=VB[:, c, :].rearrange("p (h n) -> p h n", h=2)[:, :, 0:64],
                in_=psbank(bank, 128).rearrange("p (h n) -> p h n", h=2), func=AF.Copy),
                reads=[psb[bank]], writes=[])
            psb[bank].r["act"] = P.ops["act"][-1]
            if c == 0:
                P.op("pool", I("memset", VB[0:128 - NMETA, 0, :], 0.0), after=list(B_vcol.w) + [P.ops["act"][-1]])
    P.barrier()

    arena.reset()
    ar2 = Arena(kv_t[:, kvfree0:KVW].bitcast(F32), (KVW - kvfree0) // 2)
    biasB = arena.bf16(8 * 384).rearrange("p (h n) -> p h n", h=8)
    Woa = arena.bf16(4 * 1024).rearrange("p (k n) -> p k n", k=4)
    Wob = arena.bf16(4 * 1024).rearrange("p (k n) -> p k n", k=4)
    Wring = [arena.bf16(4096), arena.bf16(4096)]
    postg = arena.f32(1024)
    ttile = arena.f32(512)
    sz = arena.f32(512)
    ta = arena.f32(512)
    EBs = [ar2.bf16(1536), ar2.bf16(1536)]
    mixedT = ar2.bf16(8 * 512).rearrange("p (k n) -> p k n", k=8)
    goaT = ar2.bf16(4 * 512).rearrange("p (k n) -> p k n", k=4)
    gobT = ar2.bf16(4 * 512).rearrange("p (k n) -> p k n", k=4)
    QTb = ar2.bf16(4 * 512).rearrange("p (k n) -> p k n", k=4)
    obt = ar2.bf16(4 * 512).rearrange("p (t n) -> p t n", t=4)
    tb = ar2.f32(512)
    smf = ar2.f32(16)
    smf2 = ar2.f32(8)
    bstage = ttile[:, 0:384]
    maskb = sz[:, 0:384]
    pstf = pst_t[:, :].bitcast(F32)
    B_biasB, B_Woa, B_Wob, B_postg = P.buf("biasB"), P.buf("Woa"), P.buf("Wob"), P.buf("postg")
    B_ring = P.bufs_n(2, "ring")
    ring_sem = [wsem[3], wsem[4]]
    B_t, B_sz, B_mixed, B_goa, B_gob, B_QTb = (P.buf("t"), P.buf("sz"), P.buf("mixed"), P.buf("goa"), P.buf("gob"), P.buf("QTb"))
    B_ob = P.bufs_n(4, "ob")
    B_EB = P.bufs_n(2, "EB")
    B_ta, B_tb = P.buf("ta"), P.buf("tb")
    B_smfs = P.bufs_n(2, "smf")
    B_smf2 = P.bufs_n(2, "smf2")
    out_sem = [wsem[5], wsem[6]]
    load_w(Woa, woascr.rearrange("k p n -> p k n"), B_Woa, wsem[0])
    load_w(Wob, wobscr.rearrange("k p n -> p k n"), B_Wob, wsem[1])
    load_w(postg, postg_d[:, :], B_postg, wsem[2])
    load_w(maskb, maskb_d[:, :], B_sz, wsem[7])
    for h8 in range(8):
        P.op("sp", I("dma_start", out=bstage, in_=strips_d[4 + h8, :, 384:768]), writes=[B_t], dsem=bst_sem)
        P.op("dve", I("tensor_tensor", out=biasB[:, h8, :], in0=bstage, in1=maskb, op=ALU.add),
             reads=[B_t, B_sz], writes=[B_biasB])
    rcnt = [0]

    def ring_load(src_ap):
        s = rcnt[0] % 2
        rcnt[0] += 1
        load_w(Wring[s].rearrange("p (k n) -> p k n", k=8), src_ap, B_ring[s], ring_sem[s])
        return s

    def proj_fm(wblk, s, col0, bank):
        for k in range(8):
            P.op("pe", I("matmul", psbank(bank), lhsT=wblk[:, k, col0:col0 + 128], rhs=hn_t[:, k, :],
                         start=(k == 0), stop=(k == 7)),
                 reads=([B_ring[s]] + B_hn) if k == 0 else [], writes=[psb[bank]] if k == 0 else [])
        last = P.ops["pe"][-1]
        psb[bank].w = [last]
        B_ring[s].r["pe"] = last
        for b in B_hn:
            b.r["pe"] = last

    ebc = [0]
    fcnt = [0]
    ACCS = [(ps_t[:, 6 * 512:6 * 512 + 260], psb[6]), (pstf[:, 0:260], B_pst)]
    for g in range(G):
        hn_load(1 + g, 4, 0)
        s = ring_load(wscr[:, :, C_QB:C_QB + 512].rearrange("k p n -> p k n"))
        wblk = Wring[s].rearrange("p (k n) -> p k n", k=8)
        for j in range(4):
            bank = j % 2
            proj_fm(wblk, s, j * 128, bank)
            P.op("dve", I("tensor_scalar", out=QTb[:, j, :], in0=psbank(bank), scalar1=bqb[:, j:j + 1],
                          scalar2=0.125, op0=ALU.add, op1=ALU.mult),
                 reads=[psb[bank], B_cst], writes=[B_QTb] if j == 0 else [])
            if j:
                B_QTb.w.append(P.ops["dve"][-1])
        for tq in range(4):
            b = 1 + 4 * g + tq
            cs = [c for c in (b - 1, b, b + 1) if 0 <= c < NB]
            jj0 = cs[0] - (b - 1)
            ncs = len(cs)
            for half in range(2):
                sl = ebc[0] % 2
                ebc[0] += 1
                r0 = half * 64
                h0 = half * 4
                banks = [3 * sl + (c - (b - 1)) for c in cs]
                for i, c in enumerate(cs):
                    jj = c - (b - 1)
                    bk = 3 * sl + jj
                    P.op("pe", I("matmul", psbank(bk), lhsT=KTB[r0:r0 + 64, c * 128:(c + 1) * 128],
                                 rhs=QTb[r0:r0 + 64, :, tq * 128:(tq + 1) * 128], start=True, stop=False),
                         reads=[B_QTb] if i == 0 else [], writes=[psb[x] for x in banks] if i == 0 else [])
                    P.op("pe", I("matmul", psbank(bk), lhsT=ident, rhs=biasB[:, h0:h0 + 4, (2 - jj) * 128:(3 - jj) * 128],
                                 start=False, stop=True), reads=[B_biasB, B_cst] if i == 0 else [])
                last = P.ops["pe"][-1]
                for x in banks:
                    psb[x].w = [last]
                B_QTb.r["pe"] = last
                lo = (3 * sl + jj0) * 512
                P.op("act", I("activation", out=EBs[sl][:, jj0 * 512:(jj0 + ncs) * 512], in_=ps_t[:, lo:lo + ncs * 512], func=AF.Exp),
                     reads=[psb[x] for x in banks], writes=[B_EB[sl]])
                acc_ap, acc_buf = ACCS[sl]
                firstmm = True
                for i4 in range(4):
                    for i, c in enumerate(cs):
                        jj = c - (b - 1)
                        P.op("pe", I("matmul", acc_ap[:, i4 * 65:(i4 + 1) * 65],
                                     lhsT=EBs[sl][:, jj * 512 + i4 * 128:jj * 512 + (i4 + 1) * 128],
                                     rhs=VB[:, c, half * 65:half * 65 + 65], start=firstmm, stop=(i == ncs - 1),
                                     skip_group_check=True),
                             reads=[B_EB[sl]] if firstmm else [], writes=[acc_buf] if firstmm else [])
                        firstmm = False
                last = P.ops["pe"][-1]
                acc_buf.w = [last]
                B_EB[sl].r["pe"] = last
                acc3 = acc_ap.rearrange("p (i n) -> p i n", n=65)
                sms = smf[:, sl * 8:sl * 8 + 8]
                P.op("dve", I("tensor_tensor", out=sms[:, 0:4], in0=acc3[:, :, 64:65].rearrange("p i o -> p (i o)"),
                              in1=esink[:, h0:h0 + 4], op=ALU.add),
                     reads=[acc_buf, B_cst], writes=[B_smfs[sl]])
                P.op("dve", I("reciprocal", out=sms[:, 4:8], in_=sms[:, 0:4]), reads=[B_smfs[sl]], writes=[B_smfs[sl]])
                for i4 in range(4):
                    P.op("dve", I("tensor_scalar", out=obt[:, tq, (h0 + i4) * 64:(h0 + i4 + 1) * 64], in0=acc3[:, i4, 0:64],
                                  scalar1=sms[:, 4 + i4:5 + i4], scalar2=None, op0=ALU.mult),
                         reads=[acc_buf, B_smfs[sl]], writes=[B_ob[tq]] if (half == 0 and i4 == 0) else [])
                    if half or i4:
                        B_ob[tq].w.append(P.ops["dve"][-1])

        def gate(col_base, which):
            s = ring_load(wscr[:, :, col_base:col_base + 512].rearrange("k p n -> p k n"))
            wblk = Wring[s].rearrange("p (k n) -> p k n", k=8)
            for c in range(4):
                bank = c % 2
                ch = col_base // 128 + c
                proj_fm(wblk, s, c * 128, bank)
                P.op("act", I("activation", out=ttile, in_=psbank(bank), func=AF.Tanh, bias=bhalf[:, ch:ch + 1], scale=0.5),
                     reads=[psb[bank], B_cst], writes=[B_t])
                P.op("dve", I("scalar_tensor_tensor", out=sz, in0=psbank(bank), scalar=bcol[:, ch:ch + 1], in1=ttile,
                              op0=ALU.add, op1=ALU.mult), reads=[psb[bank], B_t, B_cst], writes=[B_sz])
                P.op("dve", I("scalar_tensor_tensor", out=sz, in0=psbank(bank), scalar=bcol[:, ch:ch + 1], in1=sz,
                              op0=ALU.add, op1=ALU.add), reads=[psb[bank], B_cst], writes=[B_sz])
                if which == "a":
                    P.op("dve", I("tensor_tensor", out=goaT[:, c, :], in0=oat_t[:, c, g * 512:(g + 1) * 512], in1=sz, op=ALU.mult),
                         reads=[B_sz], writes=[B_goa] if c == 0 else [])
                    if c:
                        B_goa.w.append(P.ops["dve"][-1])
                else:
                    for tq in range(4):
                        P.op("pe", I("transpose", out=pst_t[:, tq * 128:(tq + 1) * 128], in_=obt[:, tq, c * 128:(c + 1) * 128],
                                     identity=ident), reads=[B_ob[tq], B_cst], writes=[B_pst] if tq == 0 else [])
                    lastT = P.ops["pe"][-1]
                    B_pst.w = [lastT]
                    for tq in range(4):
                        B_ob[tq].r["pe"] = lastT
                    P.op("dve", I("tensor_tensor", out=gobT[:, c, :], in0=pst_t[:, 0:512], in1=sz, op=ALU.mult),
                         reads=[B_pst, B_sz], writes=[B_gob] if c == 0 else [])
                    if c:
                        B_gob.w.append(P.ops["dve"][-1])

        gate(C_ZB, "b")
        gate(C_ZA, "a")
        for half in range(2):
            sa = ring_load(wscr[:, :, C_GA + half * 512:C_GA + (half + 1) * 512].rearrange("k p n -> p k n"))
            wa = Wring[sa].rearrange("p (k n) -> p k n", k=8)
            sb = None
            for c4 in range(4):
                fc = half * 4 + c4
                proj_fm(wa, sa, c4 * 128, 0)
                cha = C_GA // 128 + fc
                P.op("act", I("activation", out=ta, in_=psbank(0), func=AF.Tanh, bias=bhalf[:, cha:cha + 1], scale=0.5),
                     reads=[psb[0], B_cst], writes=[B_ta])
                if c4 == 0:
                    sb = ring_load(wscr[:, :, C_GB + half * 512:C_GB + (half + 1) * 512].rearrange("k p n -> p k n"))
                    wb = Wring[sb].rearrange("p (k n) -> p k n", k=8)
                proj_fm(wb, sb, c4 * 128, 1)
                chb = C_GB // 128 + fc
                P.op("act", I("activation", out=tb, in_=psbank(1), func=AF.Tanh, bias=bhalf[:, chb:chb + 1], scale=0.5),
                     reads=[psb[1], B_cst], writes=[B_tb])
                for k in range(4):
                    P.op("pe", I("matmul", psbank(2), lhsT=Woa[:, k, fc * 128:(fc + 1) * 128], rhs=goaT[:, k, :],
                                 start=(k == 0), stop=(k == 3)),
                         reads=[B_Woa, B_goa] if k == 0 else [], writes=[psb[2]] if k == 0 else [])
                last = P.ops["pe"][-1]
                psb[2].w = [last]
                B_goa.r["pe"] = last
                for k in range(4):
                    P.op("pe", I("matmul", psbank(3), lhsT=Wob[:, k, fc * 128:(fc + 1) * 128], rhs=gobT[:, k, :],
                                 start=(k == 0), stop=(k == 3)),
                         reads=[B_Wob, B_gob] if k == 0 else [], writes=[psb[3]] if k == 0 else [])
                last = P.ops["pe"][-1]
                psb[3].w = [last]
                B_gob.r["pe"] = last
                P.op("dve", I("scalar_tensor_tensor", out=ta, in0=ta, scalar=1.0, in1=psbank(2), op0=ALU.add, op1=ALU.mult),
                     reads=[psb[2]], writes=[B_ta])
                P.op("dve", I("scalar_tensor_tensor", out=tb, in0=tb, scalar=1.0, in1=psbank(3), op0=ALU.add, op1=ALU.mult),
                     reads=[psb[3]], writes=[B_tb])
                P.op("dve", I("tensor_tensor", out=mixedT[:, fc, :], in0=ta, in1=tb, op=ALU.add),
                     reads=[B_ta, B_tb], writes=[B_mixed] if fc == 0 else [])
                if fc:
                    B_mixed.w.append(P.ops["dve"][-1])
        so = [ring_load(woscr[:, :, hf * 512:(hf + 1) * 512].rearrange("k p n -> p k n")) for hf in range(2)]
        for tq in range(4):
            fs = fcnt[0] % 2
            fcnt[0] += 1
            fb = (4, 5) if fs == 0 else (2, 3)
            sm2 = smf2[:, fs * 4:fs * 4 + 4]
            for hf in range(2):
                wblk = Wring[so[hf]].rearrange("p (k n) -> p k n", k=8)
                bank = fb[hf]
                for k in range(8):
                    P.op("pe", I("matmul", psbank(bank), lhsT=mixedT[:, k, tq * 128:(tq + 1) * 128], rhs=wblk[:, k, :],
                                 start=(k == 0), stop=(k == 7)),
                         reads=[B_ring[so[hf]], B_mixed] if k == 0 else [], writes=[psb[bank]] if k == 0 else [])
                last = P.ops["pe"][-1]
                psb[bank].w = [last]
                B_ring[so[hf]].r["pe"] = last
                B_mixed.r["pe"] = last
            P.op("act", I("activation", out=ttile, in_=psbank(fb[0]), func=AF.Square, accum_out=sm2[:, 0:1]),
                 reads=[psb[fb[0]]], writes=[B_t, B_smf2[fs]])
            P.op("act", I("activation", out=ttile, in_=psbank(fb[1]), func=AF.Square, accum_out=sm2[:, 1:2]),
                 reads=[psb[fb[1]]], writes=[B_t, B_smf2[fs]])
            P.op("dve", I("tensor_tensor", out=sm2[:, 2:3], in0=sm2[:, 0:1], in1=sm2[:, 1:2], op=ALU.add),
                 reads=[B_smf2[fs]], writes=[B_smf2[fs]])
            P.op("dve", I("tensor_scalar", out=sm2[:, 2:3], in0=sm2[:, 2:3], scalar1=1.0 / D, scalar2=EPS, op0=ALU.mult, op1=ALU.add),
                 reads=[B_smf2[fs]], writes=[B_smf2[fs]])
            P.op("pool", I("tensor_tensor", out=sm2[:, 3:4], in0=sm2[:, 2:3], in1=nhalf[:, 0:1], op=ALU.pow),
                 reads=[B_smf2[fs], B_cst], writes=[B_smf2[fs]])
            s = xcnt[0] % 2
            xcnt[0] += 1
            xsl = xb_t[:, s * D:(s + 1) * D]
            xsrc = x_d[(4 * g + tq) * 128:(4 * g + tq + 1) * 128, :]
            P.op("sp", I("dma_start", out=xsl, in_=xsrc), writes=[xb[s]], dsem=xb_sem[s])
            for hf, (tmp, B_tmp) in enumerate(((sz, B_sz), (ta, B_ta))):
                P.op("dve", I("scalar_tensor_tensor", out=tmp, in0=psbank(fb[hf]), scalar=sm2[:, 3:4],
                              in1=postg[:, hf * 512:(hf + 1) * 512], op0=ALU.mult, op1=ALU.mult),
                     reads=[psb[fb[hf]], B_smf2[fs], B_postg], writes=[B_tmp])
                P.op("dve", I("tensor_tensor", out=xsl[:, hf * 512:(hf + 1) * 512], in0=xsl[:, hf * 512:(hf + 1) * 512], in1=tmp,
                              op=ALU.add), reads=[B_tmp, xb[s]], writes=[xb[s]] if hf == 0 else [])
                if hf:
                    xb[s].w.append(P.ops["dve"][-1])
            ysl = y_d[(4 * g + tq) * 128:(4 * g + tq + 1) * 128, :]
            P.op("pool", I("dma_start", out=ysl, in_=xsl), reads=[xb[s]], dsem=out_sem[s])
    P.barrier()

    with nc.Block() as block:
        P.flush(block)
    nc._marks = P.marks
    return nc


_NC_CACHE = {}


def host_consts(rel_bias, pre_norm_g, b_in, lambda_q1, lambda_k1, lambda_q2, lambda_k2, subln_g, sink, post_norm_g):
    f = np.float32
    rel_bias = np.asarray(rel_bias, f)
    b = np.asarray(b_in, f).reshape(-1)
    kk = np.arange(128)[:, None]
    v = np.arange(1152)[None, :]
    bucket = t5_bucket_np((512 + kk - v).astype(np.int32))
    strips = np.ascontiguousarray(np.transpose(rel_bias[bucket], (2, 0, 1)))
    cfar = np.ascontiguousarray(np.broadcast_to(np.concatenate([rel_bias[15], rel_bias[31]])[None, :], (128, 24)))
    x384 = np.arange(384)[None, :]
    rel = 128 + kk - x384
    maskb = np.where(np.abs(rel) <= 128, 0.0, -30000.0).astype(f)
    gcol = np.ascontiguousarray(np.asarray(pre_norm_g, f).reshape(8, 128).T)
    bcol = np.ascontiguousarray(b.reshape(42, 128).T)
    bq = b[C_QB:C_QB + 512].reshape(2, 4, 64)
    bqb = np.ascontiguousarray(np.transpose(bq, (1, 0, 2)).reshape(4, 128).T)
    brow = np.ascontiguousarray(np.concatenate([b[C_VA:C_VA + 512], b[C_VB:C_VB + 128]])[None, :])
    lamv = np.concatenate([np.asarray(a, f).reshape(-1) for a in (lambda_q1, lambda_k1, lambda_q2, lambda_k2)])
    lamv = np.ascontiguousarray(np.broadcast_to(lamv[None, :], (128, 256)))
    gsub = np.ascontiguousarray(np.asarray(subln_g, f).reshape(128, 1))
    sinkr = np.ascontiguousarray(np.broadcast_to(np.asarray(sink, f).reshape(1, 8), (128, 8)))
    postg = np.ascontiguousarray(np.broadcast_to(np.asarray(post_norm_g, f).reshape(1, D), (128, D)))
    ident = np.eye(128, dtype=f)
    return dict(strips=strips, cfar=cfar, maskb=maskb, gcol=gcol, bcol=bcol, bqb=bqb, brow=brow, lamv=lamv, gsub=gsub,
                sinkr=sinkr, postg=postg, ident=ident)


def kernel(x, meta_tokens, rel_bias, pre_norm_g, w_in, b_in, lambda_q1, lambda_k1, lambda_q2, lambda_k2,
           subln_g, sink, w_out_a, w_out_b, w_out, post_norm_g):
    x = np.asarray(x, np.float32)
    B, S, _ = x.shape
    NT = S // 128
    if NT not in _NC_CACHE:
        _NC_CACHE[NT] = build(NT)
    nc = _NC_CACHE[NT]
    consts = host_consts(rel_bias, pre_norm_g, b_in, lambda_q1, lambda_k1, lambda_q2, lambda_k2, subln_g, sink, post_norm_g)
    meta_pad = np.zeros((128, D), np.float32)
    meta_pad[128 - NMETA:] = np.asarray(meta_tokens, np.float32)
    shared = dict(meta_pad=meta_pad, w_in=np.ascontiguousarray(np.asarray(w_in, np.float32)[0]),
                  w_out_a=np.ascontiguousarray(np.asarray(w_out_a, np.float32)[0]),
                  w_out_b=np.ascontiguousarray(np.asarray(w_out_b, np.float32)[0]),
                  w_out=np.ascontiguousarray(np.asarray(w_out, np.float32)[0]), **consts)
    in_maps = [dict(x=np.ascontiguousarray(x[i]), **shared) for i in range(B)]
    res = run_bass_kernel_spmd(nc, in_maps, core_ids=list(range(B)))
    return np.stack([np.asarray(r["y"], np.float32) for r in res.results], axis=0)
```

```python
import math
import numpy as np
import concourse.bass as bass
import concourse.mybir as mybir
from concourse.bass_utils import run_bass_kernel_spmd

F32 = mybir.dt.float32
BF16 = mybir.dt.bfloat16
AF = mybir.ActivationFunctionType
ALU = mybir.AluOpType
AX = mybir.AxisListType

D = 1024
DIN = 5376
EPS = 1e-6
NMETA = 16
ENGS = ("sp", "act", "dve", "pool", "pe")
C_QA, C_KA, C_VA, C_ZA, C_QB, C_KB, C_VB, C_ZB, C_GA, C_GB = 0, 512, 1024, 1536, 2048, 2560, 2688, 2816, 3328, 4352
LAM_INIT = 0.8 - 0.6 * math.exp(-0.3 * 0)


class Sem:
    def __init__(self, nc, name, group=False):
        self.h = nc.alloc_semaphore(name)
        self.name = name
        self.v = 0
        self.group = group


class Op:
    __slots__ = ("eng", "fn", "deps", "dsem", "sem", "val", "need", "idx")

    def __init__(self, eng, fn, dsem):
        self.eng, self.fn, self.dsem = eng, fn, dsem
        self.deps = []
        self.sem = None
        self.val = 0
        self.need = False


def I(method, *args, **kw):
    return (method, args, kw)


class Buf:
    __slots__ = ("w", "r", "name")

    def __init__(self, name=""):
        self.name = name
        self.w = []
        self.r = {}


class Prog:
    def __init__(self, nc):
        self.nc = nc
        self.ops = {e: [] for e in ENGS}
        self.esem = {e: Sem(nc, "es_" + e) for e in ENGS}
        self.bufs = []
        self.dma_since_barrier = []
        self.marks = []

    def buf(self, name=""):
        b = Buf(name)
        self.bufs.append(b)
        return b

    def bufs_n(self, n, name=""):
        return [self.buf(name + str(i)) for i in range(n)]

    def op(self, eng, fn, reads=(), writes=(), dsem=None, after=()):
        o = Op(eng, fn, dsem)
        deps = {}
        for b in reads:
            for d in b.w:
                deps[id(d)] = d
        for b in writes:
            for d in b.w:
                deps[id(d)] = d
            for d in b.r.values():
                deps[id(d)] = d
        for d in after:
            deps[id(d)] = d
        for d in deps.values():
            if d.eng == "pe" and eng == "pe" and d.dsem is None:
                continue
            if d is o:
                continue
            o.deps.append(d)
            d.need = True
        for b in writes:
            b.w = [o]
            b.r = {}
        for b in reads:
            if b in writes:
                continue
            key = eng if dsem is None else ("dma", id(dsem))
            b.r[key] = o
        self.ops[eng].append(o)
        if dsem is not None:
            self.dma_since_barrier.append(o)
        return o

    def barrier(self):
        self.marks.append(sum(1 for o in self.ops["pe"] if o.fn is not None))
        lasts = []
        for e in ENGS:
            for o in reversed(self.ops[e]):
                if o.fn is not None and o.dsem is None:
                    lasts.append(o)
                    break
        deps = lasts + self.dma_since_barrier
        self.dma_since_barrier = []
        for e in ENGS:
            self.op(e, None, after=deps)
        for b in self.bufs:
            b.w = []
            b.r = {}

    def flush(self, block):
        for e in ENGS:
            for o in self.ops[e]:
                if o.fn is None:
                    continue
                if o.dsem is not None:
                    o.dsem.v += 16
                    o.sem, o.val = o.dsem, o.dsem.v
                elif o.need:
                    s = self.esem[e]
                    s.v += 1
                    o.sem, o.val = s, s.v
        engmap = {"sp": block.sync, "act": block.scalar, "dve": block.vector, "pool": block.gpsimd, "pe": block.tensor}
        for e in ENGS:
            ops = self.ops[e]

            def body(eng, ops=ops):
                waited = {}
                for o in ops:
                    need = {}
                    for d in o.deps:
                        v = d.sem.v if d.sem.group else d.val
                        if v > need.get(d.sem.name, (0, None))[0]:
                            need[d.sem.name] = (v, d.sem)
                    for nm, (v, sm_) in need.items():
                        if waited.get(nm, 0) >= v:
                            continue
                        waited[nm] = v
                        eng.wait_ge(sm_.h, v)
                    if o.fn is None:
                        continue
                    m, a, kw = o.fn
                    ins = getattr(eng, m)(*a, **kw)
                    if o.dsem is not None:
                        ins.then_inc(o.sem.h, 16)
                    elif o.need:
                        ins.then_inc(o.sem.h, 1)

            engmap[e](body)


def t5_bucket_np(rel):
    half = 16
    max_exact = 8
    ret = np.where(rel > 0, half, 0)
    n = np.abs(rel)
    nf = np.maximum(n, 1).astype(np.float32)
    large = max_exact + (np.log(nf / max_exact) / math.log(128 / max_exact) * (half - max_exact)).astype(np.int32)
    large = np.minimum(large, half - 1)
    return ret + np.where(n < max_exact, n, large)


def build(NT):
    G = NT // 4
    NB = NT + 1
    nc = bass.Bass("TRN2", target_bir_lowering=False)
    P = Prog(nc)

    def din(name, shape, dt=F32):
        return nc.dram_tensor(name, shape, dt, kind="ExternalInput").ap()

    x_d = din("x", [NT * 128, D])
    meta_d = din("meta_pad", [128, D])
    win_d = din("w_in", [D, DIN])
    woa_d = din("w_out_a", [512, D])
    wob_d = din("w_out_b", [512, D])
    wo_d = din("w_out", [D, D])
    gcol_d = din("gcol", [128, 8])
    bcol_d = din("bcol", [128, 42])
    bqb_d = din("bqb", [128, 4])
    brow_d = din("brow", [1, 640])
    strips_d = din("strips", [12, 128, 1152])
    cfar_d = din("cfar", [128, 24])
    maskb_d = din("maskb", [128, 384])
    lamv_d = din("lamv", [128, 256])
    gsub_d = din("gsub", [128, 1])
    sink_d = din("sinkr", [128, 8])
    postg_d = din("postg", [128, D])
    ident_d = din("ident", [128, 128])
    y_d = nc.dram_tensor("y", [NT * 128, D], F32, kind="ExternalOutput").ap()
    wscr = nc.dram_tensor("wscr", [8, 128, DIN], BF16, kind="Internal").ap()
    woascr = nc.dram_tensor("woascr", [4, 128, D], BF16, kind="Internal").ap()
    wobscr = nc.dram_tensor("wobscr", [4, 128, D], BF16, kind="Internal").ap()
    woscr = nc.dram_tensor("woscr", [8, 128, D], BF16, kind="Internal").ap()
    hscr = nc.dram_tensor("hscr", [G + 1, 128, 4096], BF16, kind="Internal").ap()

    KVW = 2 * NB * 128 + NB * 258 + 64
    KVW += KVW % 2
    KVW = max(KVW, NB * 128 + NB * 130 + 2 + 2 * 8300)
    kv_t = nc.alloc_sbuf_tensor("kvarea", [128, KVW], BF16)
    oat_t = nc.alloc_sbuf_tensor("oat", [128, 4, NT * 128], BF16)
    xb_t = nc.alloc_sbuf_tensor("xb", [128, 2 * D], F32)
    xs_t = nc.alloc_sbuf_tensor("xs", [128, 2, D], BF16)
    hn_t = nc.alloc_sbuf_tensor("hnT", [128, 8, 512], BF16)
    cst_t = nc.alloc_sbuf_tensor("cst", [128, 512], F32)
    cbf_t = nc.alloc_sbuf_tensor("cbf", [128, 1024], BF16)
    ARENA_F32 = 12 * 1024 + 512
    ar_t = nc.alloc_sbuf_tensor("arena", [128, ARENA_F32], F32)
    ps_t = nc.alloc_psum_tensor("psf", [128, 7 * 512], F32)
    pst_t = nc.alloc_psum_tensor("pst", [128, 1024], BF16)

    class Arena:
        def __init__(self, ap_f32_2d, nwords):
            self.t = ap_f32_2d
            self.n = nwords
            self.pos = 0

        def reset(self):
            self.pos = 0

        def f32(self, n):
            a = self.t[:, self.pos:self.pos + n]
            self.pos += n
            assert self.pos <= self.n, ("arena overflow", self.pos, self.n)
            return a

        def bf16(self, n):
            n2 = (n + 1) // 2
            return self.f32(n2).bitcast(BF16)[:, 0:n]

    arena = Arena(ar_t[:, :], ARENA_F32)

    o_ = [0]

    def cslot(n):
        a = cst_t[:, o_[0]:o_[0] + n]
        o_[0] += n
        return a

    gcol = cslot(8)
    bcol = cslot(42)
    bhalf = cslot(42)
    bqb = cslot(4)
    cfar = cslot(24)
    zcol = cslot(1)
    gsub8 = cslot(1)
    esink = cslot(8)
    lamt = cslot(8)
    neglam = cslot(1)
    ssq = cslot(2)
    rstd = cslot(2)
    nhalf = cslot(4)
    assert o_[0] <= 512
    ident = cbf_t[:, 0:128]
    ones_row = cbf_t[0:1, 128:256]
    brow = cbf_t[0:1, 256:896]

    B_cst = P.buf("cst")
    xb = P.bufs_n(2, "xb")
    xsb = P.bufs_n(2, "xs")
    xb_sem = [Sem(nc, "xbs%d" % i) for i in range(2)]
    xs_sem = [Sem(nc, "xss%d" % i) for i in range(2)]
    B_hn = P.bufs_n(4, "hn")
    B_hn2 = P.bufs_n(4, "hn2")
    hn_views = [hn_t[:, :, :], xb_t[:, :].bitcast(BF16).rearrange("p (k n) -> p k n", k=8)]
    B_hnsets = [B_hn, B_hn2]
    hn_sem = [Sem(nc, "hns0"), Sem(nc, "hns1")]
    hst_sem = Sem(nc, "hst")
    bst_sem = Sem(nc, "bst")
    B_ssq = P.bufs_n(2, "ssq")
    B_rstd = P.bufs_n(2, "rstd")
    psb = P.bufs_n(7, "ps")
    B_pst = P.buf("pst")
    B_kt = P.buf("kt")
    B_va = P.buf("va")
    B_oat = P.buf("oat")
    setup_sem = Sem(nc, "setup", group=True)
    ld_sem = Sem(nc, "ld")
    ld_sem.group = False

    def psbank(b, n=512):
        return ps_t[:, b * 512:b * 512 + n]

    arena.reset()
    stage_small = arena.f32(1152)
    B_stage_small = P.buf("stage_small")
    lamv = arena.f32(256)
    sinkr = arena.f32(8)
    gsubr = arena.f32(1)
    identf = arena.f32(128)
    browf = arena.f32(640)

    setup_loads = []
    for (o, i) in [(gcol, gcol_d[:, :]), (bcol, bcol_d[:, :]), (bqb, bqb_d[:, :]), (cfar, cfar_d[:, :]),
                   (lamv, lamv_d[:, :]), (sinkr, sink_d[:, :]), (gsubr, gsub_d[:, :]), (identf, ident_d[:, :]),
                   (browf[0:1, :], brow_d[:, :])]:
        op = Op("sp", (I("dma_start", out=o, in_=i)), setup_sem)
        P.ops["sp"].append(op)
        P.dma_since_barrier.append(op)
        setup_loads.append(op)

    def cop(eng, fn):
        return P.op(eng, fn, reads=[], writes=[B_cst], after=setup_loads)

    cop("dve", I("memset", zcol, 0.0))
    cop("dve", I("memset", nhalf, -0.5))
    cop("dve", I("tensor_scalar", out=bhalf, in0=bcol, scalar1=0.5, scalar2=None, op0=ALU.mult))
    cop("dve", I("tensor_scalar", out=gsub8, in0=gsubr, scalar1=(1.0 - LAM_INIT), scalar2=None, op0=ALU.mult))
    cop("dve", I("tensor_copy", out=ident, in_=identf))
    cop("dve", I("memset", ones_row, 1.0))
    cop("dve", I("tensor_copy", out=brow, in_=browf[0:1, :]))
    prod = arena.f32(128)
    cop("dve", I("tensor_tensor", out=prod[:, 0:64], in0=lamv[:, 0:64], in1=lamv[:, 64:128], op=ALU.mult))
    cop("dve", I("tensor_tensor", out=prod[:, 64:128], in0=lamv[:, 128:192], in1=lamv[:, 192:256], op=ALU.mult))
    cop("dve", I("reduce_sum", out=lamt[:, 0:1], in_=prod[:, 0:64], axis=AX.X))
    cop("dve", I("reduce_sum", out=lamt[:, 1:2], in_=prod[:, 64:128], axis=AX.X))
    cop("act", I("activation", out=lamt[:, 2:4], in_=lamt[:, 0:2], func=AF.Exp))
    cop("act", I("activation", out=esink, in_=sinkr, func=AF.Exp))
    cop("dve", I("tensor_tensor", out=lamt[:, 4:5], in0=lamt[:, 3:4], in1=lamt[:, 2:3], op=ALU.subtract))
    cop("dve", I("tensor_scalar", out=neglam, in0=lamt[:, 4:5], scalar1=-LAM_INIT, scalar2=None, op0=ALU.add))

    cvt_i = [0]

    def convert(src_ap, dst_ap, n, scale_ap=None, scale_f=None, perm_qb=False):
        s = cvt_i[0] % 2
        cvt_i[0] += 1
        st = xb_t[:, s * D:s * D + n]
        ot = xs_t[:, s, 0:n]
        P.op("sp", I("dma_start", out=st, in_=src_ap), writes=[xb[s]], dsem=xb_sem[s])
        if perm_qb:
            src_v = xb_t[:, s * D:s * D + 512].rearrange("p (s j d) -> p s j d", s=2, j=4, d=64)
            dst_v = xs_t[:, s, 0:512].rearrange("p (j s d) -> p s j d", s=2, j=4, d=64)
            P.op("dve", I("tensor_scalar", out=dst_v, in0=src_v, scalar1=scale_ap, scalar2=None, op0=ALU.mult),
                 reads=[xb[s], B_cst], writes=[xsb[s]])
            st2 = xb_t[:, s * D + 512:s * D + n]
            ot2 = xs_t[:, s, 512:n]
            o2 = P.op("dve", I("tensor_scalar", out=ot2, in0=st2, scalar1=scale_ap, scalar2=None, op0=ALU.mult),
                      reads=[xb[s], B_cst], writes=[])
            P.op("pool", I("dma_start", out=dst_ap, in_=ot), reads=[xsb[s]], dsem=xs_sem[s], after=[o2])
            xsb[s].r[("dve2")] = o2
        else:
            sc = scale_ap if scale_ap is not None else scale_f
            P.op("dve", I("tensor_scalar", out=ot, in0=st, scalar1=sc, scalar2=None, op0=ALU.mult),
                 reads=[xb[s], B_cst], writes=[xsb[s]])
            P.op("pool", I("dma_start", out=dst_ap, in_=ot), reads=[xsb[s]], dsem=xs_sem[s])

    for k in range(8):
        for cb in range(6):
            c0 = cb * 1024
            n = min(1024, DIN - c0)
            convert(win_d[k * 128:(k + 1) * 128, c0:c0 + n], wscr[k, :, c0:c0 + n], n,
                    scale_ap=gcol[:, k:k + 1], perm_qb=(cb == 2))
    for k in range(4):
        convert(woa_d[k * 128:(k + 1) * 128, :], woascr[k, :, :], 1024, scale_f=0.5)
    for k in range(4):
        convert(wob_d[k * 128:(k + 1) * 128, :], wobscr[k, :, :], 1024, scale_f=0.5)
    for k in range(8):
        convert(wo_d[k * 128:(k + 1) * 128, :], woscr[k, :, :], 1024, scale_f=0.5)
    P.barrier()

    xcnt = [0]

    junk = oat_t[:, 1, 0:1024]
    B_junk = P.buf("junk")
    HN2OK = NT * 128 >= 4096
    hnk = oat_t[:, 0, 0:4096].rearrange("p (k n) -> p k n", k=8) if HN2OK else None
    B_hnk = P.bufs_n(4, "hnk")

    def prep_tile(src_ap, col, HNd, BHNd):
        s = xcnt[0] % 2
        xcnt[0] += 1
        xt = xb_t[:, s * D:(s + 1) * D]
        xo = xs_t[:, s, :]
        P.op("sp", I("dma_start", out=xt, in_=src_ap), writes=[xb[s]], dsem=xb_sem[s])
        P.op("act", I("activation", out=junk, in_=xt, func=AF.Square, accum_out=ssq[:, s:s + 1]),
             reads=[xb[s]], writes=[B_junk, B_ssq[s]])
        P.op("dve", I("tensor_scalar", out=rstd[:, s:s + 1], in0=ssq[:, s:s + 1], scalar1=1.0 / D, scalar2=EPS,
                                              op0=ALU.mult, op1=ALU.add), reads=[B_ssq[s]], writes=[B_rstd[s]])
        P.op("pool", I("tensor_tensor", out=rstd[:, s:s + 1], in0=rstd[:, s:s + 1], in1=nhalf[:, 0:1], op=ALU.pow),
             reads=[B_rstd[s], B_cst], writes=[B_rstd[s]])
        P.op("dve", I("tensor_scalar", out=xo, in0=xt, scalar1=rstd[:, s:s + 1], scalar2=None, op0=ALU.mult),
             reads=[xb[s], B_rstd[s]], writes=[xsb[s]])
        for k in range(8):
            P.op("pe", I("transpose", out=pst_t[:, k * 128:(k + 1) * 128], in_=xs_t[:, s, k * 128:(k + 1) * 128],
                                                  identity=ident),
                 reads=[xsb[s]] if k else [xsb[s], B_cst], writes=[B_pst] if k == 0 else [])
        lastT = P.ops["pe"][-1]
        B_pst.w = [lastT]
        xsb[s].r["pe"] = lastT
        P.op("dve", I("tensor_copy", out=HNd[:, :, col * 128:(col + 1) * 128],
                                            in_=pst_t[:, :].rearrange("p (k t) -> p k t", k=8)),
             reads=[B_pst], writes=[BHNd[col]])

    def tile_src(c):
        return meta_d[:, :] if c == 0 else x_d[(c - 1) * 128:c * 128, :]

    def groups_kv():
        yield [0]
        for g in range(G):
            yield [1 + 4 * g + j for j in range(4)]

    glist = list(groups_kv())

    def hn_store(gi, nt, HNs, BHNs):
        P.op("pool", I("dma_start", out=hscr[gi].rearrange("p (k n) -> p k n", k=8)[:, :, 0:nt * 128], in_=HNs[:, :, 0:nt * 128]),
             reads=BHNs[0:nt], dsem=hst_sem)

    def hn_load(gi, nt, which):
        P.op("sp", I("dma_start", out=hn_views[which][:, :, 0:nt * 128],
                     in_=hscr[gi].rearrange("p (k n) -> p k n", k=8)[:, :, 0:nt * 128]),
             writes=B_hnsets[which], dsem=hn_sem[which])
        return hn_views[which], B_hnsets[which]

    def hn_iter(first):
        if first:
            for gi, blocks in enumerate(glist):
                HNd, BHNd = (hnk, B_hnk) if (HN2OK and gi % 2 == 1) else (hn_views[0], B_hn)
                for j, c in enumerate(blocks):
                    prep_tile(tile_src(c), j, HNd, BHNd)
                hn_store(gi, len(blocks), HNd, BHNd)
                yield blocks, HNd, BHNd
        else:
            nxt = hn_load(0, len(glist[0]), 0)
            for gi, blocks in enumerate(glist):
                cur = nxt
                if gi + 1 < len(glist):
                    nxt = hn_load(gi + 1, len(glist[gi + 1]), (gi + 1) % 2)
                yield blocks, cur[0], cur[1]

    def load_w(dst_ap, src_ap, buf, sem):
        return P.op("sp", I("dma_start", out=dst_ap, in_=src_ap), writes=[buf], dsem=sem)

    wsem = [Sem(nc, "wsem%d" % i) for i in range(8)]

    for p in range(2):
        arena.reset()
        KT = kv_t[:, 0:2 * NB * 128].rearrange("p (h n) -> p h n", h=2)
        VA = kv_t[:, 2 * NB * 128:2 * NB * 128 + NB * 258].rearrange("p (c n) -> p c n", n=258)
        wk = arena.bf16(8 * 256).rearrange("p (k n) -> p k n", k=8)
        wv = arena.bf16(8 * 256).rearrange("p (k n) -> p k n", k=8)
        B_wk, B_wv = P.buf("wk"), P.buf("wv")
        load_w(wk, wscr[:, :, C_KA + p * 256:C_KA + (p + 1) * 256].rearrange("k p n -> p k n"), B_wk, wsem[0])
        load_w(wv, wscr[:, :, C_VA + p * 256:C_VA + (p + 1) * 256].rearrange("k p n -> p k n"), B_wv, wsem[1])
        B_vcol = P.buf("vcol")
        P.op("pool", I("memset", VA[:, :, 128:129], 1.0), writes=[B_vcol])
        P.op("pool", I("memset", VA[:, :, 257:258], 1.0), writes=[B_vcol])
        B_ktc = {}
        B_vac = {}
        for blocks, HN, BHN in hn_iter(p == 0):
            nt = len(blocks)
            c0 = blocks[0]
            for hh in range(2):
                bank = hh
                for k in range(8):
                    P.op("pe", I("matmul",
                        psbank(bank, nt * 128), lhsT=wk[:, k, hh * 128:(hh + 1) * 128], rhs=HN[:, k, 0:nt * 128],
                        start=(k == 0), stop=(k == 7)),
                        reads=([B_wk] + BHN[0:nt]) if k == 0 else [], writes=[psb[bank]] if k == 0 else [])
                last = P.ops["pe"][-1]
                psb[bank].w = [last]
                for b in BHN[0:nt]:
                    b.r["pe"] = last
                kb = P.buf("ktc")
                B_ktc[(blocks[0], hh)] = kb
                ch = (C_KA + p * 256) // 128 + hh
                P.op("dve", I("tensor_scalar",
                    out=KT[:, hh, c0 * 128:(c0 + nt) * 128], in0=psbank(bank, nt * 128), scalar1=bcol[:, ch:ch + 1],
                    scalar2=None, op0=ALU.add), reads=[psb[bank], B_cst], writes=[kb])
            for j, c in enumerate(blocks):
                bank = 2 + (j % 4)
                for k in range(8):
                    P.op("pe", I("matmul",
                        psbank(bank, 256), lhsT=HN[:, k, j * 128:(j + 1) * 128], rhs=wv[:, k, :],
                        start=(k == 0), stop=False),
                        reads=([B_wv, BHN[j]]) if k == 0 else [], writes=[psb[bank]] if k == 0 else [])
                P.op("pe", I("matmul",
                    psbank(bank, 256), lhsT=ones_row, rhs=brow[0:1, p * 256:(p + 1) * 256], start=False, stop=True),
                    reads=[B_cst])
                last = P.ops["pe"][-1]
                psb[bank].w = [last]
                BHN[j].r["pe"] = last
                vb = P.buf("vac")
                B_vac[c] = vb
                P.op("act", I("activation",
                    out=VA[:, c, :].rearrange("p (h n) -> p h n", h=2)[:, :, 0:128],
                    in_=psbank(bank, 256).rearrange("p (h n) -> p h n", h=2), func=AF.Copy),
                    reads=[psb[bank]], writes=[vb])
                if c == 0:
                    P.op("pool", I("memset", VA[0:128 - NMETA, 0, :], 0.0), reads=[], writes=[vb], after=list(B_vcol.w))
        P.barrier()

        arena.reset()
        wq = arena.bf16(8 * 256).rearrange("p (k n) -> p k n", k=8)
        strips = arena.bf16(2 * 1152).rearrange("p (h n) -> p h n", h=2)
        QT = arena.bf16(2 * 2 * 512).rearrange("p (s h n) -> p s h n", s=2, h=2)
        ET = arena.bf16(3 * 1024).rearrange("p (s n) -> p s n", s=3)
        accs = arena.f32(8 * 129).rearrange("p (a n) -> p a n", a=8)
        ofin = arena.f32(4 * 128).rearrange("p (j n) -> p j n", j=4)
        t1 = arena.f32(128)
        sq = arena.f32(128)
        onb = arena.bf16(4 * 128).rearrange("p (j n) -> p j n", j=4)
        sm = arena.f32(32)
        sstage = arena.f32(1152)
        B_wq, B_strips = P.buf("wq"), P.buf("strips")
        B_QT = P.bufs_n(2, "QT")
        B_ET = P.bufs_n(3, "ET")
        B_accs, B_fin, B_on, B_sstage = P.buf("accs"), P.buf("fin"), P.buf("on"), P.buf("sstage")
        load_w(wq, wscr[:, :, C_QA + p * 256:C_QA + (p + 1) * 256].rearrange("k p n -> p k n"), B_wq, wsem[0])
        for hh in range(2):
            P.op("sp", I("dma_start", out=sstage, in_=strips_d[2 * p + hh, :, :]), writes=[B_sstage], dsem=wsem[2])
            P.op("dve", I("tensor_copy", out=strips[:, hh, :], in_=sstage), reads=[B_sstage], writes=[B_strips])
        ucnt = [0]
        ecnt = [0]
        pstf = pst_t[:, :].bitcast(F32)
        deferred = []

        def defer(n, tag, fn):
            deferred.append([n, tag, fn])

        def tick():
            for d in deferred:
                d[0] -= 1
            due = [d for d in deferred if d[0] <= 0]
            for d in due:
                deferred.remove(d)
                d[2]()

        def flush_deferred(tag=None):
            due = [d for d in deferred if tag is None or d[1] == tag]
            for d in due:
                deferred.remove(d)
                d[2]()

        def qproj_mm(hh, k0, k1, HNq, BHNq):
            for k in range(k0, k1):
                P.op("pe", I("matmul", pstf, lhsT=wq[:, k, hh * 128:(hh + 1) * 128], rhs=HNq[:, k, :],
                             start=(k == 0), stop=(k == 7)),
                     reads=([B_wq] + BHNq) if k == 0 else [], writes=[B_pst] if k == 0 else [])
            if k1 == 8:
                last = P.ops["pe"][-1]
                B_pst.w = [last]
                for b in BHNq:
                    b.r["pe"] = last

        def qproj_evac(hh, qsl):
            ch = (C_QA + p * 256) // 128 + hh
            P.op("dve", I("tensor_scalar", out=QT[:, qsl, hh, :], in0=pstf, scalar1=bcol[:, ch:ch + 1], scalar2=0.125,
                          op0=ALU.add, op1=ALU.mult), reads=[B_pst, B_cst], writes=[B_QT[qsl]] if hh == 0 else [])
            if hh == 1:
                B_QT[qsl].w.append(P.ops["dve"][-1])

        def qproj_schedule(gq, HNq, BHNq, now):
            qsl = gq % 2
            t = 22
            for hh in range(2):
                for k0 in (0, 2, 4, 6):
                    fn = (lambda hh=hh, k0=k0: qproj_mm(hh, k0, k0 + 2, HNq, BHNq))
                    if now:
                        fn()
                    else:
                        defer(t, "q", fn)
                    t += 1
                fn = (lambda hh=hh: qproj_evac(hh, qsl))
                if now:
                    fn()
                else:
                    defer(t, "q", fn)
                t += 2

        def B_ktc_get(c, hh):
            for (c0, h2), b in B_ktc.items():
                if h2 == hh and (c0 == c or (c0 >= 1 and c0 <= c < c0 + 4 and c >= 1)):
                    return b
            raise KeyError((c, hh))

        def emit_scores(c, g, hh, h, qs, b0):
            slot = ucnt[0] % 2
            ucnt[0] += 1
            bk = 2 * slot
            Dd = c - b0
            mixed = (-1 <= Dd <= 4)
            P.op("pe", I("matmul", psbank(bk), lhsT=KT[0:64, hh, c * 128:(c + 1) * 128],
                         rhs=QT[0:64, qs, hh, :], start=True, stop=not mixed),
                 reads=[B_ktc_get(c, hh), B_QT[qs]], writes=[psb[bk], psb[bk + 1]])
            P.op("pe", I("matmul", psbank(bk + 1), lhsT=KT[64:128, hh, c * 128:(c + 1) * 128],
                         rhs=QT[64:128, qs, hh, :], start=True, stop=not mixed))
            if mixed:
                o0 = (4 - Dd) * 128
                P.op("pe", I("matmul", psbank(bk), lhsT=ident, rhs=strips[:, hh, o0:o0 + 512],
                             start=False, stop=True), reads=[B_strips, B_cst])
                P.op("pe", I("matmul", psbank(bk + 1), lhsT=ident, rhs=strips[:, hh, o0:o0 + 512],
                             start=False, stop=True))
            last = P.ops["pe"][-1]
            psb[bk].w = [last]
            psb[bk + 1].w = [last]
            B_QT[qs].r["pe"] = last
            es = ecnt[0] % 3
            ecnt[0] += 1
            if mixed:
                bias_ap = zcol
            elif Dd <= -2:
                bias_ap = cfar[:, h:h + 1]
            else:
                bias_ap = cfar[:, 12 + h:13 + h]
            P.op("act", I("activation", out=ET[:, es, :], in_=ps_t[:, bk * 512:bk * 512 + 1024], func=AF.Exp,
                          bias=bias_ap, scale=1.0),
                 reads=[psb[bk], psb[bk + 1], B_cst], writes=[B_ET[es]])
            return (c, es)

        def emit_av(c, es, hh):
            first = (c == 0)
            lastc = (c == NB - 1)
            for j in range(4):
                for m in range(2):
                    a = 2 * j + m
                    bank = 4 + a // 3
                    off = (a % 3) * 129
                    P.op("pe", I("matmul", ps_t[:, bank * 512 + off:bank * 512 + off + 129],
                                 lhsT=ET[:, es, m * 512 + j * 128:m * 512 + (j + 1) * 128],
                                 rhs=VA[:, c, hh * 129:hh * 129 + 129], start=(first and a % 3 == 0), stop=lastc,
                                 skip_group_check=True),
                         reads=([B_ET[es], B_vac[c]]) if a == 0 else [],
                         writes=[psb[4], psb[5], psb[6]] if (first and a == 0) else [])
            last = P.ops["pe"][-1]
            B_ET[es].r["pe"] = last
            if lastc:
                for b in (psb[4], psb[5], psb[6]):
                    b.w = [last]

        def finalize_part1():
            for bi, na in ((4, 3), (5, 3), (6, 2)):
                a0 = (bi - 4) * 3
                P.op("dve", I("tensor_copy", out=accs[:, a0:a0 + na, :],
                              in_=psbank(bi, na * 129).rearrange("p (a n) -> p a n", a=na)),
                     reads=[psb[bi]], writes=[B_accs] if bi == 4 else [])
                if bi != 4:
                    B_accs.w.append(P.ops["dve"][-1])

            def fin(fn):
                return P.op("dve", fn, reads=[B_accs, B_fin, B_cst], writes=[B_fin])

            fin(I("reciprocal", out=sm[:, 0:8], in_=accs[:, :, 128:129].rearrange("p a o -> p (a o)")))
            fin(I("tensor_scalar", out=sm[:, 8:12], in0=sm[:, 0:8].rearrange("p (j m) -> p j m", m=2)[:, :, 1],
                  scalar1=neglam, scalar2=None, op0=ALU.mult))
            for j in range(4):
                fin(I("tensor_scalar", out=t1, in0=accs[:, 2 * j + 1, 0:128], scalar1=sm[:, 8 + j:9 + j],
                      scalar2=None, op0=ALU.mult))
                fin(I("scalar_tensor_tensor", out=ofin[:, j, :], in0=accs[:, 2 * j, 0:128],
                      scalar=sm[:, 2 * j:2 * j + 1], in1=t1, op0=ALU.mult, op1=ALU.add))
                fin(I("tensor_tensor", out=sq, in0=ofin[:, j, :], in1=ofin[:, j, :], op=ALU.mult))
                fin(I("reduce_sum", out=sm[:, 12 + j:13 + j], in_=sq, axis=AX.X))
            fin(I("tensor_scalar", out=sm[:, 16:20], in0=sm[:, 12:16], scalar1=1.0 / 128, scalar2=EPS,
                  op0=ALU.mult, op1=ALU.add))
            P.op("pool", I("tensor_tensor", out=sm[:, 16:20], in0=sm[:, 16:20], in1=nhalf[:, 0:4], op=ALU.pow),
                 reads=[B_fin, B_cst], writes=[B_fin])
            for j in range(4):
                P.op("dve", I("tensor_scalar", out=onb[:, j, :], in0=ofin[:, j, :], scalar1=sm[:, 16 + j:17 + j],
                              scalar2=None, op0=ALU.mult),
                     reads=[B_fin], writes=[B_on] if j == 0 else [])
                if j:
                    B_on.w.append(P.ops["dve"][-1])

        def finalize_part2(h, g):
            for j in range(4):
                P.op("pe", I("transpose", out=pst_t[:, j * 128:(j + 1) * 128], in_=onb[:, j, :], identity=ident),
                     reads=[B_on, B_cst] if j == 0 else [], writes=[B_pst] if j == 0 else [])
            lastT = P.ops["pe"][-1]
            B_pst.w = [lastT]
            B_on.r["pe"] = lastT
            P.op("dve", I("tensor_scalar", out=oat_t[:, h, g * 512:(g + 1) * 512], in0=pst_t[:, 0:512],
                          scalar1=gsub8, scalar2=None, op0=ALU.mult),
                 reads=[B_pst, B_cst], writes=[])

        nxt = hn_load(1, 4, 0)
        qproj_schedule(0, nxt[0], nxt[1], now=True)
        for g in range(G):
            qs = g % 2
            flush_deferred("q")
            if g + 1 < G:
                nxt = hn_load(2 + g, 4, (g + 1) % 2)
            b0 = 1 + 4 * g
            for hh in range(2):
                h = 2 * p + hh
                if hh == 1 and g + 1 < G:
                    qproj_schedule(g + 1, nxt[0], nxt[1], now=False)
                pend = [emit_scores(0, g, hh, h, qs, b0)]
                if NB > 1:
                    pend.append(emit_scores(1, g, hh, h, qs, b0))
                for c in range(2, NB):
                    cur = emit_scores(c, g, hh, h, qs, b0)
                    pv = pend.pop(0)
                    emit_av(pv[0], pv[1], hh)
                    tick()
                    pend.append(cur)
                while pend:
                    pv = pend.pop(0)
                    emit_av(pv[0], pv[1], hh)
                    tick()
                flush_deferred("f")
                finalize_part1()
                defer(20, "f", (lambda h=h, g=g: finalize_part2(h, g)))
        flush_deferred()
        P.barrier()

    arena.reset()
    KTB = kv_t[:, 0:NB * 128]
    VB = kv_t[:, NB * 128:NB * 128 + NB * 130].rearrange("p (c n) -> p c n", n=130)
    kvfree0 = NB * 128 + NB * 130
    kvfree0 += kvfree0 % 2
    wkb = arena.bf16(8 * 128).rearrange("p (k n) -> p k n", k=8)
    wvb = arena.bf16(8 * 128).rearrange("p (k n) -> p k n", k=8)
    B_wk, B_wv = P.buf("wkb"), P.buf("wvb")
    load_w(wkb, wscr[:, :, C_KB:C_KB + 128].rearrange("k p n -> p k n"), B_wk, wsem[0])
    load_w(wvb, wscr[:, :, C_VB:C_VB + 128].rearrange("k p n -> p k n"), B_wv, wsem[1])
    B_vcol = P.buf("vcolb")
    P.op("pool", I("memset", VB[:, :, 64:65], 1.0), writes=[B_vcol])
    P.op("pool", I("memset", VB[:, :, 129:130], 1.0), writes=[B_vcol])
    for blocks, HN, BHN in hn_iter(False):
        nt = len(blocks)
        c0 = blocks[0]
        for k in range(8):
            P.op("pe", I("matmul", psbank(0, nt * 128), lhsT=wkb[:, k, :], rhs=HN[:, k, 0:nt * 128],
                                               start=(k == 0), stop=(k == 7)),
                 reads=([B_wk] + BHN[0:nt]) if k == 0 else [], writes=[psb[0]] if k == 0 else [])
        last = P.ops["pe"][-1]
        psb[0].w = [last]
        for b in BHN[0:nt]:
            b.r["pe"] = last
        P.op("dve", I("tensor_scalar",
            out=KTB[:, c0 * 128:(c0 + nt) * 128], in0=psbank(0, nt * 128), scalar1=bcol[:, C_KB // 128:C_KB // 128 + 1],
            scalar2=None, op0=ALU.add), reads=[psb[0], B_cst], writes=[])
        psb[0].r["dve"] = P.ops["dve"][-1]
        for j, c in enumerate(blocks):
            bank = 2 + (j % 4)
            for k in range(8):
                P.op("pe", I("matmul",
                    psbank(bank, 128), lhsT=HN[:, k, j * 128:(j + 1) * 128], rhs=wvb[:, k, :], start=(k == 0), stop=False),
                    reads=([B_wv, BHN[j]]) if k == 0 else [], writes=[psb[bank]] if k == 0 else [])
            P.op("pe", I("matmul", psbank(bank, 128), lhsT=ones_row, rhs=brow[0:1, 512:640],
                                                     start=False, stop=True), reads=[B_cst])
            last = P.ops["pe"][-1]
            psb[bank].w = [last]
            BHN[j].r["pe"] = last
            P.op("act", I("activation",
                out=VB[:, c, :].rearrange("p (h n) -> p h n", h=2)[:, :, 0:64],
                in_=psbank(bank, 128).rearrange("p (h n) -> p h n", h=2), func=AF.Copy),
                reads=[psb[bank]], writes=[])
            psb[bank].r["act"] = P.ops["act"][-1]
            if c == 0:
                P.op("pool", I("memset", VB[0:128 - NMETA, 0, :], 0.0), after=list(B_vcol.w) + [P.ops["act"][-1]])
    P.barrier()

    arena.reset()
    ar2 = Arena(kv_t[:, kvfree0:KVW].bitcast(F32), (KVW - kvfree0) // 2)
    biasB = arena.bf16(8 * 384).rearrange("p (h n) -> p h n", h=8)
    Woa = arena.bf16(4 * 1024).rearrange("p (k n) -> p k n", k=4)
    Wob = arena.bf16(4 * 1024).rearrange("p (k n) -> p k n", k=4)
    Wring = [arena.bf16(4096), arena.bf16(4096)]
    postg = arena.f32(1024)
    ttile = arena.f32(512)
    sz = arena.f32(512)
    ta = arena.f32(512)
    EBs = [ar2.bf16(1536), ar2.bf16(1536)]
    mixedT = ar2.bf16(8 * 512).rearrange("p (k n) -> p k n", k=8)
    goaT = ar2.bf16(4 * 512).rearrange("p (k n) -> p k n", k=4)
    gobT = ar2.bf16(4 * 512).rearrange("p (k n) -> p k n", k=4)
    QTb = ar2.bf16(4 * 512).rearrange("p (k n) -> p k n", k=4)
    obt = ar2.bf16(4 * 512).rearrange("p (t n) -> p t n", t=4)
    tb = ar2.f32(512)
    smf = ar2.f32(16)
    smf2 = ar2.f32(8)
    bstage = ttile[:, 0:384]
    maskb = sz[:, 0:384]
    pstf = pst_t[:, :].bitcast(F32)
    B_biasB, B_Woa, B_Wob, B_postg = P.buf("biasB"), P.buf("Woa"), P.buf("Wob"), P.buf("postg")
    B_ring = P.bufs_n(2, "ring")
    ring_sem = [wsem[3], wsem[4]]
    B_t, B_sz, B_mixed, B_goa, B_gob, B_QTb = (P.buf("t"), P.buf("sz"), P.buf("mixed"), P.buf("goa"), P.buf("gob"), P.buf("QTb"))
    B_ob = P.bufs_n(4, "ob")
    B_EB = P.bufs_n(2, "EB")
    B_ta, B_tb = P.buf("ta"), P.buf("tb")
    B_smfs = P.bufs_n(2, "smf")
    B_smf2 = P.bufs_n(2, "smf2")
    out_sem = [wsem[5], wsem[6]]
    load_w(Woa, woascr.rearrange("k p n -> p k n"), B_Woa, wsem[0])
    load_w(Wob, wobscr.rearrange("k p n -> p k n"), B_Wob, wsem[1])
    load_w(postg, postg_d[:, :], B_postg, wsem[2])
    load_w(maskb, maskb_d[:, :], B_sz, wsem[7])
    for h8 in range(8):
        P.op("sp", I("dma_start", out=bstage, in_=strips_d[4 + h8, :, 384:768]), writes=[B_t], dsem=bst_sem)
        P.op("dve", I("tensor_tensor", out=biasB[:, h8, :], in0=bstage, in1=maskb, op=ALU.add),
             reads=[B_t, B_sz], writes=[B_biasB])
    rcnt = [0]

    def ring_load(src_ap):
        s = rcnt[0] % 2
        rcnt[0] += 1
        load_w(Wring[s].rearrange("p (k n) -> p k n", k=8), src_ap, B_ring[s], ring_sem[s])
        return s

    def proj_fm(wblk, s, col0, bank):
        for k in range(8):
            P.op("pe", I("matmul", psbank(bank), lhsT=wblk[:, k, col0:col0 + 128], rhs=hn_t[:, k, :],
                         start=(k == 0), stop=(k == 7)),
                 reads=([B_ring[s]] + B_hn) if k == 0 else [], writes=[psb[bank]] if k == 0 else [])
        last = P.ops["pe"][-1]
        psb[bank].w = [last]
        B_ring[s].r["pe"] = last
        for b in B_hn:
            b.r["pe"] = last

    ebc = [0]
    fcnt = [0]
    ACCS = [(ps_t[:, 6 * 512:6 * 512 + 260], psb[6]), (pstf[:, 0:260], B_pst)]
    hn_load(1, 4, 0)
    for g in range(G):
        s = ring_load(wscr[:, :, C_QB:C_QB + 512].rearrange("k p n -> p k n"))
        wblk = Wring[s].rearrange("p (k n) -> p k n", k=8)
        for j in range(4):
            bank = j % 2
            proj_fm(wblk, s, j * 128, bank)
            P.op("dve", I("tensor_scalar", out=QTb[:, j, :], in0=psbank(bank), scalar1=bqb[:, j:j + 1],
                          scalar2=0.125, op0=ALU.add, op1=ALU.mult),
                 reads=[psb[bank], B_cst], writes=[B_QTb] if j == 0 else [])
            if j:
                B_QTb.w.append(P.ops["dve"][-1])
        for tq in range(4):
            b = 1 + 4 * g + tq
            cs = [c for c in (b - 1, b, b + 1) if 0 <= c < NB]
            jj0 = cs[0] - (b - 1)
            ncs = len(cs)
            for half in range(2):
                sl = ebc[0] % 2
                ebc[0] += 1
                r0 = half * 64
                h0 = half * 4
                banks = [3 * sl + (c - (b - 1)) for c in cs]
                for i, c in enumerate(cs):
                    jj = c - (b - 1)
                    bk = 3 * sl + jj
                    P.op("pe", I("matmul", psbank(bk), lhsT=KTB[r0:r0 + 64, c * 128:(c + 1) * 128],
                                 rhs=QTb[r0:r0 + 64, :, tq * 128:(tq + 1) * 128], start=True, stop=False),
                         reads=[B_QTb] if i == 0 else [], writes=[psb[x] for x in banks] if i == 0 else [])
                    P.op("pe", I("matmul", psbank(bk), lhsT=ident, rhs=biasB[:, h0:h0 + 4, (2 - jj) * 128:(3 - jj) * 128],
                                 start=False, stop=True), reads=[B_biasB, B_cst] if i == 0 else [])
                last = P.ops["pe"][-1]
                for x in banks:
                    psb[x].w = [last]
                B_QTb.r["pe"] = last
                lo = (3 * sl + jj0) * 512
                P.op("act", I("activation", out=EBs[sl][:, jj0 * 512:(jj0 + ncs) * 512], in_=ps_t[:, lo:lo + ncs * 512], func=AF.Exp),
                     reads=[psb[x] for x in banks], writes=[B_EB[sl]])
                acc_ap, acc_buf = ACCS[sl]
                firstmm = True
                for i4 in range(4):
                    for i, c in enumerate(cs):
                        jj = c - (b - 1)
                        P.op("pe", I("matmul", acc_ap[:, i4 * 65:(i4 + 1) * 65],
                                     lhsT=EBs[sl][:, jj * 512 + i4 * 128:jj * 512 + (i4 + 1) * 128],
                                     rhs=VB[:, c, half * 65:half * 65 + 65], start=firstmm, stop=(i == ncs - 1),
                                     skip_group_check=True),
                             reads=[B_EB[sl]] if firstmm else [], writes=[acc_buf] if firstmm else [])
                        firstmm = False
                last = P.ops["pe"][-1]
                acc_buf.w = [last]
                B_EB[sl].r["pe"] = last
                acc3 = acc_ap.rearrange("p (i n) -> p i n", n=65)
                sms = smf[:, sl * 8:sl * 8 + 8]
                P.op("dve", I("tensor_tensor", out=sms[:, 0:4], in0=acc3[:, :, 64:65].rearrange("p i o -> p (i o)"),
                              in1=esink[:, h0:h0 + 4], op=ALU.add),
                     reads=[acc_buf, B_cst], writes=[B_smfs[sl]])
                P.op("dve", I("reciprocal", out=sms[:, 4:8], in_=sms[:, 0:4]), reads=[B_smfs[sl]], writes=[B_smfs[sl]])
                for i4 in range(4):
                    P.op("dve", I("tensor_scalar", out=obt[:, tq, (h0 + i4) * 64:(h0 + i4 + 1) * 64], in0=acc3[:, i4, 0:64],
                                  scalar1=sms[:, 4 + i4:5 + i4], scalar2=None, op0=ALU.mult),
                         reads=[acc_buf, B_smfs[sl]], writes=[B_ob[tq]] if (half == 0 and i4 == 0) else [])
                    if half or i4:
                        B_ob[tq].w.append(P.ops["dve"][-1])

        def gate(col_base, which):
            s = ring_load(wscr[:, :, col_base:col_base + 512].rearrange("k p n -> p k n"))
            wblk = Wring[s].rearrange("p (k n) -> p k n", k=8)
            for c in range(4):
                bank = c % 2
                ch = col_base // 128 + c
                proj_fm(wblk, s, c * 128, bank)
                P.op("act", I("activation", out=ttile, in_=psbank(bank), func=AF.Tanh, bias=bhalf[:, ch:ch + 1], scale=0.5),
                     reads=[psb[bank], B_cst], writes=[B_t])
                P.op("dve", I("scalar_tensor_tensor", out=sz, in0=psbank(bank), scalar=bcol[:, ch:ch + 1], in1=ttile,
                              op0=ALU.add, op1=ALU.mult), reads=[psb[bank], B_t, B_cst], writes=[B_sz])
                P.op("dve", I("scalar_tensor_tensor", out=sz, in0=psbank(bank), scalar=bcol[:, ch:ch + 1], in1=sz,
                              op0=ALU.add, op1=ALU.add), reads=[psb[bank], B_cst], writes=[B_sz])
                if which == "a":
                    P.op("dve", I("tensor_tensor", out=goaT[:, c, :], in0=oat_t[:, c, g * 512:(g + 1) * 512], in1=sz, op=ALU.mult),
                         reads=[B_sz], writes=[B_goa] if c == 0 else [])
                    if c:
                        B_goa.w.append(P.ops["dve"][-1])
                else:
                    for tq in range(4):
                        P.op("pe", I("transpose", out=pst_t[:, tq * 128:(tq + 1) * 128], in_=obt[:, tq, c * 128:(c + 1) * 128],
                                     identity=ident), reads=[B_ob[tq], B_cst], writes=[B_pst] if tq == 0 else [])
                    lastT = P.ops["pe"][-1]
                    B_pst.w = [lastT]
                    for tq in range(4):
                        B_ob[tq].r["pe"] = lastT
                    P.op("dve", I("tensor_tensor", out=gobT[:, c, :], in0=pst_t[:, 0:512], in1=sz, op=ALU.mult),
                         reads=[B_pst, B_sz], writes=[B_gob] if c == 0 else [])
                    if c:
                        B_gob.w.append(P.ops["dve"][-1])

        gate(C_ZB, "b")
        gate(C_ZA, "a")
        for half in range(2):
            sa = ring_load(wscr[:, :, C_GA + half * 512:C_GA + (half + 1) * 512].rearrange("k p n -> p k n"))
            wa = Wring[sa].rearrange("p (k n) -> p k n", k=8)
            sb = None
            for c4 in range(4):
                fc = half * 4 + c4
                proj_fm(wa, sa, c4 * 128, 0)
                cha = C_GA // 128 + fc
                P.op("act", I("activation", out=ta, in_=psbank(0), func=AF.Tanh, bias=bhalf[:, cha:cha + 1], scale=0.5),
                     reads=[psb[0], B_cst], writes=[B_ta])
                if c4 == 0:
                    sb = ring_load(wscr[:, :, C_GB + half * 512:C_GB + (half + 1) * 512].rearrange("k p n -> p k n"))
                    wb = Wring[sb].rearrange("p (k n) -> p k n", k=8)
                proj_fm(wb, sb, c4 * 128, 1)
                chb = C_GB // 128 + fc
                P.op("act", I("activation", out=tb, in_=psbank(1), func=AF.Tanh, bias=bhalf[:, chb:chb + 1], scale=0.5),
                     reads=[psb[1], B_cst], writes=[B_tb])
                for k in range(4):
                    P.op("pe", I("matmul", psbank(2), lhsT=Woa[:, k, fc * 128:(fc + 1) * 128], rhs=goaT[:, k, :],
                                 start=(k == 0), stop=(k == 3)),
                         reads=[B_Woa, B_goa] if k == 0 else [], writes=[psb[2]] if k == 0 else [])
                last = P.ops["pe"][-1]
                psb[2].w = [last]
                B_goa.r["pe"] = last
                for k in range(4):
                    P.op("pe", I("matmul", psbank(3), lhsT=Wob[:, k, fc * 128:(fc + 1) * 128], rhs=gobT[:, k, :],
                                 start=(k == 0), stop=(k == 3)),
                         reads=[B_Wob, B_gob] if k == 0 else [], writes=[psb[3]] if k == 0 else [])
                last = P.ops["pe"][-1]
                psb[3].w = [last]
                B_gob.r["pe"] = last
                P.op("dve", I("scalar_tensor_tensor", out=ta, in0=ta, scalar=1.0, in1=psbank(2), op0=ALU.add, op1=ALU.mult),
                     reads=[psb[2]], writes=[B_ta])
                P.op("dve", I("scalar_tensor_tensor", out=tb, in0=tb, scalar=1.0, in1=psbank(3), op0=ALU.add, op1=ALU.mult),
                     reads=[psb[3]], writes=[B_tb])
                P.op("dve", I("tensor_tensor", out=mixedT[:, fc, :], in0=ta, in1=tb, op=ALU.add),
                     reads=[B_ta, B_tb], writes=[B_mixed] if fc == 0 else [])
                if fc:
                    B_mixed.w.append(P.ops["dve"][-1])
        if g + 1 < G:
            hn_load(2 + g, 4, 0)
        so = [ring_load(woscr[:, :, hf * 512:(hf + 1) * 512].rearrange("k p n -> p k n")) for hf in range(2)]
        for tq in range(4):
            fs = fcnt[0] % 2
            fcnt[0] += 1
            fb = (4, 5) if fs == 0 else (2, 3)
            sm2 = smf2[:, fs * 4:fs * 4 + 4]
            for hf in range(2):
                wblk = Wring[so[hf]].rearrange("p (k n) -> p k n", k=8)
                bank = fb[hf]
                for k in range(8):
                    P.op("pe", I("matmul", psbank(bank), lhsT=mixedT[:, k, tq * 128:(tq + 1) * 128], rhs=wblk[:, k, :],
                                 start=(k == 0), stop=(k == 7)),
                         reads=[B_ring[so[hf]], B_mixed] if k == 0 else [], writes=[psb[bank]] if k == 0 else [])
                last = P.ops["pe"][-1]
                psb[bank].w = [last]
                B_ring[so[hf]].r["pe"] = last
                B_mixed.r["pe"] = last
            P.op("act", I("activation", out=ttile, in_=psbank(fb[0]), func=AF.Square, accum_out=sm2[:, 0:1]),
                 reads=[psb[fb[0]]], writes=[B_t, B_smf2[fs]])
            P.op("act", I("activation", out=ttile, in_=psbank(fb[1]), func=AF.Square, accum_out=sm2[:, 1:2]),
                 reads=[psb[fb[1]]], writes=[B_t, B_smf2[fs]])
            P.op("dve", I("tensor_tensor", out=sm2[:, 2:3], in0=sm2[:, 0:1], in1=sm2[:, 1:2], op=ALU.add),
                 reads=[B_smf2[fs]], writes=[B_smf2[fs]])
            P.op("dve", I("tensor_scalar", out=sm2[:, 2:3], in0=sm2[:, 2:3], scalar1=1.0 / D, scalar2=EPS, op0=ALU.mult, op1=ALU.add),
                 reads=[B_smf2[fs]], writes=[B_smf2[fs]])
            P.op("pool", I("tensor_tensor", out=sm2[:, 3:4], in0=sm2[:, 2:3], in1=nhalf[:, 0:1], op=ALU.pow),
                 reads=[B_smf2[fs], B_cst], writes=[B_smf2[fs]])
            s = xcnt[0] % 2
            xcnt[0] += 1
            xsl = xb_t[:, s * D:(s + 1) * D]
            xsrc = x_d[(4 * g + tq) * 128:(4 * g + tq + 1) * 128, :]
            P.op("sp", I("dma_start", out=xsl, in_=xsrc), writes=[xb[s]], dsem=xb_sem[s])
            for hf, (tmp, B_tmp) in enumerate(((sz, B_sz), (ta, B_ta))):
                P.op("dve", I("scalar_tensor_tensor", out=tmp, in0=psbank(fb[hf]), scalar=sm2[:, 3:4],
                              in1=postg[:, hf * 512:(hf + 1) * 512], op0=ALU.mult, op1=ALU.mult),
                     reads=[psb[fb[hf]], B_smf2[fs], B_postg], writes=[B_tmp])
                P.op("dve", I("tensor_tensor", out=xsl[:, hf * 512:(hf + 1) * 512], in0=xsl[:, hf * 512:(hf + 1) * 512], in1=tmp,
                              op=ALU.add), reads=[B_tmp, xb[s]], writes=[xb[s]] if hf == 0 else [])
                if hf:
                    xb[s].w.append(P.ops["dve"][-1])
            ysl = y_d[(4 * g + tq) * 128:(4 * g + tq + 1) * 128, :]
            P.op("pool", I("dma_start", out=ysl, in_=xsl), reads=[xb[s]], dsem=out_sem[s])
    P.barrier()

    with nc.Block() as block:
        P.flush(block)
    nc._marks = P.marks
    return nc


_NC_CACHE = {}


def host_consts(rel_bias, pre_norm_g, b_in, lambda_q1, lambda_k1, lambda_q2, lambda_k2, subln_g, sink, post_norm_g):
    f = np.float32
    rel_bias = np.asarray(rel_bias, f)
    b = np.asarray(b_in, f).reshape(-1)
    kk = np.arange(128)[:, None]
    v = np.arange(1152)[None, :]
    bucket = t5_bucket_np((512 + kk - v).astype(np.int32))
    strips = np.ascontiguousarray(np.transpose(rel_bias[bucket], (2, 0, 1)))
    cfar = np.ascontiguousarray(np.broadcast_to(np.concatenate([rel_bias[15], rel_bias[31]])[None, :], (128, 24)))
    x384 = np.arange(384)[None, :]
    rel = 128 + kk - x384
    maskb = np.where(np.abs(rel) <= 128, 0.0, -30000.0).astype(f)
    gcol = np.ascontiguousarray(np.asarray(pre_norm_g, f).reshape(8, 128).T)
    bcol = np.ascontiguousarray(b.reshape(42, 128).T)
    bq = b[C_QB:C_QB + 512].reshape(2, 4, 64)
    bqb = np.ascontiguousarray(np.transpose(bq, (1, 0, 2)).reshape(4, 128).T)
    brow = np.ascontiguousarray(np.concatenate([b[C_VA:C_VA + 512], b[C_VB:C_VB + 128]])[None, :])
    lamv = np.concatenate([np.asarray(a, f).reshape(-1) for a in (lambda_q1, lambda_k1, lambda_q2, lambda_k2)])
    lamv = np.ascontiguousarray(np.broadcast_to(lamv[None, :], (128, 256)))
    gsub = np.ascontiguousarray(np.asarray(subln_g, f).reshape(128, 1))
    sinkr = np.ascontiguousarray(np.broadcast_to(np.asarray(sink, f).reshape(1, 8), (128, 8)))
    postg = np.ascontiguousarray(np.broadcast_to(np.asarray(post_norm_g, f).reshape(1, D), (128, D)))
    ident = np.eye(128, dtype=f)
    return dict(strips=strips, cfar=cfar, maskb=maskb, gcol=gcol, bcol=bcol, bqb=bqb, brow=brow, lamv=lamv, gsub=gsub,
                sinkr=sinkr, postg=postg, ident=ident)


def kernel(x, meta_tokens, rel_bias, pre_norm_g, w_in, b_in, lambda_q1, lambda_k1, lambda_q2, lambda_k2,
           subln_g, sink, w_out_a, w_out_b, w_out, post_norm_g):
    x = np.asarray(x, np.float32)
    B, S, _ = x.shape
    NT = S // 128
    if NT not in _NC_CACHE:
        _NC_CACHE[NT] = build(NT)
    nc = _NC_CACHE[NT]
    consts = host_consts(rel_bias, pre_norm_g, b_in, lambda_q1, lambda_k1, lambda_q2, lambda_k2, subln_g, sink, post_norm_g)
    meta_pad = np.zeros((128, D), np.float32)
    meta_pad[128 - NMETA:] = np.asarray(meta_tokens, np.float32)
    shared = dict(meta_pad=meta_pad, w_in=np.ascontiguousarray(np.asarray(w_in, np.float32)[0]),
                  w_out_a=np.ascontiguousarray(np.asarray(w_out_a, np.float32)[0]),
                  w_out_b=np.ascontiguousarray(np.asarray(w_out_b, np.float32)[0]),
                  w_out=np.ascontiguousarray(np.asarray(w_out, np.float32)[0]), **consts)
    in_maps = [dict(x=np.ascontiguousarray(x[i]), **shared) for i in range(B)]
    res = run_bass_kernel_spmd(nc, in_maps, core_ids=list(range(B)))
    return np.stack([np.asarray(r["y"], np.float32) for r in res.results], axis=0)
```
